# Optimizing a Trainium2 kernel written in Bass

```python
import jax
import jax.numpy as jnp
from jax import lax
import numpy as np

D_MODEL = 1024
BATCH = 8
SEQ = 4096
DEPTH = 4
DEC_BATCH = 8
DEC_SEQ = 2048
PAST_LEN = 128

HEAD_DIM = 64
A_WIDTH = D_MODEL // 2
A_HEADS = A_WIDTH // HEAD_DIM
B_WIDTH = D_MODEL - A_WIDTH
IN_AB = 3 * A_WIDTH + 3 * B_WIDTH
DILATED_BRANCHES = ((128, 1), (512, 4), (2048, 16))
CONV_WIDTH = 3
ROPE_THETA = 500000.0
ROPE_DIM = HEAD_DIM // 4
RWKV_HEAD = 64
RWKV_HEADS = D_MODEL // RWKV_HEAD
DECAY_LORA = 64
AAA_LORA = 64
GATE_LORA = 128
GN_EPS = 64e-5
N_MEM = 256
CA_HEADS = 4
CA_HEAD_DIM = D_MODEL // CA_HEADS
FF_RAW = -(-8 * D_MODEL // 3)
D_FF = -(-FF_RAW // 256) * 256
N_EVEN = (DEPTH + 1) // 2
N_ODD = DEPTH // 2
RMS_EPS = 1e-6
NEG_INF = -1e30

kernel_name = 'hybrid_dilated_conv_rwkv7_encoder'


def rmsnorm(x, g):
    x32 = x.astype(jnp.float32)
    y = x32 * lax.rsqrt(jnp.mean(x32 * x32, axis=-1, keepdims=True) + RMS_EPS)
    return y.astype(x.dtype) * g


def partial_rotary(x, pos):
    half = ROPE_DIM // 2
    inv = ROPE_THETA ** (-2.0 * jnp.arange(half, dtype=jnp.float32) / ROPE_DIM)
    ang = pos.astype(jnp.float32)[:, None] * inv[None, :]
    cos = jnp.cos(ang)[None, :, None, :]
    sin = jnp.sin(ang)[None, :, None, :]
    x1 = x[..., :half].astype(jnp.float32)
    x2 = x[..., half:ROPE_DIM].astype(jnp.float32)
    rot = jnp.concatenate([x1 * cos - x2 * sin, x2 * cos + x1 * sin], axis=-1).astype(x.dtype)
    return jnp.concatenate([rot, x[..., ROPE_DIM:]], axis=-1)


def dilated_window_branch(q, k, v, window, dilation):
    B, S, H, hd = q.shape
    half = window // (2 * dilation)
    blk = half
    L = S // dilation
    nb = -(-L // blk)
    Lp = nb * blk

    def strided(t):
        return t.reshape(B, L, dilation, H, hd)

    qb = jnp.pad(strided(q), ((0, 0), (0, Lp - L), (0, 0), (0, 0), (0, 0)))
    qb = qb.reshape(B, nb, blk, dilation, H, hd)

    def neighbourhood(t):
        tb = jnp.pad(strided(t), ((0, 0), (blk, Lp - L + blk), (0, 0), (0, 0), (0, 0)))
        tb = tb.reshape(B, nb + 2, blk, dilation, H, hd)
        return jnp.concatenate([tb[:, :-2], tb[:, 1:-1], tb[:, 2:]], axis=2)

    kb = neighbourhood(k)
    vb = neighbourhood(v)
    s = jnp.einsum('bnqrhd,bnkrhd->bnrhqk', qb, kb, preferred_element_type=jnp.float32)
    qi = jnp.arange(blk)[:, None]
    kj = jnp.arange(3 * blk)[None, :]
    band = jnp.abs(kj - blk - qi) <= half
    tk = (jnp.arange(nb)[:, None] - 1) * blk + jnp.arange(3 * blk)[None, :]
    inside = (tk >= 0) & (tk < L)
    mask = band[None, :, :] & inside[:, None, :]
    s = jnp.where(mask[None, :, None, None], s, NEG_INF)
    m = jnp.max(s, axis=-1, keepdims=True)
    p = jnp.exp(s - m)
    den = jnp.sum(p, axis=-1, keepdims=True)
    o = jnp.einsum('bnrhqk,bnkrhd->bnqrhd', (p / den).astype(v.dtype), vb)
    lse = (m + jnp.log(den))[..., 0]
    o = o.reshape(B, Lp, dilation, H, hd)[:, :L].reshape(B, S, H, hd)
    lse = jnp.transpose(lse, (0, 1, 4, 2, 3)).reshape(B, Lp, dilation, H)[:, :L].reshape(B, S, H)
    return o, lse


def dilated_attention(q, k, v):
    outs, lses = [], []
    for window, dilation in DILATED_BRANCHES:
        o, l = dilated_window_branch(q, k, v, window, dilation)
        outs.append(o)
        lses.append(l)
    wts = jax.nn.softmax(jnp.stack(lses, axis=0), axis=0).astype(v.dtype)
    return jnp.einsum('gbsh,gbshd->bshd', wts, jnp.stack(outs, axis=0))


def short_gated_conv(bg, cg, h, conv_w):
    S = h.shape[1]
    u = cg * h
    pad = CONV_WIDTH // 2
    up = jnp.pad(u, ((0, 0), (pad, pad), (0, 0)))
    y = sum(conv_w[j] * up[:, j:j + S] for j in range(CONV_WIDTH))
    return bg * y


def window_conv_mixer(xn, pos, w_in, w_out, conv_w):
    B, S, _ = xn.shape
    z = xn @ w_in
    cuts = [A_WIDTH, 2 * A_WIDTH, 3 * A_WIDTH, 3 * A_WIDTH + B_WIDTH, 3 * A_WIDTH + 2 * B_WIDTH]
    q, k, v, bg, cg, h = jnp.split(z, cuts, axis=-1)

    def heads(t):
        return t.reshape(B, S, A_HEADS, HEAD_DIM)

    q = partial_rotary(heads(q), pos) * (HEAD_DIM ** -0.5)
    k = partial_rotary(heads(k), pos)
    ya = dilated_attention(q, k, heads(v)).reshape(B, S, A_WIDTH)
    yb = short_gated_conv(bg, cg, h, conv_w)
    return jnp.concatenate([ya, yb], axis=-1) @ w_out


def wkv7_scan(r, w, k, v, a, b, reverse):
    B, S, H, N = r.shape

    def step(state, inp):
        rt, wt, kt, vt, at, bt = inp
        sa = jnp.einsum('bhij,bhj->bhi', state, at)
        state = state * wt[:, :, None, :] + sa[..., None] * bt[:, :, None, :] + vt[..., None] * kt[:, :, None, :]
        return state, jnp.einsum('bhij,bhj->bhi', state, rt)

    xs = tuple(jnp.moveaxis(t, 1, 0) for t in (r, w, k, v, a, b))
    s0 = jnp.zeros((B, H, N, N), jnp.float32)
    _, out = lax.scan(step, s0, xs, reverse=reverse)
    return jnp.moveaxis(out, 0, 1)


def rwkv7_bidir_time_mix(xn, mu, w_r, w_k, w_v, w_o, w0, w1, w2, a0, a1, a2, g1, g2, k_k, k_a, r_k, lnx_w, lnx_b):
    B, S, D = xn.shape
    f32 = jnp.float32

    def heads(t):
        return t.astype(f32).reshape(B, S, RWKV_HEADS, RWKV_HEAD)

    xp = jnp.pad(xn, ((0, 0), (1, 1), (0, 0)))
    xx = 0.5 * (xp[:, :-2] + xp[:, 2:]) - xn
    xr, xw, xk, xv, xa, xg = [xn + xx * mu[i] for i in range(6)]
    r = heads(xr @ w_r)
    k_lin = xk @ w_k
    k = heads(k_lin)
    v = heads(xv @ w_v)
    g = jax.nn.sigmoid(xg @ g1) @ g2
    kk = heads(k_lin * k_k)
    kk = kk * lax.rsqrt(jnp.maximum(jnp.sum(kk * kk, axis=-1, keepdims=True), 1e-24))
    k_a_h = k_a.astype(f32).reshape(RWKV_HEADS, RWKV_HEAD)
    r_k32 = r_k.astype(f32)
    outs, bonus = [], []
    for d in range(2):
        w_log = -jax.nn.softplus(-(w0[d] + jnp.tanh(xw @ w1[d]) @ w2[d]).astype(f32)) - 0.5
        decay = heads(jnp.exp(-jnp.exp(w_log)))
        a = heads(jax.nn.sigmoid((a0[d] + (xa @ a1[d]) @ a2[d]).astype(f32)))
        k_d = k * (1.0 + (a - 1.0) * k_a_h)
        outs.append(wkv7_scan(r, decay, k_d, v, -kk, kk * a, reverse=(d == 1)))
        bonus.append(jnp.sum(r * k_d * r_k32, axis=-1, keepdims=True) * v)
    y = outs[0] + outs[1]
    mean = jnp.mean(y, axis=-1, keepdims=True)
    var = jnp.mean(jnp.square(y - mean), axis=-1, keepdims=True)
    y = (y - mean) * lax.rsqrt(var + GN_EPS)
    y = y.reshape(B, S, D) * lnx_w.astype(f32) + lnx_b.astype(f32) + (bonus[0] + bonus[1]).reshape(B, S, D)
    return (y.astype(xn.dtype) * g) @ w_o


def memory_cross_attention(xn, mem_n, w_q, w_kv, w_o):
    B, S, D = xn.shape
    M = mem_n.shape[1]
    q = (xn @ w_q).reshape(B, S, CA_HEADS, CA_HEAD_DIM)
    k, v = jnp.split(mem_n @ w_kv, 2, axis=-1)
    k = k.reshape(B, M, CA_HEADS, CA_HEAD_DIM)
    v = v.reshape(B, M, CA_HEADS, CA_HEAD_DIM)
    s = jnp.einsum('bqhd,bkhd->bhqk', q, k, preferred_element_type=jnp.float32) * (CA_HEAD_DIM ** -0.5)
    p = jax.nn.softmax(s, axis=-1).astype(v.dtype)
    o = jnp.einsum('bhqk,bkhd->bqhd', p, v).reshape(B, S, D)
    return o @ w_o


def swiglu(xn, w_gu, w_down):
    gate, up = jnp.split(xn @ w_gu, 2, axis=-1)
    return (jax.nn.silu(gate) * up) @ w_down


def encoder_trunk(x, mem, p):
    pos = jnp.arange(x.shape[1])
    for l in range(DEPTH):
        xn = rmsnorm(x, p['norm_mix'][l])
        if l % 2 == 0:
            e = l // 2
            x = x + window_conv_mixer(xn, pos, p['ab_w_in'][e], p['ab_w_out'][e], p['ab_conv'][e])
        else:
            o = l // 2
            x = x + rwkv7_bidir_time_mix(
                xn, p['rw_mu'][o], p['rw_wr'][o], p['rw_wk'][o], p['rw_wv'][o], p['rw_wo'][o],
                p['rw_w0'][o], p['rw_w1'][o], p['rw_w2'][o], p['rw_a0'][o], p['rw_a1'][o], p['rw_a2'][o],
                p['rw_g1'][o], p['rw_g2'][o], p['rw_kk'][o], p['rw_ka'][o], p['rw_rk'][o],
                p['rw_lnx_w'][o], p['rw_lnx_b'][o])
        mem_n = rmsnorm(mem, p['norm_mem'][l])
        x = x + memory_cross_attention(rmsnorm(x, p['norm_cross'][l]), mem_n,
                                       p['ca_wq'][l], p['ca_wkv'][l], p['ca_wo'][l])
        x = x + swiglu(rmsnorm(x, p['norm_ffn'][l]), p['ffn_wgu'][l], p['ffn_wdown'][l])
    return rmsnorm(x, p['norm_final'])


def setup_inputs(seed: int = 0) -> dict:
    key = jax.random.key(seed)
    keys = jax.random.split(key, 64)
    counter = iter(range(64))
    f32 = jnp.float32

    def nk():
        return keys[next(counter)]

    def nrm(shape, scale):
        return jax.random.normal(nk(), shape, f32) * scale

    def gain(shape, base=1.0):
        return base + 0.05 * jax.random.normal(nk(), shape, f32)

    def unif(shape, lo, hi):
        return jax.random.uniform(nk(), shape, f32, lo, hi)

    D = D_MODEL
    return {
        'x_prompt': nrm((BATCH, SEQ, D), 1.0),
        'x_sample': nrm((DEC_BATCH, DEC_SEQ, D), 1.0),
        'mem_prompt': nrm((BATCH, N_MEM, D), 1.0),
        'mem_sample': nrm((DEC_BATCH, N_MEM, D), 1.0),
        'norm_mix': gain((DEPTH, D)),
        'norm_cross': gain((DEPTH, D)),
        'norm_mem': gain((DEPTH, D)),
        'norm_ffn': gain((DEPTH, D)),
        'norm_final': gain((D,)),
        'ab_w_in': nrm((N_EVEN, D, IN_AB), D ** -0.5),
        'ab_w_out': nrm((N_EVEN, A_WIDTH + B_WIDTH, D), (A_WIDTH + B_WIDTH) ** -0.5),
        'ab_conv': nrm((N_EVEN, CONV_WIDTH, B_WIDTH), CONV_WIDTH ** -0.5),
        'rw_mu': unif((N_ODD, 6, D), 0.0, 1.0),
        'rw_wr': nrm((N_ODD, D, D), D ** -0.5),
        'rw_wk': nrm((N_ODD, D, D), D ** -0.5),
        'rw_wv': nrm((N_ODD, D, D), D ** -0.5),
        'rw_wo': nrm((N_ODD, D, D), D ** -0.5),
        'rw_w0': unif((N_ODD, 2, D), -5.0, 0.0),
        'rw_w1': nrm((N_ODD, 2, D, DECAY_LORA), D ** -0.5),
        'rw_w2': nrm((N_ODD, 2, DECAY_LORA, D), 0.1 * DECAY_LORA ** -0.5),
        'rw_a0': nrm((N_ODD, 2, D), 0.5),
        'rw_a1': nrm((N_ODD, 2, D, AAA_LORA), D ** -0.5),
        'rw_a2': nrm((N_ODD, 2, AAA_LORA, D), 0.5 * AAA_LORA ** -0.5),
        'rw_g1': nrm((N_ODD, D, GATE_LORA), D ** -0.5),
        'rw_g2': nrm((N_ODD, GATE_LORA, D), GATE_LORA ** -0.5),
        'rw_kk': gain((N_ODD, D), 0.85),
        'rw_ka': gain((N_ODD, D)),
        'rw_rk': nrm((N_ODD, RWKV_HEADS, RWKV_HEAD), 0.1),
        'rw_lnx_w': gain((N_ODD, D)),
        'rw_lnx_b': nrm((N_ODD, D), 0.02),
        'ca_wq': nrm((DEPTH, D, D), D ** -0.5),
        'ca_wkv': nrm((DEPTH, D, 2 * D), D ** -0.5),
        'ca_wo': nrm((DEPTH, D, D), D ** -0.5),
        'ffn_wgu': nrm((DEPTH, D, 2 * D_FF), D ** -0.5),
        'ffn_wdown': nrm((DEPTH, D_FF, D), D_FF ** -0.5),
    }


def reference(x_prompt, x_sample, mem_prompt, mem_sample, norm_mix, norm_cross, norm_mem, norm_ffn,
              norm_final, ab_w_in, ab_w_out, ab_conv, rw_mu, rw_wr, rw_wk, rw_wv, rw_wo, rw_w0, rw_w1,
              rw_w2, rw_a0, rw_a1, rw_a2, rw_g1, rw_g2, rw_kk, rw_ka, rw_rk, rw_lnx_w, rw_lnx_b,
              ca_wq, ca_wkv, ca_wo, ffn_wgu, ffn_wdown):
    p = dict(norm_mix=norm_mix, norm_cross=norm_cross, norm_mem=norm_mem, norm_ffn=norm_ffn,
             norm_final=norm_final, ab_w_in=ab_w_in, ab_w_out=ab_w_out, ab_conv=ab_conv,
             rw_mu=rw_mu, rw_wr=rw_wr, rw_wk=rw_wk, rw_wv=rw_wv, rw_wo=rw_wo,
             rw_w0=rw_w0, rw_w1=rw_w1, rw_w2=rw_w2, rw_a0=rw_a0, rw_a1=rw_a1, rw_a2=rw_a2,
             rw_g1=rw_g1, rw_g2=rw_g2, rw_kk=rw_kk, rw_ka=rw_ka, rw_rk=rw_rk,
             rw_lnx_w=rw_lnx_w, rw_lnx_b=rw_lnx_b, ca_wq=ca_wq, ca_wkv=ca_wkv, ca_wo=ca_wo,
             ffn_wgu=ffn_wgu, ffn_wdown=ffn_wdown)
    y_prompt = encoder_trunk(x_prompt, mem_prompt, p)
    y_sample = encoder_trunk(x_sample, mem_sample, p)
    return (y_prompt, y_sample)
```

```python
import numpy as np
import concourse.bass as bass
import concourse.mybir as mybir
from concourse.bass_utils import run_bass_kernel_spmd
from contextlib import ExitStack

F32 = mybir.dt.float32
BF16 = mybir.dt.bfloat16
AF = mybir.ActivationFunctionType
ALU = mybir.AluOpType

D = 1024
NMEM = 256
DFF = 2816
KAPPA = 0.6065306597126334
RMS_EPS = 1e-6
GN_EPS = 64e-5
ROPE_THETA = 500000.0
ENG = ['pe', 'act', 'dve', 'pool', 'sp']
SAME_ENG_SYNC = False


class Res:
    __slots__ = ('w', 'r', 'const')

    def __init__(self, const=False):
        self.w = None
        self.r = {}
        self.const = const


class Buf:
    __slots__ = ('ap', 'res')

    def __init__(self, ap, const=False):
        self.ap = ap
        self.res = Res(const)


class Sched:
    def __init__(self):
        self.q = {e: [] for e in ENG}
        self.cnt = {e: 0 for e in ENG}
        self.waited = {e: {} for e in ENG}
        self.dcnt = {}
        self.pending = {}
        self.nops = 0

    def _collect(self, eng, reads, writes, is_dma=False):
        deps = {}
        for r in reads:
            d = r.w
            if d is not None and deps.get(d[0], 0) < d[1]:
                deps[d[0]] = d[1]
        for w in writes:
            d = w.w
            if d is not None and deps.get(d[0], 0) < d[1]:
                deps[d[0]] = d[1]
            for k, v in w.r.items():
                if deps.get(k, 0) < v:
                    deps[k] = v
        out = []
        wt = self.waited[eng]
        for k, v in deps.items():
            if k == eng and not is_dma and not SAME_ENG_SYNC:
                continue
            if wt.get(k, 0) >= v:
                continue
            wt[k] = v
            out.append((k, v))
        return out

    def _mark(self, done, reads, writes):
        k, v = done
        for r in reads:
            if not r.const and r.r.get(k, 0) < v:
                r.r[k] = v
        for w in writes:
            w.w = done
            w.r = {}

    def op(self, eng, fn, reads=(), writes=()):
        waits = self._collect(eng, reads, writes)
        self.cnt[eng] += 1
        done = (eng, self.cnt[eng])
        self._mark(done, reads, writes)
        self.q[eng].append((waits, fn, eng, 1))
        self.nops += 1

    def dma(self, qeng, fn, key, reads=(), writes=()):
        waits = self._collect(qeng, reads, writes, is_dma=True)
        k = ('d', key)
        c = self.dcnt.get(k, 0) + 1
        self.dcnt[k] = c
        done = (k, 16 * c)
        self._mark(done, reads, writes)
        self.pending[k] = 16 * c
        self.q[qeng].append((waits, fn, k, 16))
        self.nops += 1

    def wait_all(self, engs, marks):
        for e in engs:
            wt = self.waited[e]
            ws = []
            for k, v in marks.items():
                if wt.get(k, 0) >= v:
                    continue
                wt[k] = v
                ws.append((k, v))
            if ws:
                self.q[e].append((ws, None, None, 0))


def _chunks_lhsT(W, M):
    Kin, Nout = W.shape
    if Kin < 128:
        Wp = np.zeros((128, Nout), np.float32)
        Wp[:Kin] = W
        return [np.ascontiguousarray(Wp[:, o * M:(o + 1) * M]) for o in range(Nout // M)]
    KC = Kin // 128
    W3 = W.reshape(KC, 128, Nout)
    return [np.ascontiguousarray(W3[:, :, o * M:(o + 1) * M].transpose(1, 0, 2)).reshape(128, KC * M)
            for o in range(Nout // M)]


def weight_spec(l):
    sp = []
    if l % 2 == 0:
        sp += [('in_q', 4, 1024), ('in_k', 4, 1024), ('in_v', 1, 4096), ('in_bg', 4, 1024),
               ('in_cg', 4, 1024), ('in_h', 4, 1024), ('w_out', 8, 1024)]
    else:
        sp += [('wr', 8, 1024), ('wk', 8, 1024), ('wv', 8, 1024), ('wo_r', 8, 1024),
               ('w1_0', 1, 512), ('w1_1', 1, 512), ('a1_0', 1, 512), ('a1_1', 1, 512), ('g1', 1, 1024),
               ('w2_0', 1, 1024), ('w2_1', 1, 1024), ('a2_0', 1, 1024), ('a2_1', 1, 1024), ('g2', 1, 1024)]
    sp += [('ca_q', 8, 1024), ('ca_k', 8, 1024), ('ca_v', 2, 4096), ('ca_o', 8, 1024),
           ('ff_gu', 44, 1024), ('ff_d', 8, 2816)]
    return sp


def weight_offsets(l):
    off = {}
    o = 0
    for name, n, F in weight_spec(l):
        off[name] = (o, n, F)
        o += n * F
    return off, o


def pack_weights(l, p):
    ch = {}
    if l % 2 == 0:
        e = l // 2
        win = p['ab_w_in'][e]
        ch['in_q'] = _chunks_lhsT(win[:, 0:512], 128)
        ch['in_k'] = _chunks_lhsT(win[:, 512:1024], 128)
        ch['in_v'] = _chunks_lhsT(win[:, 1024:1536], 512)
        ch['in_bg'] = _chunks_lhsT(win[:, 1536:2048], 128)
        ch['in_cg'] = _chunks_lhsT(win[:, 2048:2560], 128)
        ch['in_h'] = _chunks_lhsT(win[:, 2560:3072], 128)
        ch['w_out'] = _chunks_lhsT(p['ab_w_out'][e], 128)
    else:
        o = l // 2
        ch['wr'] = _chunks_lhsT(p['rw_wr'][o], 128)
        ch['wk'] = _chunks_lhsT(p['rw_wk'][o], 128)
        ch['wv'] = _chunks_lhsT(p['rw_wv'][o], 128)
        ch['wo_r'] = _chunks_lhsT(p['rw_wo'][o], 128)
        for d in range(2):
            ch['w1_%d' % d] = _chunks_lhsT(p['rw_w1'][o, d], 64)
            ch['a1_%d' % d] = _chunks_lhsT(p['rw_a1'][o, d], 64)
            ch['w2_%d' % d] = _chunks_lhsT(p['rw_w2'][o, d], 1024)
            ch['a2_%d' % d] = _chunks_lhsT(p['rw_a2'][o, d], 1024)
        ch['g1'] = _chunks_lhsT(p['rw_g1'][o], 128)
        ch['g2'] = _chunks_lhsT(p['rw_g2'][o], 1024)
    ch['ca_q'] = _chunks_lhsT(p['ca_wq'][l], 128)
    ch['ca_k'] = _chunks_lhsT(p['ca_wkv'][l][:, 0:1024], 128)
    ch['ca_v'] = _chunks_lhsT(p['ca_wkv'][l][:, 1024:2048], 512)
    ch['ca_o'] = _chunks_lhsT(p['ca_wo'][l], 128)
    ch['ff_gu'] = _chunks_lhsT(p['ffn_wgu'][l], 128)
    ch['ff_d'] = _chunks_lhsT(p['ffn_wdown'][l], 128)
    parts = []
    for name, n, F in weight_spec(l):
        assert len(ch[name]) == n, (name, len(ch[name]), n)
        for a in ch[name]:
            assert a.shape == (128, F), (name, a.shape, F)
            parts.append(a)
    return np.ascontiguousarray(np.concatenate(parts, axis=1))


def col_layout(depth):
    names = []
    for l in range(depth):
        names += [('nmix%d' % l, 8), ('ncross%d' % l, 8), ('nmem%d' % l, 8), ('nffn%d' % l, 8)]
        if l % 2 == 0:
            names += [('conv%d' % l, 12)]
        else:
            names += [('mu%d' % l, 48), ('w0_%d' % l, 16), ('a0_%d' % l, 16), ('kk%d' % l, 8), ('ka%d' % l, 8),
                      ('rk%d' % l, 8), ('lnw%d' % l, 8), ('lnb%d' % l, 8), ('omka%d' % l, 8)]
    names += [('nfinal', 8)]
    off = {}
    o = 0
    for n, k in names:
        off[n] = o
        o += k
    return off, o


def _colv(v):
    return np.asarray(v, np.float32).reshape(-1, 128).T


def pack_cols(depth, p):
    off, n = col_layout(depth)
    cols = np.zeros((128, n), np.float32)

    def put(name, arr):
        cols[:, off[name]:off[name] + arr.shape[1]] = arr
    for l in range(depth):
        put('nmix%d' % l, _colv(p['norm_mix'][l]))
        put('ncross%d' % l, _colv(p['norm_cross'][l]))
        put('nmem%d' % l, _colv(p['norm_mem'][l]))
        put('nffn%d' % l, _colv(p['norm_ffn'][l]))
        if l % 2 == 0:
            cw = p['ab_conv'][l // 2]
            put('conv%d' % l, np.concatenate([_colv(cw[j]) for j in range(3)], axis=1))
        else:
            o = l // 2
            put('mu%d' % l, np.concatenate([_colv(p['rw_mu'][o, i]) for i in range(6)], axis=1))
            put('w0_%d' % l, np.concatenate([_colv(p['rw_w0'][o, d]) for d in range(2)], axis=1))
            put('a0_%d' % l, np.concatenate([_colv(p['rw_a0'][o, d]) for d in range(2)], axis=1))
            put('kk%d' % l, _colv(p['rw_kk'][o]))
            put('ka%d' % l, _colv(p['rw_ka'][o]))
            put('rk%d' % l, _colv(p['rw_rk'][o].reshape(-1)))
            put('lnw%d' % l, _colv(p['rw_lnx_w'][o]))
            put('lnb%d' % l, _colv(p['rw_lnx_b'][o]))
    put('nfinal', _colv(p['norm_final']))
    return cols


CB = {}
_o = 0
for _n, _k in [('ident', 128), ('ones', 128), ('blk', 128), ('pswap', 128), ('tri_f', 256), ('tri_b', 256),
               ('mT_f', 512), ('mL_f', 512), ('mT_b', 512), ('mL_b', 512),
               ('am_n', 512), ('am_f', 512), ('am_l', 512), ('am_fl', 512), ('am_s', 512)]:
    CB[_n] = (_o, _k)
    _o += _k
NCB = _o


def make_cbf():
    c = np.zeros((128, NCB), np.float32)

    def put(n, a):
        o, k = CB[n]
        assert a.shape == (128, k), (n, a.shape)
        c[:, o:o + k] = a
    I = np.eye(128, dtype=np.float32)
    put('ident', I)
    put('ones', np.ones((128, 128), np.float32))
    blk = np.zeros((128, 128), np.float32)
    blk[0:64, 0:64] = 1
    blk[64:128, 64:128] = 1
    put('blk', blk)
    ps = np.zeros((128, 128), np.float32)
    for m in range(128):
        j = m % 64
        if j < 8:
            ps[m + 8, m] = 1
        elif j < 16:
            ps[m - 8, m] = 1
    put('pswap', ps)
    s = np.arange(128)[:, None]
    t = np.arange(128)[None, :]
    put('tri_f', np.concatenate([(s <= t), (s < t)], axis=1).astype(np.float32))
    put('tri_b', np.concatenate([(s >= t), (s > t)], axis=1).astype(np.float32))
    stf, inf_ = (s < t).astype(np.float32), (s <= t).astype(np.float32)
    stb, inb = (s > t).astype(np.float32), (s >= t).astype(np.float32)
    put('mT_f', np.concatenate([stf, inf_, stf, inf_], axis=1))
    put('mT_b', np.concatenate([stb, inb, stb, inb], axis=1))
    Lf = (t < s).astype(np.float32)
    Lb = (t > s).astype(np.float32)
    put('mL_f', np.concatenate([Lf] * 4, axis=1))
    put('mL_b', np.concatenate([Lb] * 4, axis=1))
    jj = np.arange(128)[:, None]
    i1 = np.arange(128)[None, :]
    i2 = np.arange(256)[None, :]
    M0 = ((jj - i1 >= 0) & (jj - i1 <= 128)).astype(np.float32)
    M1 = ((i2 - jj >= 0) & (i2 - jj <= 128)).astype(np.float32)
    M2 = M1[:, 0:128]
    lo = (jj >= 64).astype(np.float32)
    hi = (jj < 64).astype(np.float32)
    put('am_n', np.concatenate([M0, M1, M2], axis=1))
    put('am_f', np.concatenate([M0 * lo, M1, M2], axis=1))
    put('am_l', np.concatenate([M0, M1, M2 * hi], axis=1))
    put('am_fl', np.concatenate([M0 * lo, M1, M2 * hi], axis=1))
    put('am_s', np.concatenate([M0 * lo, M1[:, 0:128] * hi, np.zeros((128, 256), np.float32)], axis=1))
    return c


def make_rope(S):
    half = 8
    inv = np.power(np.float32(ROPE_THETA), (np.float32(-2.0) * np.arange(half, dtype=np.float32) / np.float32(16.0))).astype(np.float32)
    pos = np.arange(S, dtype=np.float32)
    ang = (pos[None, :] * inv[:, None]).astype(np.float32).astype(np.float64)
    cos = np.ones((128, S), np.float32)
    sin = np.zeros((128, S), np.float32)
    for pbase in (0, 64):
        cos[pbase:pbase + 8] = np.cos(ang)
        cos[pbase + 8:pbase + 16] = np.cos(ang)
        sin[pbase:pbase + 8] = -np.sin(ang)
        sin[pbase + 8:pbase + 16] = np.sin(ang)
    return np.stack([cos, sin], axis=1)


ARENA_BYTES = 204 * 1024
NWS = 4
WS_BYTES = 8192
WLOOK = 3


def _bp(ap):
    b = ap.base_partition
    return b() if callable(b) else b


class Gen:
    def __init__(self, nc, S_P, S_S, depth, plan, arena, banks, dram, sch, wplan):
        self.nc = nc
        self.S_P, self.S_S = S_P, S_S
        self.Stot = S_P + S_S
        self.seqs = [(0, S_P), (S_P, S_S)]
        self.depth = depth
        self.plan = plan
        self.arena = arena
        self.dram = dram
        self.sch = sch
        self.T = 512
        self.ntiles = self.Stot // 512
        self.coff, self.ncols = col_layout(depth)
        self.woff = [weight_offsets(l) for l in range(depth)]
        self.dres_ = {}
        self.wres = [Res() for _ in range(depth)]
        self.pbank = 0
        self.banks = [Buf(b[:, :]) for b in banks]
        self.bank_h = banks
        self.bankres = set(id(b.res) for b in self.banks)
        self.top = 0
        self.cb = self.carve(NCB * 2, BF16, const=True)
        self.cols = self.carve(self.ncols * 4, F32, const=True)
        self.wslots = [self.carve(WS_BYTES, BF16) for _ in range(NWS)]
        self.scr = {e: self.carve(16, F32) for e in ('act', 'dve', 'pool')}
        self.base = self.top
        self.wplan = wplan
        self.wrec = []
        self.wpos = 0
        self.wissued = 0

    def carve(self, nbytes, dt, const=False, shape=None):
        off = self.top
        self.top += (nbytes + 31) // 32 * 32
        assert self.top <= ARENA_BYTES, ('arena overflow', self.top)
        ap = self.arena[:, off // 2:(off + nbytes) // 2]
        if dt == F32:
            ap = ap.bitcast(F32)
        b = Buf(ap, const)
        return b

    def reset(self):
        self.top = self.base

    def bank(self):
        b = self.banks[self.pbank]
        h = self.bank_h[self.pbank]
        self.pbank = (self.pbank + 1) % len(self.banks)
        return b, h

    def cbv(self, name):
        o, k = CB[name]
        return self.cb.ap[:, o:o + k]

    def col(self, name, i):
        o = self.coff[name] + i
        return self.cols.ap[:, o:o + 1]

    def dres(self, name, t0, t1):
        out = []
        for t in range(t0 // 512, (t1 + 511) // 512):
            k = (name, t)
            if k not in self.dres_:
                self.dres_[k] = Res()
            out.append(self.dres_[k])
        return out

    def op(self, eng, fn, reads=(), writes=()):
        rr = [b.res if isinstance(b, Buf) else b for b in reads]
        ww = [b.res if isinstance(b, Buf) else b for b in writes]
        for r in rr:
            if id(r) in self.bankres and r not in ww:
                ww.append(r)
        self.sch.op(eng, fn, rr, ww)

    def dma(self, q, out_ap, in_ap, key, reads=(), writes=()):
        def fn(e, o=out_ap, i=in_ap):
            return e.dma_start(out=o, in_=i)
        self.sch.dma(q, fn, key, [b.res if isinstance(b, Buf) else b for b in reads],
                     [b.res if isinstance(b, Buf) else b for b in writes])

    def dma_batch(self, q, key, items):
        rs, ws = [], []
        for (o, i, rd, wr) in items:
            self.dma(q, o, i, key, rd, wr)
            rs += [b.res if isinstance(b, Buf) else b for b in rd]
            ws += [b.res if isinstance(b, Buf) else b for b in wr]
        k = ('d', key)
        final = self.sch.dcnt[k] * 16
        for r in ws:
            r.w = (k, final)
        for r in rs:
            if not r.const:
                r.r[k] = final

    def mm(self, bank, mms, reads):
        def fn(t, mms=mms):
            ins = None
            for (o, l, r, st, sp) in mms:
                kw = {}
                rp, cp = _bp(l), _bp(o)
                if rp != 0 or cp != 0:
                    kw['tile_position'] = (rp, cp)
                ins = t.matmul(o, l, r, start=st, stop=sp, skip_group_check=True, **kw)
            return ins
        self.op('pe', fn, reads, [bank])

    def tr(self, bank, trs, reads):
        idn = self.cbv('ident')

        def fn(t, trs=trs):
            ins = None
            for (o, i) in trs:
                ins = t.transpose(o, i, idn)
            return ins
        self.op('pe', fn, reads, [bank])

    def act(self, out, in_, func, reads, writes, bias=None, scale=None, eng='act'):
        kw = {}
        if bias is not None:
            kw['bias'] = bias
        if scale is not None:
            kw['scale'] = scale

        def fn(e, out=out, in_=in_):
            return e.activation(out=out, in_=in_, func=func, **kw)
        self.op('act', fn, reads, writes)

    def tt(self, eng, out, in0, in1, op, reads, writes):
        def fn(e, out=out, in0=in0, in1=in1):
            return e.tensor_tensor(out=out, in0=in0, in1=in1, op=op)
        self.op(eng, fn, reads, writes)

    def ts(self, eng, out, in0, s1, s2, op0, op1, reads, writes):
        def fn(e, out=out, in0=in0):
            if s2 is None:
                return e.tensor_scalar(out, in0, s1, None, op0)
            return e.tensor_scalar(out, in0, s1, s2, op0, op1)
        self.op(eng, fn, reads, writes)

    def stt(self, eng, out, in0, scalar, in1, op0, op1, reads, writes):
        eng = 'dve'
        def fn(e, out=out, in0=in0, in1=in1):
            return e.scalar_tensor_tensor(out=out, in0=in0, scalar=scalar, in1=in1, op0=op0, op1=op1)
        self.op(eng, fn, reads, writes)

    def cp(self, eng, out, in_, reads, writes):
        if eng == 'act':
            def fn(e, out=out, in_=in_):
                return e.copy(out, in_)
        else:
            def fn(e, out=out, in_=in_):
                return e.tensor_copy(out=out, in_=in_)
        self.op(eng, fn, reads, writes)

    def recip(self, out, in_, reads, writes):
        def fn(e, out=out, in_=in_):
            return e.reciprocal(out, in_)
        self.op('dve', fn, reads, writes)

    def memset(self, eng, ap, val, writes):
        def fn(e, ap=ap):
            return e.memset(ap, val)
        self.op(eng, fn, [], writes)

    def barrier(self):
        s = self.sch
        b0 = self.banks[0]

        idn = self.cbv('ident')
        bk = self.bank_h[0]

        def pe_fn(t):
            return t.matmul(bk[0:1, 0:1], idn[0:1, 0:1], idn[0:1, 0:1], start=True, stop=True, skip_group_check=True)
        self.op('pe', pe_fn, [self.cb], [b0])
        self.op('act', lambda e: e.memzero(self.scr['act'].ap), [], [self.scr['act']])
        self.op('dve', lambda e: e.memset(self.scr['dve'].ap, 0.0), [], [self.scr['dve']])
        self.op('pool', lambda e: e.memset(self.scr['pool'].ap, 0.0), [], [self.scr['pool']])
        marks = dict(s.pending)
        for e in ('pe', 'act', 'dve', 'pool'):
            marks[e] = s.cnt[e]
        s.wait_all(ENG, marks)
        s.pending = {}

    def _wissue(self, i, req):
        l, name, idx = req
        o, n, F = self.woff[l][0][name]
        slot = self.wslots[i % NWS]
        src = self.dram['wbf%d' % l][:, o + idx * F:o + (idx + 1) * F]
        self.dma('sp', slot.ap[:, 0:F], src, ('w', i % NWS), [self.wres[l]], [slot])
        self.wissued = i + 1

    def wget(self, l, name, idx, M=None):
        req = (l, name, idx)
        i = self.wpos
        self.wpos += 1
        if self.wplan is None:
            self.wrec.append(req)
            self._wissue(i, req)
        else:
            assert self.wplan[i] == req, (i, self.wplan[i], req)
            tgt = min(i + WLOOK, len(self.wplan) - 1)
            while self.wissued <= tgt:
                self._wissue(self.wissued, self.wplan[self.wissued])
        slot = self.wslots[i % NWS]
        F = self.woff[l][0][name][2]
        ap = slot.ap[:, 0:F]
        if M is not None:
            ap = ap.rearrange("p (k m) -> p k m", m=M)
        return slot, ap

    def prologue(self):
        d = self.dram
        self.dma('pool', self.cb.ap, d['cbf'], 'c0', [], [self.cb])
        self.dma('sp', self.cols.ap, d['cols'], 'c1', [], [self.cols])
        PIECE = 8192
        for l in range(self.depth):
            tot = self.woff[l][1]
            for c0 in range(0, tot, PIECE):
                c1 = min(tot, c0 + PIECE)
                self.dma('pool', d['wbf%d' % l][:, c0:c1], d['wpk%d' % l][:, c0:c1], 'wc%d' % l, [], [self.wres[l]])
        for l in range(self.depth):
            if l % 2 == 1:
                o1 = self.coff['omka%d' % l]
                o0 = self.coff['ka%d' % l]
                self.ts('dve', self.cols.ap[:, o1:o1 + 8], self.cols.ap[:, o0:o0 + 8], -1.0, 1.0, ALU.mult, ALU.add,
                        [self.cols], [self.cols])

    def rmsnorm(self, xt, xt_ap, ncolsT, gname, sq, sq_ap, rstd, rstd_ap, outs):
        n = ncolsT
        self.act(sq_ap, xt_ap, AF.Square, [xt], [sq], scale=1.0 / 32.0)
        bk, _ = self.bank()
        ones = self.cbv('ones')
        self.mm(bk, [(bk.ap[:, 0:n], ones, sq_ap[:, c, :], c == 0, c == 7) for c in range(8)], [sq, self.cb])
        self.act(rstd_ap, bk.ap[:, 0:n], AF.Sqrt, [bk], [rstd], bias=RMS_EPS, scale=1.0)
        self.recip(rstd_ap, rstd_ap, [rstd], [rstd])
        for (engs, ob, oap) in outs:
            for c in range(8):
                self.stt(engs[c % len(engs)], oap[:, c, :], xt_ap[:, c, :], self.col(gname, c), rstd_ap,
                         ALU.mult, ALU.mult, [xt, rstd, self.cols], [ob])

    def linear(self, l, wname, n_oc, KC, rhs_fn, rhs_reads, N, evac, oc0=0, M=128, Kp=128):
        for oc in range(n_oc):
            slot, w = self.wget(l, wname, oc0 + oc, M=M)
            bk, _ = self.bank()
            self.mm(bk, [(bk.ap[0:M, 0:N], w[0:Kp, kc, :], rhs_fn(kc), kc == 0, kc == KC - 1) for kc in range(KC)],
                    [slot] + rhs_reads)
            evac(oc, bk)

    def phase_caffn(self, l, final):
        self.reset()
        T = 512
        d = self.dram
        xts = [self.carve(8 * T * 4, F32) for _ in range(2)]
        sq = self.carve(8 * T * 2, BF16)
        xn = self.carve(8 * T * 2, BF16)
        qT = self.carve(8 * T * 2, BF16)
        oT = self.carve(8 * T * 2, BF16)
        pTs = [self.carve(2 * T * 2, BF16) for _ in range(2)]
        rdens = [self.carve(T * 4, F32) for _ in range(2)]
        rstd = self.carve(T * 4, F32)
        sgs = [self.carve(T * 2, BF16) for _ in range(2)]
        hb = self.carve(22 * T * 2, BF16)
        kTs = [self.carve(8 * 256 * 2, BF16) for _ in range(2)]
        vtoks = [self.carve(2 * 1024 * 2, BF16) for _ in range(2)]
        ystage = self.carve(8 * T * 4, F32) if final else None
        v3 = lambda b, n: b.ap.rearrange("p (c t) -> p c t", c=n)

        mns = []
        for si in range(2):
            mt = xts[si]
            mt_ap = mt.ap[:, 0:8 * 256].rearrange("p (c t) -> p c t", c=8)
            src = d['memT'].rearrange("(c p) t -> p c t", p=128)[:, :, si * 256:(si + 1) * 256]
            self.dma('sp', mt_ap, src, ('x', si), [], [mt])
            mn = sq if si == 0 else xn
            mn_ap = mn.ap[:, 0:8 * 256].rearrange("p (c t) -> p c t", c=8)
            sq_ap = qT.ap[:, 0:8 * 256].rearrange("p (c t) -> p c t", c=8)
            self.rmsnorm(mt, mt_ap, 256, 'nmem%d' % l, qT, sq_ap, rstd, rstd.ap[:, 0:256], [(('dve', 'pool'), mn, mn_ap)])
            mns.append((mn, mn_ap))
        for oc in range(8):
            slot, w = self.wget(l, 'ca_k', oc, M=128)
            for si in range(2):
                mn, mn_ap = mns[si]
                bk, _ = self.bank()
                self.mm(bk, [(bk.ap[:, 0:256], w[:, kc, :], mn_ap[:, kc, :], kc == 0, kc == 7) for kc in range(8)], [slot, mn])
                self.cp('act', v3(kTs[si], 8)[:, oc, :], bk.ap[:, 0:256], [bk], [kTs[si]])
        for half in range(2):
            slot, w = self.wget(l, 'ca_v', half, M=512)
            for si in range(2):
                mn, mn_ap = mns[si]
                for blk in range(2):
                    bk, _ = self.bank()
                    self.mm(bk, [(bk.ap[:, 0:512], mn_ap[:, kc, blk * 128:(blk + 1) * 128], w[:, kc, :], kc == 0, kc == 7)
                                 for kc in range(8)], [slot, mn])
                    self.cp('dve', v3(vtoks[si], 2)[:, blk, half * 512:(half + 1) * 512], bk.ap[:, 0:512], [bk], [vtoks[si]])

        ones = self.cbv('ones')
        xsrc = 'xs'
        for ti in range(self.ntiles):
            tok0 = ti * T
            si = 0 if tok0 < self.S_P else 1
            xt = xts[ti % 2]
            xt_ap = v3(xt, 8)
            src = d[xsrc].rearrange("(c p) s -> p c s", p=128)[:, :, tok0:tok0 + T]
            self.dma('sp', xt_ap, src, ('x', ti % 2), self.dres(xsrc, tok0, tok0 + T), [xt])
            xn_ap = v3(xn, 8)
            sq_ap = v3(sq, 8)
            self.rmsnorm(xt, xt_ap, T, 'ncross%d' % l, sq, sq_ap, rstd, rstd.ap, [(('dve', 'pool'), xn, xn_ap)])
            qT_ap = v3(qT, 8)
            oT_ap = v3(oT, 8)
            kT_ap = v3(kTs[si], 8)
            vt_ap = v3(vtoks[si], 2)

            def ev_q(oc, bk):
                self.cp('act', qT_ap[:, oc, :], bk.ap, [bk], [qT])
            self.linear(l, 'ca_q', 8, 8, lambda kc: xn_ap[:, kc, :], [xn], T, ev_q)
            for h in range(4):
                pT = pTs[h % 2]
                pT_ap = v3(pT, 2)
                rden = rdens[h % 2]
                for blk in range(2):
                    bk, _ = self.bank()
                    self.mm(bk, [(bk.ap, kT_ap[:, 2 * h + dc, blk * 128:(blk + 1) * 128], qT_ap[:, 2 * h + dc, :], dc == 0, dc == 1)
                                 for dc in range(2)], [kTs[si], qT])
                    self.act(pT_ap[:, blk, :], bk.ap, AF.Exp, [bk], [pT], scale=1.0 / 16.0)
                bk, _ = self.bank()
                self.mm(bk, [(bk.ap, ones, pT_ap[:, blk, :], blk == 0, blk == 1) for blk in range(2)], [pT, self.cb])
                self.recip(rden.ap, bk.ap, [bk], [rden])
                for dc in range(2):
                    bk, _ = self.bank()
                    c = 2 * h + dc
                    self.mm(bk, [(bk.ap, vt_ap[:, blk, c * 128:(c + 1) * 128], pT_ap[:, blk, :], blk == 0, blk == 1)
                                 for blk in range(2)], [vtoks[si], pT])
                    self.tt('dve', oT_ap[:, c, :], bk.ap, rden.ap, ALU.mult, [bk, rden], [oT])

            def ev_add(oc, bk):
                self.tt('dve', xt_ap[:, oc, :], bk.ap, xt_ap[:, oc, :], ALU.add, [bk, xt], [xt])
            self.linear(l, 'ca_o', 8, 8, lambda kc: oT_ap[:, kc, :], [oT], T, ev_add)
            self.rmsnorm(xt, xt_ap, T, 'nffn%d' % l, sq, sq_ap, rstd, rstd.ap, [(('dve', 'pool'), xn, xn_ap)])
            h_ap = v3(hb, 22)
            for c in range(22):
                slg, wg = self.wget(l, 'ff_gu', c, M=128)
                bg, _ = self.bank()
                self.mm(bg, [(bg.ap, wg[:, kc, :], xn_ap[:, kc, :], kc == 0, kc == 7) for kc in range(8)], [slg, xn])
                slu, wu = self.wget(l, 'ff_gu', 22 + c, M=128)
                bu, _ = self.bank()
                self.mm(bu, [(bu.ap, wu[:, kc, :], xn_ap[:, kc, :], kc == 0, kc == 7) for kc in range(8)], [slu, xn])
                sg = sgs[c % 2]
                self.act(sg.ap, bg.ap, AF.Silu, [bg], [sg])
                self.tt('dve', h_ap[:, c, :], bu.ap, sg.ap, ALU.mult, [bu, sg], [hb])
            self.linear(l, 'ff_d', 8, 22, lambda kc: h_ap[:, kc, :], [hb], T, ev_add)
            if not final:
                dst = d['xs'].rearrange("(c p) s -> p c s", p=128)[:, :, tok0:tok0 + T]
                self.dma('sp', dst, xt_ap, ('xst', ti % 2), [xt], self.dres('xs', tok0, tok0 + T))
            else:
                ys_ap = v3(ystage, 8)
                self.rmsnorm(xt, xt_ap, T, 'nfinal', sq, sq_ap, rstd, rstd.ap, [(('dve', 'pool'), ystage, ys_ap)])
                dst = d['yT'].rearrange("(c p) s -> p c s", p=128)[:, :, tok0:tok0 + T]
                self.dma('sp', dst, ys_ap, ('yst', 0), [ystage], self.dres('yT', tok0, tok0 + T))
        self.barrier()

    def phase_even(self, l):
        self.even_e1(l)
        self.even_e2(l)
        self.even_e3(l)

    def even_e1(self, l):
        self.reset()
        T = 512
        d = self.dram
        v3 = lambda b, n: b.ap.rearrange("p (c t) -> p c t", c=n)
        xts = [self.carve(8 * T * 4, F32) for _ in range(2)]
        sq = self.carve(8 * T * 2, BF16)
        xn = self.carve(8 * T * 2, BF16)
        rstd = self.carve(T * 4, F32)
        rope = self.carve(2 * self.S_P * 4, F32, const=True)
        rope_ap = rope.ap.rearrange("p (a s) -> p a s", a=2)
        qraws = [self.carve(T * 2, BF16) for _ in range(2)]
        t1s = [self.carve(T * 4, F32) for _ in range(2)]
        t2s = [self.carve(T * 4, F32) for _ in range(2)]
        cgs = [self.carve(T * 4, F32) for _ in range(2)]
        stg = {n: [self.carve(4 * T * 2, BF16) for _ in range(2)] for n in ('q', 'k', 'u', 'bg', 'v')}
        self.dma('sp', rope_ap, d['rope'], 'c2', [], [rope])
        pswap = self.cbv('pswap')
        cnt = 0
        for ti in range(self.ntiles):
            tok0 = ti * T
            s0 = 0 if tok0 < self.S_P else self.S_P
            pos0 = tok0 - s0
            xt = xts[ti % 2]
            xt_ap = v3(xt, 8)
            src = d['xs'].rearrange("(c p) s -> p c s", p=128)[:, :, tok0:tok0 + T]
            self.dma('sp', xt_ap, src, ('x', ti % 2), self.dres('xs', tok0, tok0 + T), [xt])
            xn_ap = v3(xn, 8)
            self.rmsnorm(xt, xt_ap, T, 'nmix%d' % l, sq, v3(sq, 8), rstd, rstd.ap, [(('dve',), xn, xn_ap)])
            import os
            SK = os.environ.get('E1SKIP', '')
            for wname, sn, dn in (('in_q', 'q', 'qT'), ('in_k', 'k', 'kT')):
                if 'q' in SK:
                    continue
                st = stg[sn][ti % 2]
                st_ap = v3(st, 4)
                for oc in range(4):
                    slot, w = self.wget(l, wname, oc, M=128)
                    bq, _ = self.bank()
                    self.mm(bq, [(bq.ap, w[:, kc, :], xn_ap[:, kc, :], kc == 0, kc == 7) for kc in range(8)], [slot, xn])
                    qraw = qraws[cnt % 2]
                    t1 = t1s[cnt % 2]
                    t2 = t2s[cnt % 2]
                    cnt += 1
                    self.cp('act', qraw.ap, bq.ap, [bq], [qraw])
                    bp, _ = self.bank()
                    self.mm(bp, [(bp.ap, pswap, qraw.ap, True, True)], [qraw, self.cb])
                    self.tt('dve', t1.ap, bq.ap, rope_ap[:, 0, pos0:pos0 + T], ALU.mult, [bq, rope], [t1])
                    self.tt('dve', t2.ap, bp.ap, rope_ap[:, 1, pos0:pos0 + T], ALU.mult, [bp, rope], [t2])
                    self.tt('pool', st_ap[:, oc, :], t1.ap, t2.ap, ALU.add, [t1, t2], [st])
                dst = d[dn].rearrange("(c p) s -> p c s", p=128)[:, :, tok0:tok0 + T]
                self.dma('sp', dst, st_ap, ('st' + sn, ti % 2), [st], self.dres(dn, tok0, tok0 + T))
            if 'v' in SK:
                continue
            slot, w = self.wget(l, 'in_v', 0, M=512)
            st = stg['v'][ti % 2]
            st_ap = v3(st, 4)
            for tb in range(4):
                bk, _ = self.bank()
                self.mm(bk, [(bk.ap, xn_ap[:, kc, tb * 128:(tb + 1) * 128], w[:, kc, :], kc == 0, kc == 7) for kc in range(8)],
                        [slot, xn])
                self.cp('act', st_ap[:, tb, :], bk.ap, [bk], [st])
            dst = d['vtok'][tok0:tok0 + T, :].rearrange("(tb p) f -> p tb f", p=128)
            self.dma('sp', dst, st_ap, ('stv', ti % 2), [st], self.dres('vtok', tok0, tok0 + T))
            if 'c' in SK:
                continue
            ust = stg['u'][ti % 2]
            bgst = stg['bg'][ti % 2]
            for cc in range(4):
                slot, w = self.wget(l, 'in_cg', cc, M=128)
                bc, _ = self.bank()
                self.mm(bc, [(bc.ap, w[:, kc, :], xn_ap[:, kc, :], kc == 0, kc == 7) for kc in range(8)], [slot, xn])
                slot, w = self.wget(l, 'in_h', cc, M=128)
                bh, _ = self.bank()
                self.mm(bh, [(bh.ap, w[:, kc, :], xn_ap[:, kc, :], kc == 0, kc == 7) for kc in range(8)], [slot, xn])
                slot, w = self.wget(l, 'in_bg', cc, M=128)
                bb, _ = self.bank()
                self.mm(bb, [(bb.ap, w[:, kc, :], xn_ap[:, kc, :], kc == 0, kc == 7) for kc in range(8)], [slot, xn])
                cg = cgs[cc % 2]
                self.cp('act', cg.ap, bc.ap, [bc], [cg])
                self.tt('dve', v3(ust, 4)[:, cc, :], bh.ap, cg.ap, ALU.mult, [bh, cg], [ust])
                self.cp('act', v3(bgst, 4)[:, cc, :], bb.ap, [bb], [bgst])
            for sn, dn, st in (('u', 'uT', ust), ('bg', 'bgT', bgst)):
                dst = d[dn].rearrange("(c p) s -> p c s", p=128)[:, :, tok0:tok0 + T]
                self.dma('sp', dst, v3(st, 4), ('st' + sn, ti % 2), [st], self.dres(dn, tok0, tok0 + T))
        self.barrier()

    def even_e2(self, l):
        self.reset()
        d = self.dram
        SM = self.S_P
        DIL = (1, 4, 16)
        q1 = self.carve(SM * 2, BF16)
        qd = {1: q1, 4: self.carve(SM * 2, BF16), 16: self.carve(SM * 2, BF16)}
        kp = {dd: self.carve((SM + 128 * dd) * 2, BF16) for dd in DIL}
        vb = {dd: self.carve((SM // 128 + dd) * 128 * 2, BF16) for dd in DIL}
        acc = self.carve(2 * SM * 4, F32)
        pTs = [self.carve(512 * 2, BF16) for _ in range(4)]
        yast = self.carve(SM * 2, BF16)
        ones = self.cbv('ones')
        pcnt = 0
        for si, (s0, S) in enumerate(self.seqs):
            for hp in range(4):
                frow = slice(hp * 128, (hp + 1) * 128)
                for dd in DIL:
                    self.memset('pool', kp[dd].ap, 0.0, [kp[dd]])
                    self.memset('pool', vb[dd].ap, 0.0, [vb[dd]])
                self.dma('sp', q1.ap[:, 0:S], d['qT'][frow, s0:s0 + S], ('e2q', 0), self.dres('qT', s0, s0 + S), [q1])
                self.dma('sp', kp[1].ap[:, 64:64 + S], d['kT'][frow, s0:s0 + S], ('e2k', 0), self.dres('kT', s0, s0 + S), [kp[1]])
                qv, kv, vv = {}, {}, {}
                qv[1] = q1.ap[:, 0:S].rearrange("p (r m) -> p r m", r=1)
                kv[1] = kp[1].ap[:, 0:S + 128].rearrange("p (r m) -> p r m", r=1)
                for dd in (4, 16):
                    L = S // dd
                    qv[dd] = qd[dd].ap[:, 0:S].rearrange("p (r m) -> p r m", r=dd)
                    kv[dd] = kp[dd].ap[:, 0:S + 128 * dd].rearrange("p (r m) -> p r m", r=dd)
                    self.cp('pool', qv[dd], q1.ap[:, 0:S].rearrange("p (m r) -> p r m", r=dd), [q1], [qd[dd]])
                    self.cp('pool', kv[dd][:, :, 64:64 + L], kp[1].ap[:, 64:64 + S].rearrange("p (m r) -> p r m", r=dd),
                            [kp[1]], [kp[dd]])
                for dd in DIL:
                    L = S // dd
                    nb = L // 128 + 1
                    v4 = vb[dd].ap[:, 0:dd * nb * 128].rearrange("p (r b f) -> p r b f", r=dd, b=nb)
                    vv[dd] = v4
                    vres = self.dres('vtok', s0, s0 + S)
                    R0 = s0 + 64 * dd
                    R1 = s0 + dd * (L - 64)
                    if nb > 2:
                        srcm = d['vtok'][R0:R0 + dd * 128 * (nb - 2), frow].rearrange("(b j r) f -> r j b f", j=128, r=dd)
                    srcf = d['vtok'][s0:s0 + 64 * dd, frow].rearrange("(j r) f -> r j f", r=dd)
                    srcl = d['vtok'][R1:R1 + 64 * dd, frow].rearrange("(j r) f -> r j f", r=dd)
                    for r in range(dd):
                        if nb > 2:
                            self.dma('sp', v4[:, r, 1:nb - 1, :], srcm[r], ('e2v', dd), vres, [vb[dd]])
                        self.dma('sp', v4[64:128, r, 0, :], srcf[r], ('e2v', dd), vres, [vb[dd]])
                        self.dma('sp', v4[0:64, r, nb - 1, :], srcl[r], ('e2v', dd), vres, [vb[dd]])
                for dd in DIL:
                    L = S // dd
                    Q = min(256, L)
                    ntile = L // Q
                    accv = acc.ap[:, 0:2 * SM].rearrange("p (a s) -> p a s", a=2)[:, :, 0:S].rearrange(
                        "p a (m r) -> p a m r", r=dd)
                    for r in range(dd):
                        for n in range(ntile):
                            if Q == 128:
                                mname = 'am_s'
                            else:
                                first, last = (n == 0), (n == ntile - 1)
                                mname = 'am_fl' if (first and last) else ('am_f' if first else ('am_l' if last else 'am_n'))
                            mask = self.cbv(mname)
                            W = 512 if Q == 256 else 256
                            qa = qv[dd][:, r, n * Q:(n + 1) * Q]
                            b0 = 2 * n if Q == 256 else 0
                            kb = lambda b: kv[dd][:, r, 128 * b:128 * b + 128]
                            pts = []
                            for hd in range(2):
                                pr = slice(64 * hd, 64 * hd + 64)
                                bk, _ = self.bank()
                                if Q == 256:
                                    mms = [(bk.ap[:, 0:128], kb(b0)[pr], qa[pr, 0:128], True, True),
                                           (bk.ap[:, 128:384], kb(b0 + 1)[pr], qa[pr, 0:256], True, True),
                                           (bk.ap[:, 384:512], kb(b0 + 2)[pr], qa[pr, 128:256], True, True)]
                                else:
                                    mms = [(bk.ap[:, 0:128], kb(0)[pr], qa[pr, 0:128], True, True),
                                           (bk.ap[:, 128:256], kb(1)[pr], qa[pr, 0:128], True, True)]
                                self.mm(bk, mms, [qd[dd], kp[dd]])
                                pT = pTs[pcnt % 4]
                                pcnt += 1
                                self.act(pT.ap[:, 0:W], bk.ap[:, 0:W], AF.Exp, [bk], [pT], scale=0.125)
                                self.tt('pool', pT.ap[:, 0:W], pT.ap[:, 0:W], mask[:, 0:W], ALU.mult, [pT, self.cb], [pT])
                                pts.append(pT)
                            bo, _ = self.bank()
                            mms = []
                            for hd in range(2):
                                pr = slice(64 * hd, 64 * hd + 64)
                                fs = slice(64 * hd, 64 * hd + 64)
                                pT = pts[hd].ap
                                vblk = lambda b: vv[dd][:, r, b, fs]
                                if Q == 256:
                                    mms += [(bo.ap[pr, 0:256], vblk(b0 + 1), pT[:, 128:384], True, False),
                                            (bo.ap[pr, 0:128], vblk(b0), pT[:, 0:128], False, False),
                                            (bo.ap[pr, 128:256], vblk(b0 + 2), pT[:, 384:512], False, False),
                                            (bo.ap[pr, 256:512], ones[:, 0:64], pT[:, 128:384], False, False),
                                            (bo.ap[pr, 256:384], ones[:, 0:64], pT[:, 0:128], False, False),
                                            (bo.ap[pr, 384:512], ones[:, 0:64], pT[:, 384:512], False, True)]
                                else:
                                    mms += [(bo.ap[pr, 0:128], vblk(0), pT[:, 0:128], True, False),
                                            (bo.ap[pr, 0:128], vblk(1), pT[:, 128:256], False, False),
                                            (bo.ap[pr, 256:384], ones[:, 0:64], pT[:, 0:128], False, False),
                                            (bo.ap[pr, 256:384], ones[:, 0:64], pT[:, 128:256], False, True)]
                            self.mm(bo, mms, [vb[dd], self.cb] + pts)
                            av = accv[:, :, n * Q:(n + 1) * Q, r]
                            bov = bo.ap.rearrange("p (a q) -> p a q", a=2)[:, :, 0:Q]
                            if dd == 1:
                                self.cp('dve', av, bov, [bo], [acc])
                            else:
                                self.tt('dve', av, bov, av, ALU.add, [bo, acc], [acc])
                a2 = acc.ap[:, 0:2 * SM].rearrange("p (a s) -> p a s", a=2)
                self.recip(a2[:, 1, 0:S], a2[:, 1, 0:S], [acc], [acc])
                self.tt('dve', yast.ap[:, 0:S], a2[:, 0, 0:S], a2[:, 1, 0:S], ALU.mult, [acc], [yast])
                self.dma('sp', d['yaT'][frow, s0:s0 + S], yast.ap[:, 0:S], ('e2y', 0), [yast], self.dres('yaT', s0, s0 + S))
        self.barrier()

    def even_e3(self, l):
        self.reset()
        T = 512
        d = self.dram
        v3 = lambda b, n: b.ap.rearrange("p (c t) -> p c t", c=n)
        xts = [self.carve(8 * T * 4, F32) for _ in range(2)]
        yas = [self.carve(4 * T * 2, BF16) for _ in range(2)]
        uhs = [self.carve(4 * (T + 2) * 2, BF16) for _ in range(2)]
        bgs = [self.carve(4 * T * 2, BF16) for _ in range(2)]
        ybs = [self.carve(4 * T * 2, BF16) for _ in range(2)]
        tas = [self.carve(T * 4, F32) for _ in range(2)]
        for ti in range(self.ntiles):
            tok0 = ti * T
            s0, S = self.seqs[0] if tok0 < self.S_P else self.seqs[1]
            xt, ya, uh, bg, yb = xts[ti % 2], yas[ti % 2], uhs[ti % 2], bgs[ti % 2], ybs[ti % 2]
            xt_ap = v3(xt, 8)
            fm = lambda n: d[n].rearrange("(c p) s -> p c s", p=128)
            self.dma('sp', xt_ap, fm('xs')[:, :, tok0:tok0 + T], ('x', ti % 2), self.dres('xs', tok0, tok0 + T), [xt])
            self.dma('sp', v3(ya, 4), fm('yaT')[:, :, tok0:tok0 + T], ('e3a', ti % 2), self.dres('yaT', tok0, tok0 + T), [ya])
            self.dma('sp', v3(bg, 4), fm('bgT')[:, :, tok0:tok0 + T], ('e3b', ti % 2), self.dres('bgT', tok0, tok0 + T), [bg])
            uh_ap = uh.ap.rearrange("p (c t) -> p c t", c=4)
            lo = tok0 - 1
            hi = tok0 + T + 1
            c0 = 0
            if lo < s0:
                self.memset('pool', uh_ap[:, :, 0:1], 0.0, [uh])
                lo += 1
                c0 = 1
            c1 = T + 2
            if hi > s0 + S:
                self.memset('pool', uh_ap[:, :, T + 1:T + 2], 0.0, [uh])
                hi -= 1
                c1 = T + 1
            self.dma('sp', uh_ap[:, :, c0:c1], fm('uT')[:, :, lo:hi], ('e3u', ti % 2), self.dres('uT', lo, hi), [uh])
            for cc in range(4):
                ta = tas[cc % 2]
                cw = lambda j: self.col('conv%d' % l, j * 4 + cc)
                self.ts('dve', ta.ap, uh_ap[:, cc, 0:T], cw(0), None, ALU.mult, None, [uh, self.cols], [ta])
                self.stt('dve', ta.ap, uh_ap[:, cc, 1:T + 1], cw(1), ta.ap, ALU.mult, ALU.add, [uh, ta, self.cols], [ta])
                self.stt('dve', ta.ap, uh_ap[:, cc, 2:T + 2], cw(2), ta.ap, ALU.mult, ALU.add, [uh, ta, self.cols], [ta])
                self.tt('dve', v3(yb, 4)[:, cc, :], ta.ap, v3(bg, 4)[:, cc, :], ALU.mult, [ta, bg], [yb])

            def ev_add(oc, bk):
                self.tt('dve', xt_ap[:, oc, :], bk.ap, xt_ap[:, oc, :], ALU.add, [bk, xt], [xt])
            self.linear(l, 'w_out', 8, 8, lambda kc: (v3(ya, 4)[:, kc, :] if kc < 4 else v3(yb, 4)[:, kc - 4, :]), [ya, yb], T, ev_add)
            self.dma('sp', fm('xs')[:, :, tok0:tok0 + T], xt_ap, ('xst', ti % 2), [xt], self.dres('xs', tok0, tok0 + T))
        self.barrier()

    def phase_odd(self, l):
        self.odd_r1(l)
        self.odd_r2(l)
        self.odd_r3(l)

    def wload(self, l, name, idx, buf, key):
        o, n, F = self.woff[l][0][name]
        src = self.dram['wbf%d' % l][:, o + idx * F:o + (idx + 1) * F]
        self.dma('sp', buf.ap[:, 0:F], src, key, [self.wres[l]], [buf])

    def odd_r1(self, l):
        self.reset()
        T = 512
        d = self.dram
        fm = lambda n: d[n].rearrange("(c p) s -> p c s", p=128)
        xh = self.carve(8 * (T + 2) * 4, F32)
        xh_ap = xh.ap.rearrange("p (c t) -> p c t", c=8)
        xx = self.carve(8 * T * 4, F32)
        xx_ap = xx.ap.rearrange("p (c t) -> p c t", c=8)
        sq = self.carve(8 * (T + 2) * 2, BF16)
        sq_ap = sq.ap.rearrange("p (c t) -> p c t", c=8)
        rstd = self.carve((T + 2) * 4, F32)
        mixes = [self.carve(8 * T * 2, BF16) for _ in range(4)]
        lw = {}
        for nm, F in (('w1_0', 512), ('w1_1', 512), ('a1_0', 512), ('a1_1', 512), ('g1', 1024),
                      ('w2_0', 1024), ('w2_1', 1024), ('a2_0', 1024), ('a2_1', 1024), ('g2', 1024)):
            lw[nm] = self.carve(F * 2, BF16)
        items = []
        for nm in lw:
            o_, n_, F_ = self.woff[l][0][nm]
            items.append((lw[nm].ap[:, 0:F_], d['wbf%d' % l][:, o_:o_ + F_], [self.wres[l]], [lw[nm]]))
        self.dma_batch('sp', 'r1w', items)
        hid = {nm: self.carve(T * 2, BF16) for nm in ('hw0', 'hw1', 'ha0', 'ha1', 'hg')}
        f32t = {nm: [self.carve(T * 4, F32) for _ in range(1 if nm in ('kkr', 'tmp', 'rn') else 2)] for nm in
                ('rf', 'kl', 'vf', 'as0', 'as1', 'kkr', 'kd0', 'kd1', 'tmp', 'rn')}
        bft = {nm: [self.carve(T * 2, BF16) for _ in range(1 if nm in ('sqk', 'rkb') else 2)] for nm in
               ('r', 'v', 'g', 'sig0', 'sig1', 'kk', 'b0', 'b1', 'kd0', 'kd1', 'bon', 'sqk', 'rkb')}
        blk = self.cbv('blk')
        mi = 0
        for ti in range(self.ntiles):
            tok0 = ti * T
            s0, S = self.seqs[0] if tok0 < self.S_P else self.seqs[1]
            lo, hi, c0, c1 = tok0 - 1, tok0 + T + 1, 0, T + 2
            if lo < s0:
                self.memset('pool', xh_ap[:, :, 0:1], 0.0, [xh])
                lo, c0 = lo + 1, 1
            if hi > s0 + S:
                self.memset('pool', xh_ap[:, :, T + 1:T + 2], 0.0, [xh])
                hi, c1 = hi - 1, T + 1
            self.dma('sp', xh_ap[:, :, c0:c1], fm('xs')[:, :, lo:hi], ('x', 0), self.dres('xs', lo, hi), [xh])
            self.rmsnorm(xh, xh_ap[:, :, 0:T], T, 'nmix%d' % l, sq, sq_ap[:, :, 0:T], rstd, rstd.ap[:, 0:T],
                         [(('dve',), xh, xh_ap[:, :, 0:T])])
            self.rmsnorm(xh, xh_ap[:, :, T:T + 2], 2, 'nmix%d' % l, sq, sq_ap[:, :, T:T + 2], rstd, rstd.ap[:, T:T + 2],
                         [(('dve',), xh, xh_ap[:, :, T:T + 2])])
            self.tt('dve', xx_ap, xh_ap[:, :, 0:T], xh_ap[:, :, 2:T + 2], ALU.add, [xh], [xx])
            self.stt('dve', xx_ap, xx_ap, 0.5, xh_ap[:, :, 1:T + 1], ALU.mult, ALU.subtract, [xx, xh], [xx])

            def mk_mix(i):
                nonlocal mi
                mb = mixes[mi % 4]
                mi += 1
                m_ap = mb.ap.rearrange("p (c t) -> p c t", c=8)
                for c in range(8):
                    self.stt('dve', m_ap[:, c, :], xx_ap[:, c, :], self.col('mu%d' % l, i * 8 + c), xh_ap[:, c, 1:T + 1],
                             ALU.mult, ALU.add, [xx, xh, self.cols], [mb])
                return mb, m_ap

            def lora1(mb, m_ap, wn, hn, func, M):
                w = lw[wn].ap[:, 0:8 * M].rearrange("p (k m) -> p k m", m=M)
                bk, _ = self.bank()
                self.mm(bk, [(bk.ap[0:M, :], w[:, kc, :], m_ap[:, kc, :], kc == 0, kc == 7) for kc in range(8)], [lw[wn], mb])
                if func is None:
                    self.cp('act', hid[hn].ap[0:M, :], bk.ap[0:M, :], [bk], [hid[hn]])
                else:
                    self.act(hid[hn].ap[0:M, :], bk.ap[0:M, :], func, [bk], [hid[hn]])
            mb, m_ap = mk_mix(1)
            lora1(mb, m_ap, 'w1_0', 'hw0', AF.Tanh, 64)
            lora1(mb, m_ap, 'w1_1', 'hw1', AF.Tanh, 64)
            mb, m_ap = mk_mix(4)
            lora1(mb, m_ap, 'a1_0', 'ha0', None, 64)
            lora1(mb, m_ap, 'a1_1', 'ha1', None, 64)
            mb, m_ap = mk_mix(5)
            lora1(mb, m_ap, 'g1', 'hg', AF.Sigmoid, 128)
            mr, mr_ap = mk_mix(0)
            mk_, mk_ap = mk_mix(2)
            mv, mv_ap = mk_mix(3)
            for oc in range(8):
                par = oc % 2
                F = lambda nm: f32t[nm][par % len(f32t[nm])]
                B = lambda nm: bft[nm][par % len(bft[nm])]
                ocs = slice(oc * 128, (oc + 1) * 128)

                def proj(wname, m_, m_ap_, dst):
                    slot, w = self.wget(l, wname, oc, M=128)
                    bk, _ = self.bank()
                    self.mm(bk, [(bk.ap, w[:, kc, :], m_ap_[:, kc, :], kc == 0, kc == 7) for kc in range(8)], [slot, m_])
                    self.cp('act', dst.ap, bk.ap, [bk], [dst])
                proj('wr', mr, mr_ap, F('rf'))
                proj('wk', mk_, mk_ap, F('kl'))
                proj('wv', mv, mv_ap, F('vf'))
                self.cp('pool', B('r').ap, F('rf').ap, [F('rf')], [B('r')])
                self.cp('pool', B('v').ap, F('vf').ap, [F('vf')], [B('v')])
                bk, _ = self.bank()
                self.mm(bk, [(bk.ap, lw['g2'].ap[:, ocs], hid['hg'].ap, True, True)], [lw['g2'], hid['hg']])
                self.cp('act', B('g').ap, bk.ap, [bk], [B('g')])
                for dd in range(2):
                    bk, _ = self.bank()
                    self.mm(bk, [(bk.ap, lw['w2_%d' % dd].ap[0:64, ocs], hid['hw%d' % dd].ap[0:64, :], True, True)],
                            [lw['w2_%d' % dd], hid['hw%d' % dd]])
                    self.act(B('sig%d' % dd).ap, bk.ap, AF.Sigmoid, [bk, self.cols], [B('sig%d' % dd)],
                             bias=self.col('w0_%d' % l, dd * 8 + oc), scale=1.0)
                    bk, _ = self.bank()
                    self.mm(bk, [(bk.ap, lw['a2_%d' % dd].ap[0:64, ocs], hid['ha%d' % dd].ap[0:64, :], True, True)],
                            [lw['a2_%d' % dd], hid['ha%d' % dd]])
                    self.act(F('as%d' % dd).ap, bk.ap, AF.Sigmoid, [bk, self.cols], [F('as%d' % dd)],
                             bias=self.col('a0_%d' % l, dd * 8 + oc), scale=1.0)
                self.ts('dve', F('kkr').ap, F('kl').ap, self.col('kk%d' % l, oc), None, ALU.mult, None, [F('kl'), self.cols], [F('kkr')])
                self.act(B('sqk').ap, F('kkr').ap, AF.Square, [F('kkr')], [B('sqk')])
                bk, _ = self.bank()
                self.mm(bk, [(bk.ap, blk, B('sqk').ap, True, True)], [B('sqk'), self.cb])
                self.act(F('rn').ap, bk.ap, AF.Sqrt, [bk], [F('rn')], bias=1e-24, scale=1.0)
                self.recip(F('rn').ap, F('rn').ap, [F('rn')], [F('rn')])
                self.tt('dve', B('kk').ap, F('kkr').ap, F('rn').ap, ALU.mult, [F('kkr'), F('rn')], [B('kk')])
                for dd in range(2):
                    self.tt('dve', B('b%d' % dd).ap, B('kk').ap, F('as%d' % dd).ap, ALU.mult, [B('kk'), F('as%d' % dd)], [B('b%d' % dd)])
                    self.ts('dve', F('tmp').ap, F('as%d' % dd).ap, self.col('ka%d' % l, oc), self.col('omka%d' % l, oc),
                            ALU.mult, ALU.add, [F('as%d' % dd), self.cols], [F('tmp')])
                    self.tt('dve', F('kd%d' % dd).ap, F('tmp').ap, F('kl').ap, ALU.mult, [F('tmp'), F('kl')], [F('kd%d' % dd)])
                    self.cp('pool', B('kd%d' % dd).ap, F('kd%d' % dd).ap, [F('kd%d' % dd)], [B('kd%d' % dd)])
                self.tt('dve', F('tmp').ap, F('kd0').ap, F('kd1').ap, ALU.add, [F('kd0'), F('kd1')], [F('tmp')])
                self.stt('dve', B('rkb').ap, F('rf').ap, self.col('rk%d' % l, oc), F('tmp').ap, ALU.mult, ALU.mult,
                         [F('rf'), F('tmp'), self.cols], [B('rkb')])
                bk, _ = self.bank()
                self.mm(bk, [(bk.ap, blk, B('rkb').ap, True, True)], [B('rkb'), self.cb])
                self.tt('dve', B('bon').ap, bk.ap, F('vf').ap, ALU.mult, [bk, F('vf')], [B('bon')])
                self.dma_batch('sp', ('r1s', par), [
                    (d[dn][ocs, tok0:tok0 + T], B(nm).ap, [B(nm)], self.dres(dn, tok0, tok0 + T))
                    for nm, dn in (('r', 'rT'), ('v', 'vT'), ('g', 'gT'), ('kk', 'kkT'), ('bon', 'bonT'), ('sig0', 'sigT0'),
                                   ('sig1', 'sigT1'), ('b0', 'bT0'), ('b1', 'bT1'), ('kd0', 'kdT0'), ('kd1', 'kdT1'))])
        self.barrier()

    def odd_r2(self, l):
        self.reset()
        d = self.dram
        fm = lambda n: d[n].rearrange("(c p) s -> p c s", p=128)
        NSET = 3
        ident = self.cbv('ident')

        def mkset(i):
            b = {}
            for nm in ('rT', 'kkT', 'bT', 'kdT', 'sigT', 'vT'):
                b[nm] = self.carve(512 * 2, BF16)
            b['sigtok'] = self.carve(512 * 2, BF16)
            b['GG'] = self.carve(4 * 2 * 128 * 4, F32)
            b['Ginv'] = self.carve(512 * 4, F32)
            b['Ghat'] = self.carve(512 * 4, F32)
            b['AR'] = self.carve(1024 * 2, BF16)
            b['BK'] = self.carve(1024 * 2, BF16)
            b['BKh'] = self.carve(1024 * 2, BF16)
            b['Z0'] = self.carve(1024 * 2, BF16)
            b['Z1'] = self.carve(1024 * 2, BF16)
            for nm in ('bhtok', 'khtok', 'vtok'):
                b[nm] = self.carve(512 * 2, BF16)
            b['XT'] = self.carve(8 * 512 * 2, BF16)
            b['X0'] = self.carve(8 * 128 * 2, BF16)
            b['XX0'] = self.carve(8 * 256 * 2, BF16)
            b['XX1'] = self.carve(8 * 256 * 2, BF16)
            b['RH'] = self.carve(512 * 2, BF16)
            b['GT'] = self.carve(256 * 2, BF16)
            b['H'] = self.carve(256 * 4, F32)
            b['Ht'] = self.carve(256 * 4, F32)
            b['Hb'] = self.carve(256 * 2, BF16)
            b['ost'] = self.carve(512 * 2, BF16)
            b['id'] = i
            return b
        sets = [mkset(i) for i in range(NSET)]

        def unit(b, si, dr, cg, n):
            s0, S = self.seqs[si]
            t0 = s0 + 128 * n
            sid = b['id']
            cs = slice(4 * cg, 4 * cg + 4)
            sfx = str(dr)
            v3 = lambda nm, c: b[nm].ap.rearrange("p (c t) -> p c t", c=c)
            self.dma_batch('sp', ('r2l', sid), [
                (v3(nm, 4), fm(dn)[:, cs, t0:t0 + 128], self.dres(dn, t0, t0 + 128), [b[nm]])
                for nm, dn in (('rT', 'rT'), ('kkT', 'kkT'), ('bT', 'bT' + sfx), ('kdT', 'kdT' + sfx), ('sigT', 'sigT' + sfx), ('vT', 'vT'))])
            yield
            bk, bh = self.bank()
            bkb = bh.bitcast(BF16)
            self.tr(bk, [(bkb[:, c * 128:(c + 1) * 128], v3('sigT', 4)[:, c, :]) for c in range(4)], [b['sigT'], self.cb])
            self.cp('act', b['sigtok'].ap, bkb[:, 0:512], [bk], [b['sigtok']])
            yield
            tri = self.cbv('tri_f' if dr == 0 else 'tri_b')
            GG = b['GG'].ap.rearrange("p (c a t) -> p c a t", c=4, a=2)
            Ginv = v3('Ginv', 4)
            Ghat = v3('Ghat', 4)
            for j in range(2):
                bk, _ = self.bank()
                self.mm(bk, [(bk.ap[:, cc * 256:(cc + 1) * 256], b['sigtok'].ap[:, (2 * j + cc) * 128:(2 * j + cc + 1) * 128], tri, True, True)
                             for cc in range(2)], [b['sigtok'], self.cb])
                bv = bk.ap.rearrange("p (c a t) -> p c a t", c=2, a=2)
                self.act(b['GG'].ap[:, j * 512:(j + 1) * 512], bk.ap, AF.Exp, [bk], [b['GG']], scale=-KAPPA)
                self.act(Ginv[:, 2 * j:2 * j + 2, :], bv[:, :, 0, :], AF.Exp, [bk], [b['Ginv']], scale=KAPPA)
            tl = 127 if dr == 0 else 0
            gC = GG[:, :, 0, tl:tl + 1]
            self.tt('dve', Ghat, Ginv, gC.broadcast_to([128, 4, 128]), ALU.mult, [b['Ginv'], b['GG']], [b['Ghat']])
            yield
            AR = b['AR'].ap.rearrange("p (c a t) -> p c a t", c=4, a=2)
            BK = b['BK'].ap.rearrange("p (c a t) -> p c a t", c=4, a=2)
            BKh = b['BKh'].ap.rearrange("p (c a t) -> p c a t", c=4, a=2)
            self.stt('dve', AR[:, :, 0, :], v3('kkT', 4), -1.0, GG[:, :, 1, :], ALU.mult, ALU.mult, [b['kkT'], b['GG']], [b['AR']])
            self.tt('pool', AR[:, :, 1, :], v3('rT', 4), GG[:, :, 0, :], ALU.mult, [b['rT'], b['GG']], [b['AR']])
            self.tt('dve', BK[:, :, 0, :], v3('bT', 4), Ginv, ALU.mult, [b['bT'], b['Ginv']], [b['BK']])
            self.tt('pool', BK[:, :, 1, :], v3('kdT', 4), Ginv, ALU.mult, [b['kdT'], b['Ginv']], [b['BK']])
            self.tt('dve', BKh[:, :, 0, :], v3('bT', 4), Ghat, ALU.mult, [b['bT'], b['Ghat']], [b['BKh']])
            self.tt('pool', BKh[:, :, 1, :], v3('kdT', 4), Ghat, ALU.mult, [b['kdT'], b['Ghat']], [b['BKh']])
            yield
            Z = [b['Z0'], b['Z1']]
            Zv = [z.ap.rearrange("p (h x) -> p h x", h=8) for z in Z]
            bk, bh = self.bank()
            bkb = bh.bitcast(BF16)
            self.tr(bk, [(bkb[:, c * 128:(c + 1) * 128], AR[:, c, 0, :]) for c in range(4)], [b['AR'], self.cb])
            self.cp('act', Zv[0][:, :, 0:64], bkb[:, 0:512].rearrange("p (h x) -> p h x", h=8), [bk], [Z[0]])
            for src, srcb, dst, eng in ((lambda c: BKh[:, c, 0, :], b['BKh'], 'bhtok', 'dve'),
                                        (lambda c: BKh[:, c, 1, :], b['BKh'], 'khtok', 'act'),
                                        (lambda c: v3('vT', 4)[:, c, :], b['vT'], 'vtok', 'dve')):
                bk, bh = self.bank()
                bkb = bh.bitcast(BF16)
                self.tr(bk, [(bkb[:, c * 128:(c + 1) * 128], src(c)) for c in range(4)], [srcb, self.cb])
                self.cp(eng, b[dst].ap, bkb[:, 0:512], [bk], [b[dst]])
            yield
            XT = b['XT'].ap.rearrange("p (h x) -> p h x", h=8)
            X0 = b['X0'].ap.rearrange("p (h x) -> p h x", h=8)
            mT = self.cbv('mT_f' if dr == 0 else 'mT_b')
            mL = self.cbv('mL_f' if dr == 0 else 'mL_b')
            for h in range(8):
                c, half = h // 2, h % 2
                pr = slice(64 * half, 64 * half + 64)
                ar2 = AR[pr, c, :, :].rearrange("p a t -> p (a t)")
                bk, _ = self.bank()
                self.mm(bk, [(bk.ap[:, 0:256], BK[pr, c, 0, :], ar2, True, True),
                             (bk.ap[:, 256:512], BK[pr, c, 1, :], ar2, True, True)], [b['BK'], b['AR']])
                self.tt('dve' if h % 2 == 0 else 'dve', XT[:, h, :], bk.ap, mT, ALU.mult, [bk, self.cb], [b['XT']])
            X0s = b['X0'].ap.rearrange("p (c f t) -> p c f t", c=4, f=2)
            for half in range(2):
                bk, _ = self.bank()
                pr = slice(64 * half, 64 * half + 64)
                mms = [(bk.ap[:, c * 128:(c + 1) * 128], AR[pr, c, 0, :], BK[pr, c, 0, :], True, True) for c in range(4)]
                self.mm(bk, mms, [b['AR'], b['BK']])
                self.tt('dve', X0s[:, :, half, :], bk.ap.rearrange("p (c t) -> p c t", c=4),
                        mL.rearrange("p (c t) -> p c t", c=4), ALU.mult, [bk, self.cb], [b['X0']])
            yield
            bk, _ = self.bank()
            self.mm(bk, [(bk.ap[:, h * 64:(h + 1) * 64], XT[:, h, 256:384], b['vtok'].ap[:, h * 64:(h + 1) * 64], True, True)
                         for h in range(8)], [b['XT'], b['vtok']])
            self.cp('act', Zv[0][:, :, 64:128], bk.ap.rearrange("p (h x) -> p h x", h=8), [bk], [Z[0]])
            yield
            XX = [b['XX0'], b['XX1']]
            XXv = [x.ap.rearrange("p (h a t) -> p h a t", h=8, a=2) for x in XX]
            for lev in range(7):
                zc, zn = lev % 2, (lev + 1) % 2
                if lev == 0:
                    Xk = lambda h: X0[:, h, :]
                    XTk = lambda h: XT[:, h, 0:128]
                    xres = [b['X0'], b['XT']]
                else:
                    pv = XXv[(lev - 1) % 2]
                    Xk = lambda h, pv=pv: pv[:, h, 0, :]
                    XTk = lambda h, pv=pv: pv[:, h, 1, :]
                    xres = [XX[(lev - 1) % 2]]
                for j in range(2):
                    bk, _ = self.bank()
                    mms = [(bk.ap, ident, Z[zc].ap[:, j * 512:(j + 1) * 512], True, False)]
                    for hl in range(4):
                        h = 4 * j + hl
                        mms.append((bk.ap[:, hl * 128:(hl + 1) * 128], XTk(h), Zv[zc][:, h, :], False, hl == 3))
                    self.mm(bk, mms, xres + [Z[zc], self.cb])
                    self.cp('act' if j == 0 else 'dve', Z[zn].ap[:, j * 512:(j + 1) * 512], bk.ap, [bk], [Z[zn]])
                if lev < 6:
                    for j in range(4):
                        bk, _ = self.bank()
                        mms = []
                        for hl in range(2):
                            h = 2 * j + hl
                            mms.append((bk.ap[:, hl * 256:hl * 256 + 128], XTk(h), Xk(h), True, True))
                            mms.append((bk.ap[:, hl * 256 + 128:hl * 256 + 256], Xk(h), XTk(h), True, True))
                        self.mm(bk, mms, xres)
                        self.cp('act' if j % 2 == 0 else 'dve', XX[lev % 2].ap[:, j * 512:(j + 1) * 512], bk.ap, [bk], [XX[lev % 2]])
                yield
            Zf, Zfv = Z[1], Zv[1]
            bk, _ = self.bank()
            mms = []
            for h in range(8):
                c, half = h // 2, h % 2
                pr = slice(64 * half, 64 * half + 64)
                mms.append((bk.ap[pr, c * 128:(c + 1) * 128], Zfv[:, h, 0:64], XT[:, h, 128:256], True, True))
            self.mm(bk, mms, [Zf, b['XT']])
            self.tt('dve', v3('RH', 4), bk.ap.rearrange("p (c t) -> p c t", c=4), AR[:, :, 1, :], ALU.add, [bk, b['AR']], [b['RH']])
            bk, _ = self.bank()
            mms = []
            for h in range(8):
                c, half = h // 2, h % 2
                pr = slice(64 * half, 64 * half + 64)
                mms.append((bk.ap[pr, c * 64:(c + 1) * 64], Zfv[:, h, 0:64], b['bhtok'].ap[:, h * 64:(h + 1) * 64], True, True))
            self.mm(bk, mms, [Zf, b['bhtok']])
            self.cp('act', b['GT'].ap, bk.ap[:, 0:256], [bk], [b['GT']])
            yield
            Hb = v3('Hb', 4)
            RH = v3('RH', 4)
            GT = v3('GT', 4)
            bk, _ = self.bank()
            mms = []
            for h in range(8):
                c, half = h // 2, h % 2
                pr = slice(64 * half, 64 * half + 64)
                o = bk.ap[pr, c * 128:(c + 1) * 128]
                mms.append((o, Hb[pr, c, :], RH[pr, c, :], True, False))
                mms.append((o, Zfv[:, h, 64:128], XT[:, h, 128:256], False, False))
                mms.append((o, b['vtok'].ap[:, h * 64:(h + 1) * 64], XT[:, h, 384:512], False, True))
            self.mm(bk, mms, [b['Hb'], b['RH'], Zf, b['XT'], b['vtok']])
            self.cp('act', b['ost'].ap, bk.ap, [bk], [b['ost']])
            self.dma('sp', fm('oT' + sfx)[:, cs, t0:t0 + 128], v3('ost', 4), ('r2o', sid), [b['ost']], self.dres('oT' + sfx, t0, t0 + 128))
            bk, _ = self.bank()
            mms = []
            for h in range(8):
                c, half = h // 2, h % 2
                pr = slice(64 * half, 64 * half + 64)
                o = bk.ap[pr, c * 64:(c + 1) * 64]
                mms.append((o, GT[pr, c, :], Hb[pr, c, :], True, False))
                mms.append((o, b['bhtok'].ap[:, h * 64:(h + 1) * 64], Zfv[:, h, 64:128], False, False))
                mms.append((o, b['khtok'].ap[:, h * 64:(h + 1) * 64], b['vtok'].ap[:, h * 64:(h + 1) * 64], False, True))
            self.mm(bk, mms, [b['GT'], b['Hb'], b['bhtok'], b['khtok'], Zf, b['vtok']])
            Hv = v3('H', 4)
            Htv = v3('Ht', 4)
            self.tt('dve', Htv, Hv, gC.broadcast_to([128, 4, 64]), ALU.mult, [b['H'], b['GG']], [b['Ht']])
            self.tt('dve', Hv, Htv, bk.ap[:, 0:256].rearrange("p (c t) -> p c t", c=4), ALU.add, [bk, b['Ht']], [b['H']])
            self.cp('act', b['Hb'].ap, b['H'].ap, [b['H']], [b['Hb']])
            yield

        def stream(b, si, dr, cg):
            self.memset('pool', b['H'].ap, 0.0, [b['H']])
            self.memset('pool', b['Hb'].ap, 0.0, [b['Hb']])
            nch = self.seqs[si][1] // 128
            order = range(nch) if dr == 0 else range(nch - 1, -1, -1)
            for n in order:
                yield from unit(b, si, dr, cg, n)
        todo = [(si, dr, cg) for si in range(2) for dr in range(2) for cg in range(2)]
        active = []
        free = list(range(NSET))
        while todo or active:
            while todo and free:
                i = free.pop(0)
                active.append((i, stream(sets[i], *todo.pop(0))))
            nxt = []
            for (i, gen) in active:
                try:
                    next(gen)
                    nxt.append((i, gen))
                except StopIteration:
                    free.append(i)
            active = nxt
        self.barrier()

    def odd_r3(self, l):
        self.reset()
        T = 512
        d = self.dram
        fm = lambda n: d[n].rearrange("(c p) s -> p c s", p=128)
        v3 = lambda b, n: b.ap.rearrange("p (c t) -> p c t", c=n)
        xts = [self.carve(8 * T * 4, F32) for _ in range(2)]
        lds = {nm: [self.carve(8 * T * 2, BF16) for _ in range(2)] for nm in ('oT0', 'oT1', 'bonT', 'gT')}
        yfs = [self.carve(8 * T * 2, BF16) for _ in range(2)]
        tmp = {nm: [self.carve(T * 4, F32) for _ in range(2)] for nm in ('y', 'yc', 'sd', 'yn')}
        sqs = [self.carve(T * 2, BF16) for _ in range(2)]
        blk = self.cbv('blk')
        for ti in range(self.ntiles):
            tok0 = ti * T
            xt = xts[ti % 2]
            xt_ap = v3(xt, 8)
            self.dma('sp', xt_ap, fm('xs')[:, :, tok0:tok0 + T], ('x', ti % 2), self.dres('xs', tok0, tok0 + T), [xt])
            L = {nm: lds[nm][ti % 2] for nm in lds}
            self.dma_batch('sp', ('r3', ti % 2), [
                (v3(L[nm], 8), fm(nm)[:, :, tok0:tok0 + T], self.dres(nm, tok0, tok0 + T), [L[nm]]) for nm in lds])
            yf = yfs[ti % 2]
            for c in range(8):
                p = c % 2
                y, yc, sd, yn, sqb = tmp['y'][p], tmp['yc'][p], tmp['sd'][p], tmp['yn'][p], sqs[p]
                o0 = v3(L['oT0'], 8)[:, c, :]
                o1 = v3(L['oT1'], 8)[:, c, :]
                self.tt('dve', y.ap, o0, o1, ALU.add, [L['oT0'], L['oT1']], [y])
                bk, _ = self.bank()
                self.mm(bk, [(bk.ap, blk, o0, True, False), (bk.ap, blk, o1, False, True)], [L['oT0'], L['oT1'], self.cb])
                self.stt('dve', yc.ap, bk.ap, -1.0 / 64.0, y.ap, ALU.mult, ALU.add, [bk, y], [yc])
                self.act(sqb.ap, yc.ap, AF.Square, [yc], [sqb], scale=0.125)
                bk, _ = self.bank()
                self.mm(bk, [(bk.ap, blk, sqb.ap, True, True)], [sqb, self.cb])
                self.act(sd.ap, bk.ap, AF.Sqrt, [bk], [sd], bias=GN_EPS, scale=1.0)
                self.recip(sd.ap, sd.ap, [sd], [sd])
                self.tt('dve', yn.ap, yc.ap, sd.ap, ALU.mult, [yc, sd], [yn])
                self.act(yn.ap, yn.ap, AF.Identity, [yn, self.cols], [yn], bias=self.col('lnb%d' % l, c), scale=self.col('lnw%d' % l, c))
                self.tt('pool', yn.ap, yn.ap, v3(L['bonT'], 8)[:, c, :], ALU.add, [yn, L['bonT']], [yn])
                self.tt('pool', v3(yf, 8)[:, c, :], yn.ap, v3(L['gT'], 8)[:, c, :], ALU.mult, [yn, L['gT']], [yf])

            def ev_add(oc, bk):
                self.tt('dve', xt_ap[:, oc, :], bk.ap, xt_ap[:, oc, :], ALU.add, [bk, xt], [xt])
            self.linear(l, 'wo_r', 8, 8, lambda kc: v3(yf, 8)[:, kc, :], [yf], T, ev_add)
            self.dma('sp', fm('xs')[:, :, tok0:tok0 + T], xt_ap, ('xst', ti % 2), [xt], self.dres('xs', tok0, tok0 + T))
        self.barrier()

    def phase_load(self):
        self.reset()
        T = 512
        d = self.dram
        xts = [self.carve(8 * T * 4, F32) for _ in range(2)]
        for ti in range(self.ntiles):
            tok0 = ti * T
            xt = xts[ti % 2]
            xt_ap = xt.ap.rearrange("p (c t) -> p c t", c=8)
            src = d['xT'].rearrange("(c p) s -> p c s", p=128)[:, :, tok0:tok0 + T]
            self.dma('sp', xt_ap, src, ('x', ti % 2), [], [xt])
            dst = d['xs'].rearrange("(c p) s -> p c s", p=128)[:, :, tok0:tok0 + T]
            self.dma('sp', dst, xt_ap, ('xst', ti % 2), [xt], self.dres('xs', tok0, tok0 + T))
        self.barrier()

    def run(self):
        self.prologue()
        for ph in self.plan:
            fn = getattr(self, 'phase_' + ph[0], None) or getattr(self, ph[0])
            fn(*ph[1:])


def default_plan(depth):
    plan = [('load',)]
    for l in range(depth):
        if l % 2 == 0:
            plan += [('even', l)]
        else:
            plan += [('odd', l)]
        plan += [('caffn', l, l == depth - 1)]
    return plan


def build_program(S_P, S_S, depth, plan):
    nc = bass.Bass("TRN2", target_bir_lowering=False)
    Stot = S_P + S_S
    dram = {}

    def dt_(name, shape, dt, kind):
        dram[name] = nc.dram_tensor(name, shape, dt, kind=kind).ap()
    _, ncols = col_layout(depth)
    dt_('xT', [D, Stot], F32, 'ExternalInput')
    dt_('memT', [D, 2 * NMEM], F32, 'ExternalInput')
    dt_('cols', [128, ncols], F32, 'ExternalInput')
    dt_('cbf', [128, NCB], F32, 'ExternalInput')
    dt_('rope', [128, 2, S_P], F32, 'ExternalInput')
    for l in range(depth):
        tot = weight_offsets(l)[1]
        dt_('wpk%d' % l, [128, tot], F32, 'ExternalInput')
        dt_('wbf%d' % l, [128, tot], BF16, 'Internal')
    dt_('yT', [D, Stot], F32, 'ExternalOutput')
    dt_('xs', [D, Stot], F32, 'Internal')
    for n in ('qT', 'kT', 'uT', 'bgT', 'yaT'):
        dt_(n, [512, Stot], BF16, 'Internal')
    dt_('vtok', [Stot, 512], BF16, 'Internal')
    for n in ('rT', 'kkT', 'vT', 'gT', 'bonT', 'sigT0', 'sigT1', 'bT0', 'bT1', 'kdT0', 'kdT1', 'oT0', 'oT1'):
        dt_(n, [D, Stot], BF16, 'Internal')

    with ExitStack() as es:
        arena = es.enter_context(nc.sbuf_tensor("arena", [128, ARENA_BYTES // 2], BF16))
        banks = [es.enter_context(nc.psum_tensor("pb%d" % i, [128, 512], F32)) for i in range(8)]
        g1 = Gen(nc, S_P, S_S, depth, plan, arena, banks, dram, Sched(), None)
        g1.run()
        wplan = g1.wrec
        sch = Sched()
        g = Gen(nc, S_P, S_S, depth, plan, arena, banks, dram, sch, wplan)
        g.run()
        final_marks = dict(sch.pending)
        keys = set(ENG[:4])
        for e in ENG:
            for (waits, fn, sk, inc) in sch.q[e]:
                for k, v in waits:
                    keys.add(k)
                if sk is not None:
                    keys.add(sk)
        for k in final_marks:
            keys.add(k)
        sems = {}
        for i, k in enumerate(sorted(keys, key=str)):
            sems[k] = es.enter_context(nc.semaphore("s%d" % i))
        handles = {'pe': None, 'act': None, 'dve': None, 'pool': None, 'sp': None}

        def run_q(ename, h):
            for (waits, fn, sk, inc) in sch.q[ename]:
                for k, v in waits:
                    h.wait_ge(sems[k], v)
                if fn is not None:
                    ins = fn(h)
                    ins.then_inc(sems[sk], inc)
        with nc.Block() as block:
            @block.tensor
            def _(t):
                run_q('pe', t)

            @block.scalar
            def _(s):
                run_q('act', s)

            @block.vector
            def _(v):
                run_q('dve', v)

            @block.gpsimd
            def _(gp):
                run_q('pool', gp)

            @block.sync
            def _(sy):
                run_q('sp', sy)
                for k, v in final_marks.items():
                    sy.wait_ge(sems[k], v)
        info = dict(nops=sch.nops, nsems=len(sems), q={e: len(sch.q[e]) for e in ENG})
    return nc, info


def prep_inputs(inputs, S_P, S_S, depth, n_cores=8):
    p = {k: np.asarray(v, np.float32) for k, v in inputs.items()}
    shared = {'cols': pack_cols(depth, p), 'cbf': make_cbf(), 'rope': np.ascontiguousarray(make_rope(S_P))}
    for l in range(depth):
        shared['wpk%d' % l] = pack_weights(l, p)
    in_maps = []
    for i in range(n_cores):
        m = dict(shared)
        m['xT'] = np.ascontiguousarray(np.concatenate([p['x_prompt'][i].T, p['x_sample'][i].T], axis=1))
        m['memT'] = np.ascontiguousarray(np.concatenate([p['mem_prompt'][i].T, p['mem_sample'][i].T], axis=1))
        in_maps.append(m)
    return in_maps


def run_model(inputs, S_P, S_S, depth, plan, n_cores=8, trace=False):
    nc, info = build_program(S_P, S_S, depth, plan)
    in_maps = prep_inputs(inputs, S_P, S_S, depth, n_cores)
    res = run_bass_kernel_spmd(nc, in_maps, core_ids=list(range(n_cores)), trace=trace)
    yp = np.stack([res.results[i]['yT'][:, :S_P].T for i in range(n_cores)], axis=0)
    ys = np.stack([res.results[i]['yT'][:, S_P:].T for i in range(n_cores)], axis=0)
    return (np.ascontiguousarray(yp, dtype=np.float32), np.ascontiguousarray(ys, dtype=np.float32)), info, res


def kernel(**inputs):
    (yp, ys), info, res = run_model(inputs, 4096, 2048, 4, default_plan(4))
    return (yp, ys)
```

```python
import numpy as np
import concourse.bass as bass
import concourse.mybir as mybir
from concourse.bass_utils import run_bass_kernel_spmd
from contextlib import ExitStack

F32 = mybir.dt.float32
BF16 = mybir.dt.bfloat16
AF = mybir.ActivationFunctionType
ALU = mybir.AluOpType

D = 1024
NMEM = 256
DFF = 2816
KAPPA = 0.6065306597126334
RMS_EPS = 1e-6
GN_EPS = 64e-5
ROPE_THETA = 500000.0
ENG = ['pe', 'act', 'dve', 'pool', 'sp']
SAME_ENG_SYNC = False
NO_R1_STORES = False
R1_DEPTH = 2


class Res:
    __slots__ = ('w', 'r', 'const')

    def __init__(self, const=False):
        self.w = None
        self.r = {}
        self.const = const


class Buf:
    __slots__ = ('ap', 'res')

    def __init__(self, ap, const=False):
        self.ap = ap
        self.res = Res(const)


class Sched:
    def __init__(self):
        self.q = {e: [] for e in ENG}
        self.cnt = {e: 0 for e in ENG}
        self.waited = {e: {} for e in ENG}
        self.dcnt = {}
        self.pending = {}
        self.nops = 0

    def _collect(self, eng, reads, writes, is_dma=False):
        deps = {}
        for r in reads:
            d = r.w
            if d is not None and deps.get(d[0], 0) < d[1]:
                deps[d[0]] = d[1]
        for w in writes:
            d = w.w
            if d is not None and deps.get(d[0], 0) < d[1]:
                deps[d[0]] = d[1]
            for k, v in w.r.items():
                if deps.get(k, 0) < v:
                    deps[k] = v
        out = []
        wt = self.waited[eng]
        for k, v in deps.items():
            if k == eng and not is_dma and not SAME_ENG_SYNC:
                continue
            if wt.get(k, 0) >= v:
                continue
            wt[k] = v
            out.append((k, v))
        return out

    def _mark(self, done, reads, writes):
        k, v = done
        for r in reads:
            if not r.const and r.r.get(k, 0) < v:
                r.r[k] = v
        for w in writes:
            w.w = done
            w.r = {}

    def op(self, eng, fn, reads=(), writes=()):
        waits = self._collect(eng, reads, writes)
        self.cnt[eng] += 1
        done = (eng, self.cnt[eng])
        self._mark(done, reads, writes)
        self.q[eng].append((waits, fn, eng, 1))
        self.nops += 1

    def dma(self, qeng, fn, key, reads=(), writes=()):
        waits = self._collect(qeng, reads, writes, is_dma=True)
        k = ('d', key)
        c = self.dcnt.get(k, 0) + 1
        self.dcnt[k] = c
        done = (k, 16 * c)
        self._mark(done, reads, writes)
        self.pending[k] = 16 * c
        self.q[qeng].append((waits, fn, k, 16))
        self.nops += 1

    def wait_all(self, engs, marks):
        for e in engs:
            wt = self.waited[e]
            ws = []
            for k, v in marks.items():
                if wt.get(k, 0) >= v:
                    continue
                wt[k] = v
                ws.append((k, v))
            if ws:
                self.q[e].append((ws, None, None, 0))


def _chunks_lhsT(W, M):
    Kin, Nout = W.shape
    if Kin < 128:
        Wp = np.zeros((128, Nout), np.float32)
        Wp[:Kin] = W
        return [np.ascontiguousarray(Wp[:, o * M:(o + 1) * M]) for o in range(Nout // M)]
    KC = Kin // 128
    W3 = W.reshape(KC, 128, Nout)
    return [np.ascontiguousarray(W3[:, :, o * M:(o + 1) * M].transpose(1, 0, 2)).reshape(128, KC * M)
            for o in range(Nout // M)]


def weight_spec(l):
    sp = []
    if l % 2 == 0:
        sp += [('in_q', 4, 1024), ('in_k', 4, 1024), ('in_v', 2, 2048), ('in_bg', 4, 1024),
               ('in_cg', 4, 1024), ('in_h', 4, 1024), ('w_out', 8, 1024)]
    else:
        sp += [('wr', 8, 1024), ('wk', 8, 1024), ('wv', 8, 1024), ('wo_r', 8, 1024),
               ('w1_0', 1, 512), ('w1_1', 1, 512), ('a1_0', 1, 512), ('a1_1', 1, 512), ('g1', 1, 1024),
               ('w2_0', 1, 1024), ('w2_1', 1, 1024), ('a2_0', 1, 1024), ('a2_1', 1, 1024), ('g2', 1, 1024)]
    sp += [('ca_q', 8, 1024), ('ca_k', 8, 1024), ('ca_v', 4, 2048), ('ca_o', 8, 1024),
           ('ff_gu', 44, 1024), ('ff_d', 8, 2816)]
    return sp


def weight_offsets(l):
    off = {}
    o = 0
    for name, n, F in weight_spec(l):
        off[name] = (o, n, F)
        o += n * F
    return off, o


def pack_weights(l, p):
    ch = {}
    if l % 2 == 0:
        e = l // 2
        win = p['ab_w_in'][e]
        ch['in_q'] = _chunks_lhsT(win[:, 0:512], 128)
        ch['in_k'] = _chunks_lhsT(win[:, 512:1024], 128)
        ch['in_v'] = [h for a in _chunks_lhsT(win[:, 1024:1536], 512) for h in (a[:, 0:2048], a[:, 2048:4096])]
        ch['in_bg'] = _chunks_lhsT(win[:, 1536:2048], 128)
        ch['in_cg'] = _chunks_lhsT(win[:, 2048:2560], 128)
        ch['in_h'] = _chunks_lhsT(win[:, 2560:3072], 128)
        ch['w_out'] = _chunks_lhsT(p['ab_w_out'][e], 128)
    else:
        o = l // 2
        ch['wr'] = _chunks_lhsT(p['rw_wr'][o], 128)
        ch['wk'] = _chunks_lhsT(p['rw_wk'][o], 128)
        ch['wv'] = _chunks_lhsT(p['rw_wv'][o], 128)
        ch['wo_r'] = _chunks_lhsT(p['rw_wo'][o], 128)
        for d in range(2):
            ch['w1_%d' % d] = _chunks_lhsT(p['rw_w1'][o, d], 64)
            ch['a1_%d' % d] = _chunks_lhsT(p['rw_a1'][o, d], 64)
            ch['w2_%d' % d] = _chunks_lhsT(p['rw_w2'][o, d], 1024)
            ch['a2_%d' % d] = _chunks_lhsT(p['rw_a2'][o, d], 1024)
        ch['g1'] = _chunks_lhsT(p['rw_g1'][o], 128)
        ch['g2'] = _chunks_lhsT(p['rw_g2'][o], 1024)
    ch['ca_q'] = _chunks_lhsT(p['ca_wq'][l], 128)
    ch['ca_k'] = _chunks_lhsT(p['ca_wkv'][l][:, 0:1024], 128)
    ch['ca_v'] = [h for a in _chunks_lhsT(p['ca_wkv'][l][:, 1024:2048], 512) for h in (a[:, 0:2048], a[:, 2048:4096])]
    ch['ca_o'] = _chunks_lhsT(p['ca_wo'][l], 128)
    ch['ff_gu'] = _chunks_lhsT(p['ffn_wgu'][l], 128)
    ch['ff_d'] = _chunks_lhsT(p['ffn_wdown'][l], 128)
    parts = []
    for name, n, F in weight_spec(l):
        assert len(ch[name]) == n, (name, len(ch[name]), n)
        for a in ch[name]:
            assert a.shape == (128, F), (name, a.shape, F)
            parts.append(a)
    return np.ascontiguousarray(np.concatenate(parts, axis=1))


def col_layout(depth):
    names = []
    for l in range(depth):
        names += [('nmix%d' % l, 8), ('ncross%d' % l, 8), ('nmem%d' % l, 8), ('nffn%d' % l, 8)]
        if l % 2 == 0:
            names += [('conv%d' % l, 12)]
        else:
            names += [('mu%d' % l, 48), ('w0_%d' % l, 16), ('a0_%d' % l, 16), ('kk%d' % l, 8), ('ka%d' % l, 8),
                      ('rk%d' % l, 8), ('lnw%d' % l, 8), ('lnb%d' % l, 8), ('omka%d' % l, 8)]
    names += [('nfinal', 8)]
    off = {}
    o = 0
    for n, k in names:
        off[n] = o
        o += k
    return off, o


def _colv(v):
    return np.asarray(v, np.float32).reshape(-1, 128).T


def pack_cols(depth, p):
    off, n = col_layout(depth)
    cols = np.zeros((128, n), np.float32)

    def put(name, arr):
        cols[:, off[name]:off[name] + arr.shape[1]] = arr
    for l in range(depth):
        put('nmix%d' % l, _colv(p['norm_mix'][l]))
        put('ncross%d' % l, _colv(p['norm_cross'][l]))
        put('nmem%d' % l, _colv(p['norm_mem'][l]))
        put('nffn%d' % l, _colv(p['norm_ffn'][l]))
        if l % 2 == 0:
            cw = p['ab_conv'][l // 2]
            put('conv%d' % l, np.concatenate([_colv(cw[j]) for j in range(3)], axis=1))
        else:
            o = l // 2
            put('mu%d' % l, np.concatenate([_colv(p['rw_mu'][o, i]) for i in range(6)], axis=1))
            put('w0_%d' % l, np.concatenate([_colv(p['rw_w0'][o, d]) for d in range(2)], axis=1))
            put('a0_%d' % l, np.concatenate([_colv(p['rw_a0'][o, d]) for d in range(2)], axis=1))
            put('kk%d' % l, _colv(p['rw_kk'][o]))
            put('ka%d' % l, _colv(p['rw_ka'][o]))
            put('rk%d' % l, _colv(p['rw_rk'][o].reshape(-1)))
            put('lnw%d' % l, _colv(p['rw_lnx_w'][o]))
            put('lnb%d' % l, _colv(p['rw_lnx_b'][o]))
    put('nfinal', _colv(p['norm_final']))
    return cols


CB = {}
_o = 0
for _n, _k in [('ident', 128), ('ones', 128), ('blk', 128), ('pswap', 128), ('tri_f', 256), ('tri_b', 256),
               ('mT_f', 512), ('mL_f', 512), ('mT_b', 512), ('mL_b', 512),
               ('am_n', 512), ('am_f', 512), ('am_l', 512), ('am_fl', 512), ('am_s', 512)]:
    CB[_n] = (_o, _k)
    _o += _k
NCB = _o


def make_cbf():
    c = np.zeros((128, NCB), np.float32)

    def put(n, a):
        o, k = CB[n]
        assert a.shape == (128, k), (n, a.shape)
        c[:, o:o + k] = a
    I = np.eye(128, dtype=np.float32)
    put('ident', I)
    put('ones', np.ones((128, 128), np.float32))
    blk = np.zeros((128, 128), np.float32)
    blk[0:64, 0:64] = 1
    blk[64:128, 64:128] = 1
    put('blk', blk)
    ps = np.zeros((128, 128), np.float32)
    for m in range(128):
        j = m % 64
        if j < 8:
            ps[m + 8, m] = 1
        elif j < 16:
            ps[m - 8, m] = 1
    put('pswap', ps)
    s = np.arange(128)[:, None]
    t = np.arange(128)[None, :]
    put('tri_f', np.concatenate([(s <= t), (s < t)], axis=1).astype(np.float32))
    put('tri_b', np.concatenate([(s >= t), (s > t)], axis=1).astype(np.float32))
    stf, inf_ = (s < t).astype(np.float32), (s <= t).astype(np.float32)
    stb, inb = (s > t).astype(np.float32), (s >= t).astype(np.float32)
    put('mT_f', np.concatenate([stf, inf_, stf, inf_], axis=1))
    put('mT_b', np.concatenate([stb, inb, stb, inb], axis=1))
    Lf = (t < s).astype(np.float32)
    Lb = (t > s).astype(np.float32)
    put('mL_f', np.concatenate([Lf] * 4, axis=1))
    put('mL_b', np.concatenate([Lb] * 4, axis=1))
    jj = np.arange(128)[:, None]
    i1 = np.arange(128)[None, :]
    i2 = np.arange(256)[None, :]
    M0 = ((jj - i1 >= 0) & (jj - i1 <= 128)).astype(np.float32)
    M1 = ((i2 - jj >= 0) & (i2 - jj <= 128)).astype(np.float32)
    M2 = M1[:, 0:128]
    lo = (jj >= 64).astype(np.float32)
    hi = (jj < 64).astype(np.float32)
    put('am_n', np.concatenate([M0, M1, M2], axis=1))
    put('am_f', np.concatenate([M0 * lo, M1, M2], axis=1))
    put('am_l', np.concatenate([M0, M1, M2 * hi], axis=1))
    put('am_fl', np.concatenate([M0 * lo, M1, M2 * hi], axis=1))
    put('am_s', np.concatenate([M0 * lo, M1[:, 0:128] * hi, np.zeros((128, 256), np.float32)], axis=1))
    return c


def make_rope(S):
    half = 8
    inv = np.power(np.float32(ROPE_THETA), (np.float32(-2.0) * np.arange(half, dtype=np.float32) / np.float32(16.0))).astype(np.float32)
    pos = np.arange(S, dtype=np.float32)
    ang = (pos[None, :] * inv[:, None]).astype(np.float32).astype(np.float64)
    cos = np.ones((128, S), np.float32)
    sin = np.zeros((128, S), np.float32)
    for pbase in (0, 64):
        cos[pbase:pbase + 8] = np.cos(ang)
        cos[pbase + 8:pbase + 16] = np.cos(ang)
        sin[pbase:pbase + 8] = -np.sin(ang)
        sin[pbase + 8:pbase + 16] = np.sin(ang)
    return np.stack([cos, sin], axis=1)


ARENA_BYTES = 204 * 1024
NWS = 5
WS_BYTES = 5632
WLOOK = 3


def _bp(ap):
    b = ap.base_partition
    return b() if callable(b) else b


class Gen:
    def __init__(self, nc, S_P, S_S, depth, plan, arena, banks, dram, sch, wplan):
        self.nc = nc
        self.S_P, self.S_S = S_P, S_S
        self.Stot = S_P + S_S
        self.seqs = [(0, S_P), (S_P, S_S)]
        self.depth = depth
        self.plan = plan
        self.arena = arena
        self.dram = dram
        self.sch = sch
        self.T = 512
        self.ntiles = self.Stot // 512
        self.coff, self.ncols = col_layout(depth)
        self.woff = [weight_offsets(l) for l in range(depth)]
        self.dres_ = {}
        self.wres = [Res() for _ in range(depth)]
        self.pbank = 0
        self.banks = [Buf(b[:, :]) for b in banks]
        self.bank_h = banks
        self.bankres = set(id(b.res) for b in self.banks)
        self.top = 0
        self.cb = self.carve(NCB * 2, BF16, const=True)
        self.cols = self.carve(self.ncols * 4, F32, const=True)
        self.wslots = [self.carve(WS_BYTES, BF16) for _ in range(NWS)]
        self.scr = {e: self.carve(16, F32) for e in ('act', 'dve', 'pool')}
        self.base = self.top
        self.wplan = wplan
        self.wrec = []
        self.wpos = 0
        self.wissued = 0

    def carve(self, nbytes, dt, const=False, shape=None):
        off = self.top
        self.top += (nbytes + 31) // 32 * 32
        assert self.top <= ARENA_BYTES, ('arena overflow', self.top)
        ap = self.arena[:, off // 2:(off + nbytes) // 2]
        if dt == F32:
            ap = ap.bitcast(F32)
        b = Buf(ap, const)
        return b

    def reset(self):
        self.top = self.base

    def bank(self):
        b = self.banks[self.pbank]
        h = self.bank_h[self.pbank]
        self.pbank = (self.pbank + 1) % len(self.banks)
        return b, h

    def cbv(self, name):
        o, k = CB[name]
        return self.cb.ap[:, o:o + k]

    def col(self, name, i):
        o = self.coff[name] + i
        return self.cols.ap[:, o:o + 1]

    def dres(self, name, t0, t1):
        out = []
        for t in range(t0 // 512, (t1 + 511) // 512):
            k = (name, t)
            if k not in self.dres_:
                self.dres_[k] = Res()
            out.append(self.dres_[k])
        return out

    def op(self, eng, fn, reads=(), writes=()):
        rr = [b.res if isinstance(b, Buf) else b for b in reads]
        ww = [b.res if isinstance(b, Buf) else b for b in writes]
        for r in rr:
            if id(r) in self.bankres and r not in ww:
                ww.append(r)
        self.sch.op(eng, fn, rr, ww)

    def dma(self, q, out_ap, in_ap, key, reads=(), writes=()):
        def fn(e, o=out_ap, i=in_ap):
            return e.dma_start(out=o, in_=i)
        self.sch.dma(q, fn, key, [b.res if isinstance(b, Buf) else b for b in reads],
                     [b.res if isinstance(b, Buf) else b for b in writes])

    def dma_batch(self, q, key, items):
        rs, ws = [], []
        for (o, i, rd, wr) in items:
            self.dma(q, o, i, key, rd, wr)
            rs += [b.res if isinstance(b, Buf) else b for b in rd]
            ws += [b.res if isinstance(b, Buf) else b for b in wr]
        k = ('d', key)
        final = self.sch.dcnt[k] * 16
        for r in ws:
            r.w = (k, final)
        for r in rs:
            if not r.const:
                r.r[k] = final

    def mm(self, bank, mms, reads):
        def fn(t, mms=mms):
            ins = None
            for (o, l, r, st, sp) in mms:
                kw = {}
                rp, cp = _bp(l), _bp(o)
                if rp != 0 or cp != 0:
                    kw['tile_position'] = (rp, cp)
                ins = t.matmul(o, l, r, start=st, stop=sp, skip_group_check=True, **kw)
            return ins
        self.op('pe', fn, reads, [bank])

    def tr(self, bank, trs, reads):
        idn = self.cbv('ident')

        def fn(t, trs=trs):
            ins = None
            for (o, i) in trs:
                ins = t.transpose(o, i, idn)
            return ins
        self.op('pe', fn, reads, [bank])

    def act(self, out, in_, func, reads, writes, bias=None, scale=None, eng='act'):
        kw = {}
        if bias is not None:
            kw['bias'] = bias
        if scale is not None:
            kw['scale'] = scale

        def fn(e, out=out, in_=in_):
            return e.activation(out=out, in_=in_, func=func, **kw)
        self.op('act', fn, reads, writes)

    def tt(self, eng, out, in0, in1, op, reads, writes):
        def fn(e, out=out, in0=in0, in1=in1):
            return e.tensor_tensor(out=out, in0=in0, in1=in1, op=op)
        self.op(eng, fn, reads, writes)

    def ts(self, eng, out, in0, s1, s2, op0, op1, reads, writes):
        def fn(e, out=out, in0=in0):
            if s2 is None:
                return e.tensor_scalar(out, in0, s1, None, op0)
            return e.tensor_scalar(out, in0, s1, s2, op0, op1)
        self.op(eng, fn, reads, writes)

    def stt(self, eng, out, in0, scalar, in1, op0, op1, reads, writes):
        eng = 'dve'
        def fn(e, out=out, in0=in0, in1=in1):
            return e.scalar_tensor_tensor(out=out, in0=in0, scalar=scalar, in1=in1, op0=op0, op1=op1)
        self.op(eng, fn, reads, writes)

    def cp(self, eng, out, in_, reads, writes):
        if eng == 'act':
            def fn(e, out=out, in_=in_):
                return e.copy(out, in_)
        else:
            def fn(e, out=out, in_=in_):
                return e.tensor_copy(out=out, in_=in_)
        self.op(eng, fn, reads, writes)

    def recip(self, out, in_, reads, writes):
        def fn(e, out=out, in_=in_):
            return e.reciprocal(out, in_)
        self.op('dve', fn, reads, writes)

    def memset(self, eng, ap, val, writes):
        def fn(e, ap=ap):
            return e.memset(ap, val)
        self.op(eng, fn, [], writes)

    def pipeline(self, gens, depth):
        gens = iter(gens)
        active = []
        while True:
            while len(active) < depth:
                g = next(gens, None)
                if g is None:
                    break
                active.append(g)
            if not active:
                break
            nxt = []
            for g in active:
                try:
                    next(g)
                    nxt.append(g)
                except StopIteration:
                    pass
            active = nxt

    def barrier(self):
        s = self.sch
        b0 = self.banks[0]

        idn = self.cbv('ident')
        bk = self.bank_h[0]

        def pe_fn(t):
            return t.matmul(bk[0:1, 0:1], idn[0:1, 0:1], idn[0:1, 0:1], start=True, stop=True, skip_group_check=True)
        self.op('pe', pe_fn, [self.cb], [b0])
        self.op('act', lambda e: e.memzero(self.scr['act'].ap), [], [self.scr['act']])
        self.op('dve', lambda e: e.memset(self.scr['dve'].ap, 0.0), [], [self.scr['dve']])
        self.op('pool', lambda e: e.memset(self.scr['pool'].ap, 0.0), [], [self.scr['pool']])
        marks = dict(s.pending)
        for e in ('pe', 'act', 'dve', 'pool'):
            marks[e] = s.cnt[e]
        s.wait_all(ENG, marks)
        s.pending = {}

    def _wissue(self, i, req):
        l, name, idx = req
        o, n, F = self.woff[l][0][name]
        slot = self.wslots[i % NWS]
        src = self.dram['wbf%d' % l][:, o + idx * F:o + (idx + 1) * F]
        self.dma('sp', slot.ap[:, 0:F], src, ('w', i % NWS), [self.wres[l]], [slot])
        self.wissued = i + 1

    def wget(self, l, name, idx, M=None):
        req = (l, name, idx)
        i = self.wpos
        self.wpos += 1
        if self.wplan is None:
            self.wrec.append(req)
            self._wissue(i, req)
        else:
            assert self.wplan[i] == req, (i, self.wplan[i], req)
            tgt = min(i + WLOOK, len(self.wplan) - 1)
            while self.wissued <= tgt:
                self._wissue(self.wissued, self.wplan[self.wissued])
        slot = self.wslots[i % NWS]
        F = self.woff[l][0][name][2]
        ap = slot.ap[:, 0:F]
        if M is not None:
            ap = ap.rearrange("p (k m) -> p k m", m=M)
        return slot, ap

    def prologue(self):
        d = self.dram
        self.dma('pool', self.cb.ap, d['cbf'], 'c0', [], [self.cb])
        self.dma('sp', self.cols.ap, d['cols'], 'c1', [], [self.cols])
        PIECE = 8192
        for l in range(self.depth):
            tot = self.woff[l][1]
            for c0 in range(0, tot, PIECE):
                c1 = min(tot, c0 + PIECE)
                self.dma('pool', d['wbf%d' % l][:, c0:c1], d['wpk%d' % l][:, c0:c1], 'wc%d' % l, [], [self.wres[l]])
        for l in range(self.depth):
            if l % 2 == 1:
                o1 = self.coff['omka%d' % l]
                o0 = self.coff['ka%d' % l]
                self.ts('dve', self.cols.ap[:, o1:o1 + 8], self.cols.ap[:, o0:o0 + 8], -1.0, 1.0, ALU.mult, ALU.add,
                        [self.cols], [self.cols])

    def rmsnorm(self, xt, xt_ap, ncolsT, gname, sq, sq_ap, rstd, rstd_ap, outs):
        n = ncolsT
        self.act(sq_ap, xt_ap, AF.Square, [xt], [sq], scale=1.0 / 32.0)
        bk, _ = self.bank()
        ones = self.cbv('ones')
        self.mm(bk, [(bk.ap[:, 0:n], ones, sq_ap[:, c, :], c == 0, c == 7) for c in range(8)], [sq, self.cb])
        self.act(rstd_ap, bk.ap[:, 0:n], AF.Sqrt, [bk], [rstd], bias=RMS_EPS, scale=1.0)
        self.recip(rstd_ap, rstd_ap, [rstd], [rstd])
        for (engs, ob, oap) in outs:
            for c in range(8):
                self.stt(engs[c % len(engs)], oap[:, c, :], xt_ap[:, c, :], self.col(gname, c), rstd_ap,
                         ALU.mult, ALU.mult, [xt, rstd, self.cols], [ob])

    def linear(self, l, wname, n_oc, KC, rhs_fn, rhs_reads, N, evac, oc0=0, M=128, Kp=128):
        for oc in range(n_oc):
            slot, w = self.wget(l, wname, oc0 + oc, M=M)
            bk, _ = self.bank()
            self.mm(bk, [(bk.ap[0:M, 0:N], w[0:Kp, kc, :], rhs_fn(kc), kc == 0, kc == KC - 1) for kc in range(KC)],
                    [slot] + rhs_reads)
            evac(oc, bk)

    def linear_gen(self, l, wname, n_oc, KC, rhs_fn, rhs_reads, N, evac, oc0=0, M=128, Kp=128, every=2):
        for oc in range(n_oc):
            slot, w = self.wget(l, wname, oc0 + oc, M=M)
            bk, _ = self.bank()
            self.mm(bk, [(bk.ap[0:M, 0:N], w[0:Kp, kc, :], rhs_fn(kc), kc == 0, kc == KC - 1) for kc in range(KC)],
                    [slot] + rhs_reads)
            evac(oc, bk)
            if oc % every == every - 1:
                yield

    def phase_caffn(self, l, final):
        self.reset()
        T = 512
        d = self.dram
        xts = [self.carve(8 * T * 4, F32) for _ in range(2)]
        sq = self.carve(8 * T * 2, BF16)
        xn = self.carve(8 * T * 2, BF16)
        qT = self.carve(8 * T * 2, BF16)
        oT = self.carve(8 * T * 2, BF16)
        pTs = [self.carve(2 * T * 2, BF16) for _ in range(2)]
        rdens = [self.carve(T * 4, F32) for _ in range(2)]
        rstd = self.carve(T * 4, F32)
        sgs = [self.carve(T * 2, BF16) for _ in range(2)]
        hb = self.carve(22 * T * 2, BF16)
        kTs = [self.carve(8 * 256 * 2, BF16) for _ in range(2)]
        vtoks = [self.carve(2 * 1024 * 2, BF16) for _ in range(2)]
        ystage = self.carve(8 * T * 4, F32) if final else None
        v3 = lambda b, n: b.ap.rearrange("p (c t) -> p c t", c=n)

        mns = []
        for si in range(2):
            mt = xts[si]
            mt_ap = mt.ap[:, 0:8 * 256].rearrange("p (c t) -> p c t", c=8)
            src = d['memT'].rearrange("(c p) t -> p c t", p=128)[:, :, si * 256:(si + 1) * 256]
            self.dma('sp', mt_ap, src, ('x', si), [], [mt])
            mn = sq if si == 0 else xn
            mn_ap = mn.ap[:, 0:8 * 256].rearrange("p (c t) -> p c t", c=8)
            sq_ap = qT.ap[:, 0:8 * 256].rearrange("p (c t) -> p c t", c=8)
            self.rmsnorm(mt, mt_ap, 256, 'nmem%d' % l, qT, sq_ap, rstd, rstd.ap[:, 0:256], [(('dve', 'pool'), mn, mn_ap)])
            mns.append((mn, mn_ap))
        for oc in range(8):
            slot, w = self.wget(l, 'ca_k', oc, M=128)
            for si in range(2):
                mn, mn_ap = mns[si]
                bk, _ = self.bank()
                self.mm(bk, [(bk.ap[:, 0:256], w[:, kc, :], mn_ap[:, kc, :], kc == 0, kc == 7) for kc in range(8)], [slot, mn])
                self.cp('act', v3(kTs[si], 8)[:, oc, :], bk.ap[:, 0:256], [bk], [kTs[si]])
        for half in range(2):
            slot0, w0 = self.wget(l, 'ca_v', 2 * half, M=512)
            slot1, w1 = self.wget(l, 'ca_v', 2 * half + 1, M=512)
            wk = lambda kc: (w0[:, kc, :] if kc < 4 else w1[:, kc - 4, :])
            for si in range(2):
                mn, mn_ap = mns[si]
                for blk in range(2):
                    bk, _ = self.bank()
                    self.mm(bk, [(bk.ap[:, 0:512], mn_ap[:, kc, blk * 128:(blk + 1) * 128], wk(kc), kc == 0, kc == 7)
                                 for kc in range(8)], [slot0, slot1, mn])
                    self.cp('dve', v3(vtoks[si], 2)[:, blk, half * 512:(half + 1) * 512], bk.ap[:, 0:512], [bk], [vtoks[si]])

        ones = self.cbv('ones')
        xsrc = 'xs'
        for ti in range(self.ntiles):
            tok0 = ti * T
            si = 0 if tok0 < self.S_P else 1
            xt = xts[ti % 2]
            xt_ap = v3(xt, 8)
            src = d[xsrc].rearrange("(c p) s -> p c s", p=128)[:, :, tok0:tok0 + T]
            self.dma('sp', xt_ap, src, ('x', ti % 2), self.dres(xsrc, tok0, tok0 + T), [xt])
            xn_ap = v3(xn, 8)
            sq_ap = v3(sq, 8)
            self.rmsnorm(xt, xt_ap, T, 'ncross%d' % l, sq, sq_ap, rstd, rstd.ap, [(('dve', 'pool'), xn, xn_ap)])
            qT_ap = v3(qT, 8)
            oT_ap = v3(oT, 8)
            kT_ap = v3(kTs[si], 8)
            vt_ap = v3(vtoks[si], 2)

            def ev_q(oc, bk):
                self.cp('act', qT_ap[:, oc, :], bk.ap, [bk], [qT])
            self.linear(l, 'ca_q', 8, 8, lambda kc: xn_ap[:, kc, :], [xn], T, ev_q)
            for h in range(4):
                pT = pTs[h % 2]
                pT_ap = v3(pT, 2)
                rden = rdens[h % 2]
                for blk in range(2):
                    bk, _ = self.bank()
                    self.mm(bk, [(bk.ap, kT_ap[:, 2 * h + dc, blk * 128:(blk + 1) * 128], qT_ap[:, 2 * h + dc, :], dc == 0, dc == 1)
                                 for dc in range(2)], [kTs[si], qT])
                    self.act(pT_ap[:, blk, :], bk.ap, AF.Exp, [bk], [pT], scale=1.0 / 16.0)
                bk, _ = self.bank()
                self.mm(bk, [(bk.ap, ones, pT_ap[:, blk, :], blk == 0, blk == 1) for blk in range(2)], [pT, self.cb])
                self.recip(rden.ap, bk.ap, [bk], [rden])
                for dc in range(2):
                    bk, _ = self.bank()
                    c = 2 * h + dc
                    self.mm(bk, [(bk.ap, vt_ap[:, blk, c * 128:(c + 1) * 128], pT_ap[:, blk, :], blk == 0, blk == 1)
                                 for blk in range(2)], [vtoks[si], pT])
                    self.tt('dve', oT_ap[:, c, :], bk.ap, rden.ap, ALU.mult, [bk, rden], [oT])

            def ev_add(oc, bk):
                self.tt('dve', xt_ap[:, oc, :], bk.ap, xt_ap[:, oc, :], ALU.add, [bk, xt], [xt])
            self.linear(l, 'ca_o', 8, 8, lambda kc: oT_ap[:, kc, :], [oT], T, ev_add)
            self.rmsnorm(xt, xt_ap, T, 'nffn%d' % l, sq, sq_ap, rstd, rstd.ap, [(('dve', 'pool'), xn, xn_ap)])
            h_ap = v3(hb, 22)
            for c in range(22):
                slg, wg = self.wget(l, 'ff_gu', c, M=128)
                bg, _ = self.bank()
                self.mm(bg, [(bg.ap, wg[:, kc, :], xn_ap[:, kc, :], kc == 0, kc == 7) for kc in range(8)], [slg, xn])
                slu, wu = self.wget(l, 'ff_gu', 22 + c, M=128)
                bu, _ = self.bank()
                self.mm(bu, [(bu.ap, wu[:, kc, :], xn_ap[:, kc, :], kc == 0, kc == 7) for kc in range(8)], [slu, xn])
                sg = sgs[c % 2]
                self.act(sg.ap, bg.ap, AF.Silu, [bg], [sg])
                self.tt('dve', h_ap[:, c, :], bu.ap, sg.ap, ALU.mult, [bu, sg], [hb])
            self.linear(l, 'ff_d', 8, 22, lambda kc: h_ap[:, kc, :], [hb], T, ev_add)
            if not final:
                dst = d['xs'].rearrange("(c p) s -> p c s", p=128)[:, :, tok0:tok0 + T]
                self.dma('sp', dst, xt_ap, ('xst', ti % 2), [xt], self.dres('xs', tok0, tok0 + T))
            else:
                ys_ap = v3(ystage, 8)
                self.rmsnorm(xt, xt_ap, T, 'nfinal', sq, sq_ap, rstd, rstd.ap, [(('dve', 'pool'), ystage, ys_ap)])
                dst = d['yT'].rearrange("(c p) s -> p c s", p=128)[:, :, tok0:tok0 + T]
                self.dma('sp', dst, ys_ap, ('yst', 0), [ystage], self.dres('yT', tok0, tok0 + T))
        self.barrier()

    def phase_even(self, l):
        self.even_e1(l)
        self.even_e2(l)
        self.even_e3(l)

    def even_e1(self, l):
        self.reset()
        T = 512
        d = self.dram
        v3 = lambda b, n: b.ap.rearrange("p (c t) -> p c t", c=n)
        xts = [self.carve(8 * T * 4, F32) for _ in range(2)]
        sqs_ = [self.carve(8 * T * 2, BF16) for _ in range(2)]
        xns_ = [self.carve(8 * T * 2, BF16) for _ in range(2)]
        rstds_ = [self.carve(T * 4, F32) for _ in range(2)]
        rope = self.carve(2 * self.S_P * 4, F32, const=True)
        rope_ap = rope.ap.rearrange("p (a s) -> p a s", a=2)
        qraws = [self.carve(T * 2, BF16) for _ in range(2)]
        t1s = [self.carve(T * 4, F32) for _ in range(2)]
        t2s = [self.carve(T * 4, F32) for _ in range(2)]
        cgs = [self.carve(T * 4, F32) for _ in range(2)]
        stg = {n: [self.carve(4 * T * 2, BF16) for _ in range(2)] for n in ('q', 'k', 'u', 'bg', 'v')}
        self.dma('sp', rope_ap, d['rope'], 'c2', [], [rope])
        pswap = self.cbv('pswap')
        cnt = 0

        def tile(ti):
            nonlocal cnt
            sq, xn, rstd = sqs_[ti % 2], xns_[ti % 2], rstds_[ti % 2]
            tok0 = ti * T
            s0 = 0 if tok0 < self.S_P else self.S_P
            pos0 = tok0 - s0
            xt = xts[ti % 2]
            xt_ap = v3(xt, 8)
            src = d['xs'].rearrange("(c p) s -> p c s", p=128)[:, :, tok0:tok0 + T]
            self.dma('sp', xt_ap, src, ('x', ti % 2), self.dres('xs', tok0, tok0 + T), [xt])
            xn_ap = v3(xn, 8)
            self.rmsnorm(xt, xt_ap, T, 'nmix%d' % l, sq, v3(sq, 8), rstd, rstd.ap, [(('dve',), xn, xn_ap)])
            yield
            for wname, sn, dn in (('in_q', 'q', 'qT'), ('in_k', 'k', 'kT')):
                st = stg[sn][ti % 2]
                st_ap = v3(st, 4)
                for oc in range(4):
                    slot, w = self.wget(l, wname, oc, M=128)
                    bq, _ = self.bank()
                    self.mm(bq, [(bq.ap, w[:, kc, :], xn_ap[:, kc, :], kc == 0, kc == 7) for kc in range(8)], [slot, xn])
                    qraw = qraws[cnt % 2]
                    t1 = t1s[cnt % 2]
                    t2 = t2s[cnt % 2]
                    cnt += 1
                    self.cp('act', qraw.ap, bq.ap, [bq], [qraw])
                    bp, _ = self.bank()
                    self.mm(bp, [(bp.ap, pswap, qraw.ap, True, True)], [qraw, self.cb])
                    self.tt('dve', t1.ap, bq.ap, rope_ap[:, 0, pos0:pos0 + T], ALU.mult, [bq, rope], [t1])
                    self.tt('dve', t2.ap, bp.ap, rope_ap[:, 1, pos0:pos0 + T], ALU.mult, [bp, rope], [t2])
                    self.tt('pool', st_ap[:, oc, :], t1.ap, t2.ap, ALU.add, [t1, t2], [st])
                    yield
                dst = d[dn].rearrange("(c p) s -> p c s", p=128)[:, :, tok0:tok0 + T]
                self.dma('sp', dst, st_ap, ('st' + sn, ti % 2), [st], self.dres(dn, tok0, tok0 + T))
            slot0, w0 = self.wget(l, 'in_v', 0, M=512)
            slot1, w1 = self.wget(l, 'in_v', 1, M=512)
            wk = lambda kc: (w0[:, kc, :] if kc < 4 else w1[:, kc - 4, :])
            st = stg['v'][ti % 2]
            st_ap = v3(st, 4)
            for tb in range(4):
                bk, _ = self.bank()
                self.mm(bk, [(bk.ap, xn_ap[:, kc, tb * 128:(tb + 1) * 128], wk(kc), kc == 0, kc == 7) for kc in range(8)],
                        [slot0, slot1, xn])
                self.cp('act', st_ap[:, tb, :], bk.ap, [bk], [st])
            yield
            dst = d['vtok'][tok0:tok0 + T, :].rearrange("(tb p) f -> p tb f", p=128)
            self.dma('sp', dst, st_ap, ('stv', ti % 2), [st], self.dres('vtok', tok0, tok0 + T))
            ust = stg['u'][ti % 2]
            bgst = stg['bg'][ti % 2]
            for cc in range(4):
                slot, w = self.wget(l, 'in_cg', cc, M=128)
                bc, _ = self.bank()
                self.mm(bc, [(bc.ap, w[:, kc, :], xn_ap[:, kc, :], kc == 0, kc == 7) for kc in range(8)], [slot, xn])
                slot, w = self.wget(l, 'in_h', cc, M=128)
                bh, _ = self.bank()
                self.mm(bh, [(bh.ap, w[:, kc, :], xn_ap[:, kc, :], kc == 0, kc == 7) for kc in range(8)], [slot, xn])
                slot, w = self.wget(l, 'in_bg', cc, M=128)
                bb, _ = self.bank()
                self.mm(bb, [(bb.ap, w[:, kc, :], xn_ap[:, kc, :], kc == 0, kc == 7) for kc in range(8)], [slot, xn])
                cg = cgs[cc % 2]
                self.cp('act', cg.ap, bc.ap, [bc], [cg])
                self.tt('dve', v3(ust, 4)[:, cc, :], bh.ap, cg.ap, ALU.mult, [bh, cg], [ust])
                self.cp('act', v3(bgst, 4)[:, cc, :], bb.ap, [bb], [bgst])
                yield
            for sn, dn, st in (('u', 'uT', ust), ('bg', 'bgT', bgst)):
                dst = d[dn].rearrange("(c p) s -> p c s", p=128)[:, :, tok0:tok0 + T]
                self.dma('sp', dst, v3(st, 4), ('st' + sn, ti % 2), [st], self.dres(dn, tok0, tok0 + T))
            yield
        self.pipeline((tile(ti) for ti in range(self.ntiles)), 2)
        self.barrier()

    def even_e2(self, l):
        self.reset()
        d = self.dram
        SM = self.S_P
        DIL = (1, 4, 16)
        q1 = self.carve(SM * 2, BF16)
        qd = {1: q1, 4: self.carve(SM * 2, BF16), 16: self.carve(SM * 2, BF16)}
        kp = {dd: self.carve((SM + 128 * dd) * 2, BF16) for dd in DIL}
        vb = {dd: self.carve((SM // 128 + dd) * 128 * 2, BF16) for dd in DIL}
        acc = self.carve(2 * SM * 4, F32)
        pTs = [self.carve(512 * 2, BF16) for _ in range(6)]
        yast = self.carve(SM * 2, BF16)
        ones = self.cbv('ones')
        pcnt = 0
        for si, (s0, S) in enumerate(self.seqs):
            for hp in range(4):
                frow = slice(hp * 128, (hp + 1) * 128)
                for dd in DIL:
                    L_ = S // dd
                    nb_ = L_ // 128 + 1
                    kv_ = kp[dd].ap[:, 0:S + 128 * dd].rearrange("p (r m) -> p r m", r=dd)
                    self.memset('pool', kv_[:, :, 0:64], 0.0, [kp[dd]])
                    self.memset('pool', kv_[:, :, 64 + L_:128 + L_], 0.0, [kp[dd]])
                    v4_ = vb[dd].ap[:, 0:dd * nb_ * 128].rearrange("p (r b f) -> p r b f", r=dd, b=nb_)
                    self.memset('pool', v4_[0:64, :, 0, :], 0.0, [vb[dd]])
                    self.memset('pool', v4_[64:128, :, nb_ - 1, :], 0.0, [vb[dd]])
                self.dma('sp', q1.ap[:, 0:S], d['qT'][frow, s0:s0 + S], ('e2q', 0), self.dres('qT', s0, s0 + S), [q1])
                self.dma('sp', kp[1].ap[:, 64:64 + S], d['kT'][frow, s0:s0 + S], ('e2k', 0), self.dres('kT', s0, s0 + S), [kp[1]])
                qv, kv, vv = {}, {}, {}
                qv[1] = q1.ap[:, 0:S].rearrange("p (r m) -> p r m", r=1)
                kv[1] = kp[1].ap[:, 0:S + 128].rearrange("p (r m) -> p r m", r=1)
                for dd in (4, 16):
                    L = S // dd
                    qv[dd] = qd[dd].ap[:, 0:S].rearrange("p (r m) -> p r m", r=dd)
                    kv[dd] = kp[dd].ap[:, 0:S + 128 * dd].rearrange("p (r m) -> p r m", r=dd)
                    self.cp('act', qv[dd], q1.ap[:, 0:S].rearrange("p (m r) -> p r m", r=dd), [q1], [qd[dd]])
                    self.cp('pool', kv[dd][:, :, 64:64 + L], kp[1].ap[:, 64:64 + S].rearrange("p (m r) -> p r m", r=dd),
                            [kp[1]], [kp[dd]])
                for dd in DIL:
                    L = S // dd
                    nb = L // 128 + 1
                    v4 = vb[dd].ap[:, 0:dd * nb * 128].rearrange("p (r b f) -> p r b f", r=dd, b=nb)
                    vv[dd] = v4
                    vres = self.dres('vtok', s0, s0 + S)
                    R0 = s0 + 64 * dd
                    R1 = s0 + dd * (L - 64)
                    if nb > 2:
                        srcm = d['vtok'][R0:R0 + dd * 128 * (nb - 2), frow].rearrange("(b j r) f -> r j b f", j=128, r=dd)
                    srcf = d['vtok'][s0:s0 + 64 * dd, frow].rearrange("(j r) f -> r j f", r=dd)
                    srcl = d['vtok'][R1:R1 + 64 * dd, frow].rearrange("(j r) f -> r j f", r=dd)
                    for r in range(dd):
                        if nb > 2:
                            self.dma('sp', v4[:, r, 1:nb - 1, :], srcm[r], ('e2v', dd), vres, [vb[dd]])
                        self.dma('sp', v4[64:128, r, 0, :], srcf[r], ('e2v', dd), vres, [vb[dd]])
                        self.dma('sp', v4[0:64, r, nb - 1, :], srcl[r], ('e2v', dd), vres, [vb[dd]])
                for dd in DIL:
                    L = S // dd
                    Q = min(256, L)
                    ntile = L // Q
                    accv = acc.ap[:, 0:2 * SM].rearrange("p (a s) -> p a s", a=2)[:, :, 0:S].rearrange(
                        "p a (m r) -> p a m r", r=dd)
                    def tbody(r, n, dd=dd, L=L, Q=Q, ntile=ntile, accv=accv):
                        nonlocal pcnt
                        if True:
                            if Q == 128:
                                mname = 'am_s'
                            else:
                                first, last = (n == 0), (n == ntile - 1)
                                mname = 'am_fl' if (first and last) else ('am_f' if first else ('am_l' if last else 'am_n'))
                            mask = self.cbv(mname)
                            W = 512 if Q == 256 else 256
                            qa = qv[dd][:, r, n * Q:(n + 1) * Q]
                            b0 = 2 * n if Q == 256 else 0
                            kb = lambda b: kv[dd][:, r, 128 * b:128 * b + 128]
                            pts = []
                            for hd in range(2):
                                pr = slice(64 * hd, 64 * hd + 64)
                                bk, _ = self.bank()
                                if Q == 256:
                                    mms = [(bk.ap[:, 0:128], kb(b0)[pr], qa[pr, 0:128], True, True),
                                           (bk.ap[:, 128:384], kb(b0 + 1)[pr], qa[pr, 0:256], True, True),
                                           (bk.ap[:, 384:512], kb(b0 + 2)[pr], qa[pr, 128:256], True, True)]
                                else:
                                    mms = [(bk.ap[:, 0:128], kb(0)[pr], qa[pr, 0:128], True, True),
                                           (bk.ap[:, 128:256], kb(1)[pr], qa[pr, 0:128], True, True)]
                                self.mm(bk, mms, [qd[dd], kp[dd]])
                                pT = pTs[pcnt % 6]
                                pcnt += 1
                                pts.append((pT, bk))
                            yield
                            for (pT, bk) in pts:
                                self.act(pT.ap[:, 0:W], bk.ap[:, 0:W], AF.Exp, [bk], [pT], scale=0.125)
                                self.tt('dve', pT.ap[:, 0:W], pT.ap[:, 0:W], mask[:, 0:W], ALU.mult, [pT, self.cb], [pT])
                            pts = [p for (p, _) in pts]
                            yield
                            bo, _ = self.bank()
                            mms = []
                            for hd in range(2):
                                pr = slice(64 * hd, 64 * hd + 64)
                                fs = slice(64 * hd, 64 * hd + 64)
                                pT = pts[hd].ap
                                vblk = lambda b: vv[dd][:, r, b, fs]
                                if Q == 256:
                                    mms += [(bo.ap[pr, 0:256], vblk(b0 + 1), pT[:, 128:384], True, False),
                                            (bo.ap[pr, 0:128], vblk(b0), pT[:, 0:128], False, False),
                                            (bo.ap[pr, 128:256], vblk(b0 + 2), pT[:, 384:512], False, False),
                                            (bo.ap[pr, 256:512], ones[:, 0:64], pT[:, 128:384], False, False),
                                            (bo.ap[pr, 256:384], ones[:, 0:64], pT[:, 0:128], False, False),
                                            (bo.ap[pr, 384:512], ones[:, 0:64], pT[:, 384:512], False, True)]
                                else:
                                    mms += [(bo.ap[pr, 0:128], vblk(0), pT[:, 0:128], True, False),
                                            (bo.ap[pr, 0:128], vblk(1), pT[:, 128:256], False, False),
                                            (bo.ap[pr, 256:384], ones[:, 0:64], pT[:, 0:128], False, False),
                                            (bo.ap[pr, 256:384], ones[:, 0:64], pT[:, 128:256], False, True)]
                            self.mm(bo, mms, [vb[dd], self.cb] + pts)
                            yield
                            av = accv[:, :, n * Q:(n + 1) * Q, r]
                            bov = bo.ap.rearrange("p (a q) -> p a q", a=2)[:, :, 0:Q]
                            if dd == 1:
                                self.cp('dve', av, bov, [bo], [acc])
                            else:
                                self.tt('dve', av, bov, av, ALU.add, [bo, acc], [acc])
                            yield
                    self.pipeline((tbody(r, n) for r in range(dd) for n in range(ntile)), 3)
                a2 = acc.ap[:, 0:2 * SM].rearrange("p (a s) -> p a s", a=2)
                self.recip(a2[:, 1, 0:S], a2[:, 1, 0:S], [acc], [acc])
                self.tt('dve', yast.ap[:, 0:S], a2[:, 0, 0:S], a2[:, 1, 0:S], ALU.mult, [acc], [yast])
                self.dma('sp', d['yaT'][frow, s0:s0 + S], yast.ap[:, 0:S], ('e2y', 0), [yast], self.dres('yaT', s0, s0 + S))
        self.barrier()

    def even_e3(self, l):
        self.reset()
        T = 512
        d = self.dram
        v3 = lambda b, n: b.ap.rearrange("p (c t) -> p c t", c=n)
        xts = [self.carve(8 * T * 4, F32) for _ in range(2)]
        yas = [self.carve(4 * T * 2, BF16) for _ in range(2)]
        uhs = [self.carve(4 * (T + 2) * 2, BF16) for _ in range(2)]
        bgs = [self.carve(4 * T * 2, BF16) for _ in range(2)]
        ybs = [self.carve(4 * T * 2, BF16) for _ in range(2)]
        tas = [self.carve(T * 4, F32) for _ in range(2)]

        def tile(ti):
            tok0 = ti * T
            s0, S = self.seqs[0] if tok0 < self.S_P else self.seqs[1]
            xt, ya, uh, bg, yb = xts[ti % 2], yas[ti % 2], uhs[ti % 2], bgs[ti % 2], ybs[ti % 2]
            xt_ap = v3(xt, 8)
            fm = lambda n: d[n].rearrange("(c p) s -> p c s", p=128)
            self.dma('sp', xt_ap, fm('xs')[:, :, tok0:tok0 + T], ('x', ti % 2), self.dres('xs', tok0, tok0 + T), [xt])
            self.dma('sp', v3(ya, 4), fm('yaT')[:, :, tok0:tok0 + T], ('e3a', ti % 2), self.dres('yaT', tok0, tok0 + T), [ya])
            self.dma('sp', v3(bg, 4), fm('bgT')[:, :, tok0:tok0 + T], ('e3b', ti % 2), self.dres('bgT', tok0, tok0 + T), [bg])
            uh_ap = uh.ap.rearrange("p (c t) -> p c t", c=4)
            lo = tok0 - 1
            hi = tok0 + T + 1
            c0 = 0
            if lo < s0:
                self.memset('pool', uh_ap[:, :, 0:1], 0.0, [uh])
                lo += 1
                c0 = 1
            c1 = T + 2
            if hi > s0 + S:
                self.memset('pool', uh_ap[:, :, T + 1:T + 2], 0.0, [uh])
                hi -= 1
                c1 = T + 1
            self.dma('sp', uh_ap[:, :, c0:c1], fm('uT')[:, :, lo:hi], ('e3u', ti % 2), self.dres('uT', lo, hi), [uh])
            for cc in range(4):
                ta = tas[cc % 2]
                cw = lambda j: self.col('conv%d' % l, j * 4 + cc)
                self.ts('dve', ta.ap, uh_ap[:, cc, 0:T], cw(0), None, ALU.mult, None, [uh, self.cols], [ta])
                self.stt('dve', ta.ap, uh_ap[:, cc, 1:T + 1], cw(1), ta.ap, ALU.mult, ALU.add, [uh, ta, self.cols], [ta])
                self.stt('dve', ta.ap, uh_ap[:, cc, 2:T + 2], cw(2), ta.ap, ALU.mult, ALU.add, [uh, ta, self.cols], [ta])
                self.tt('dve', v3(yb, 4)[:, cc, :], ta.ap, v3(bg, 4)[:, cc, :], ALU.mult, [ta, bg], [yb])
                yield

            def ev_add(oc, bk):
                self.tt('dve', xt_ap[:, oc, :], bk.ap, xt_ap[:, oc, :], ALU.add, [bk, xt], [xt])
            yield from self.linear_gen(l, 'w_out', 8, 8, lambda kc: (v3(ya, 4)[:, kc, :] if kc < 4 else v3(yb, 4)[:, kc - 4, :]),
                                       [ya, yb], T, ev_add)
            self.dma('sp', fm('xs')[:, :, tok0:tok0 + T], xt_ap, ('xst', ti % 2), [xt], self.dres('xs', tok0, tok0 + T))
            yield
        self.pipeline((tile(ti) for ti in range(self.ntiles)), 2)
        self.barrier()

    def phase_odd(self, l):
        self.odd_r1(l)
        self.odd_r2(l)
        self.odd_r3(l)

    def wload(self, l, name, idx, buf, key):
        o, n, F = self.woff[l][0][name]
        src = self.dram['wbf%d' % l][:, o + idx * F:o + (idx + 1) * F]
        self.dma('sp', buf.ap[:, 0:F], src, key, [self.wres[l]], [buf])

    def odd_r1(self, l):
        self.reset()
        T = 512
        d = self.dram
        fm = lambda n: d[n].rearrange("(c p) s -> p c s", p=128)
        xh = self.carve(8 * (T + 2) * 4, F32)
        xh_ap = xh.ap.rearrange("p (c t) -> p c t", c=8)
        xx = self.carve(8 * T * 4, F32)
        xx_ap = xx.ap.rearrange("p (c t) -> p c t", c=8)
        sq = self.carve(8 * (T + 2) * 2, BF16)
        sq_ap = sq.ap.rearrange("p (c t) -> p c t", c=8)
        rstd = self.carve((T + 2) * 4, F32)
        mixes = [self.carve(8 * T * 2, BF16) for _ in range(4)]
        lw = {}
        for nm, F in (('w1_0', 512), ('w1_1', 512), ('a1_0', 512), ('a1_1', 512), ('g1', 1024),
                      ('w2_0', 1024), ('w2_1', 1024), ('a2_0', 1024), ('a2_1', 1024), ('g2', 1024)):
            lw[nm] = self.carve(F * 2, BF16)
        items = []
        for nm in lw:
            o_, n_, F_ = self.woff[l][0][nm]
            items.append((lw[nm].ap[:, 0:F_], d['wbf%d' % l][:, o_:o_ + F_], [self.wres[l]], [lw[nm]]))
        self.dma_batch('sp', 'r1w', items)
        hid = {nm: self.carve(T * 2, BF16) for nm in ('hw0', 'hw1', 'ha0', 'ha1', 'hg')}
        f32t = {nm: [self.carve(T * 4, F32) for _ in range(2)] for nm in
                ('rf', 'kl', 'vf', 'as0', 'as1', 'kkr', 'kd0', 'kd1', 'tmp', 'rn')}
        bft = {nm: [self.carve(T * 2, BF16) for _ in range(2)] for nm in
               ('r', 'v', 'g', 'sig0', 'sig1', 'kk', 'b0', 'b1', 'kd0', 'kd1', 'bon', 'sqk', 'rkb')}
        blk = self.cbv('blk')
        mi = 0
        for ti in range(self.ntiles):
            tok0 = ti * T
            s0, S = self.seqs[0] if tok0 < self.S_P else self.seqs[1]
            lo, hi, c0, c1 = tok0 - 1, tok0 + T + 1, 0, T + 2
            if lo < s0:
                self.memset('pool', xh_ap[:, :, 0:1], 0.0, [xh])
                lo, c0 = lo + 1, 1
            if hi > s0 + S:
                self.memset('pool', xh_ap[:, :, T + 1:T + 2], 0.0, [xh])
                hi, c1 = hi - 1, T + 1
            self.dma('sp', xh_ap[:, :, c0:c1], fm('xs')[:, :, lo:hi], ('x', 0), self.dres('xs', lo, hi), [xh])
            self.rmsnorm(xh, xh_ap[:, :, 0:T], T, 'nmix%d' % l, sq, sq_ap[:, :, 0:T], rstd, rstd.ap[:, 0:T],
                         [(('dve',), xh, xh_ap[:, :, 0:T])])
            self.rmsnorm(xh, xh_ap[:, :, T:T + 2], 2, 'nmix%d' % l, sq, sq_ap[:, :, T:T + 2], rstd, rstd.ap[:, T:T + 2],
                         [(('dve',), xh, xh_ap[:, :, T:T + 2])])
            self.tt('dve', xx_ap, xh_ap[:, :, 0:T], xh_ap[:, :, 2:T + 2], ALU.add, [xh], [xx])
            self.stt('dve', xx_ap, xx_ap, 0.5, xh_ap[:, :, 1:T + 1], ALU.mult, ALU.subtract, [xx, xh], [xx])

            def mk_mix(i):
                nonlocal mi
                mb = mixes[mi % 4]
                mi += 1
                m_ap = mb.ap.rearrange("p (c t) -> p c t", c=8)
                for c in range(8):
                    self.stt('dve', m_ap[:, c, :], xx_ap[:, c, :], self.col('mu%d' % l, i * 8 + c), xh_ap[:, c, 1:T + 1],
                             ALU.mult, ALU.add, [xx, xh, self.cols], [mb])
                return mb, m_ap

            def lora1(mb, m_ap, wn, hn, func, M):
                w = lw[wn].ap[:, 0:8 * M].rearrange("p (k m) -> p k m", m=M)
                bk, _ = self.bank()
                self.mm(bk, [(bk.ap[0:M, :], w[:, kc, :], m_ap[:, kc, :], kc == 0, kc == 7) for kc in range(8)], [lw[wn], mb])
                if func is None:
                    self.cp('act', hid[hn].ap[0:M, :], bk.ap[0:M, :], [bk], [hid[hn]])
                else:
                    self.act(hid[hn].ap[0:M, :], bk.ap[0:M, :], func, [bk], [hid[hn]])
            mb, m_ap = mk_mix(1)
            lora1(mb, m_ap, 'w1_0', 'hw0', AF.Tanh, 64)
            lora1(mb, m_ap, 'w1_1', 'hw1', AF.Tanh, 64)
            mb, m_ap = mk_mix(4)
            lora1(mb, m_ap, 'a1_0', 'ha0', None, 64)
            lora1(mb, m_ap, 'a1_1', 'ha1', None, 64)
            mb, m_ap = mk_mix(5)
            lora1(mb, m_ap, 'g1', 'hg', AF.Sigmoid, 128)
            mr, mr_ap = mk_mix(0)
            mk_, mk_ap = mk_mix(2)
            mv, mv_ap = mk_mix(3)
            def ocbody(oc):
                par = oc % 2
                F = lambda nm: f32t[nm][par % len(f32t[nm])]
                B = lambda nm: bft[nm][par % len(bft[nm])]
                ocs = slice(oc * 128, (oc + 1) * 128)

                def proj(wname, m_, m_ap_, dst):
                    slot, w = self.wget(l, wname, oc, M=128)
                    bk, _ = self.bank()
                    self.mm(bk, [(bk.ap, w[:, kc, :], m_ap_[:, kc, :], kc == 0, kc == 7) for kc in range(8)], [slot, m_])
                    self.cp('act', dst.ap, bk.ap, [bk], [dst])
                proj('wr', mr, mr_ap, F('rf'))
                proj('wk', mk_, mk_ap, F('kl'))
                proj('wv', mv, mv_ap, F('vf'))
                self.cp('pool', B('r').ap, F('rf').ap, [F('rf')], [B('r')])
                self.cp('pool', B('v').ap, F('vf').ap, [F('vf')], [B('v')])
                yield
                bk, _ = self.bank()
                self.mm(bk, [(bk.ap, lw['g2'].ap[:, ocs], hid['hg'].ap, True, True)], [lw['g2'], hid['hg']])
                self.cp('act', B('g').ap, bk.ap, [bk], [B('g')])
                for dd in range(2):
                    bk, _ = self.bank()
                    self.mm(bk, [(bk.ap, lw['w2_%d' % dd].ap[0:64, ocs], hid['hw%d' % dd].ap[0:64, :], True, True)],
                            [lw['w2_%d' % dd], hid['hw%d' % dd]])
                    self.act(B('sig%d' % dd).ap, bk.ap, AF.Sigmoid, [bk, self.cols], [B('sig%d' % dd)],
                             bias=self.col('w0_%d' % l, dd * 8 + oc), scale=1.0)
                    bk, _ = self.bank()
                    self.mm(bk, [(bk.ap, lw['a2_%d' % dd].ap[0:64, ocs], hid['ha%d' % dd].ap[0:64, :], True, True)],
                            [lw['a2_%d' % dd], hid['ha%d' % dd]])
                    self.act(F('as%d' % dd).ap, bk.ap, AF.Sigmoid, [bk, self.cols], [F('as%d' % dd)],
                             bias=self.col('a0_%d' % l, dd * 8 + oc), scale=1.0)
                yield
                self.ts('dve', F('kkr').ap, F('kl').ap, self.col('kk%d' % l, oc), None, ALU.mult, None, [F('kl'), self.cols], [F('kkr')])
                self.act(B('sqk').ap, F('kkr').ap, AF.Square, [F('kkr')], [B('sqk')])
                yield
                bk, _ = self.bank()
                self.mm(bk, [(bk.ap, blk, B('sqk').ap, True, True)], [B('sqk'), self.cb])
                self.act(F('rn').ap, bk.ap, AF.Sqrt, [bk], [F('rn')], bias=1e-24, scale=1.0)
                yield
                self.recip(F('rn').ap, F('rn').ap, [F('rn')], [F('rn')])
                self.tt('dve', B('kk').ap, F('kkr').ap, F('rn').ap, ALU.mult, [F('kkr'), F('rn')], [B('kk')])
                for dd in range(2):
                    self.tt('dve', B('b%d' % dd).ap, B('kk').ap, F('as%d' % dd).ap, ALU.mult, [B('kk'), F('as%d' % dd)], [B('b%d' % dd)])
                    self.ts('dve', F('tmp').ap, F('as%d' % dd).ap, self.col('ka%d' % l, oc), self.col('omka%d' % l, oc),
                            ALU.mult, ALU.add, [F('as%d' % dd), self.cols], [F('tmp')])
                    self.tt('dve', F('kd%d' % dd).ap, F('tmp').ap, F('kl').ap, ALU.mult, [F('tmp'), F('kl')], [F('kd%d' % dd)])
                    self.cp('pool', B('kd%d' % dd).ap, F('kd%d' % dd).ap, [F('kd%d' % dd)], [B('kd%d' % dd)])
                self.tt('dve', F('tmp').ap, F('kd0').ap, F('kd1').ap, ALU.add, [F('kd0'), F('kd1')], [F('tmp')])
                self.stt('dve', B('rkb').ap, F('rf').ap, self.col('rk%d' % l, oc), F('tmp').ap, ALU.mult, ALU.mult,
                         [F('rf'), F('tmp'), self.cols], [B('rkb')])
                yield
                bk, _ = self.bank()
                self.mm(bk, [(bk.ap, blk, B('rkb').ap, True, True)], [B('rkb'), self.cb])
                yield
                self.tt('dve', B('bon').ap, bk.ap, F('vf').ap, ALU.mult, [bk, F('vf')], [B('bon')])
                if not NO_R1_STORES:
                  self.dma_batch('sp', ('r1s', par), [
                    (d[dn][ocs, tok0:tok0 + T], B(nm).ap, [B(nm)], self.dres(dn, tok0, tok0 + T))
                    for nm, dn in (('r', 'rT'), ('v', 'vT'), ('g', 'gT'), ('kk', 'kkT'), ('bon', 'bonT'), ('sig0', 'sigT0'),
                                   ('sig1', 'sigT1'), ('b0', 'bT0'), ('b1', 'bT1'), ('kd0', 'kdT0'), ('kd1', 'kdT1'))])
                yield
            gens_ = [ocbody(oc) for oc in range(8)]

            def adv(g, k):
                for _ in range(k):
                    if next(g, 'END') == 'END':
                        return
            adv(gens_[0], 2)
            for oc in range(8):
                if oc + 1 < 8:
                    adv(gens_[oc + 1], 2)
                adv(gens_[oc], 100)
        self.barrier()

    def odd_r2(self, l):
        self.reset()
        d = self.dram
        fm = lambda n: d[n].rearrange("(c p) s -> p c s", p=128)
        NSET = 3
        ident = self.cbv('ident')

        def mkset(i):
            b = {}
            for nm in ('rT', 'kkT', 'bT', 'kdT', 'sigT', 'vT'):
                b[nm] = self.carve(512 * 2, BF16)
            b['sigtok'] = self.carve(512 * 2, BF16)
            b['GG'] = self.carve(4 * 2 * 128 * 4, F32)
            b['Ginv'] = self.carve(512 * 4, F32)
            b['Ghat'] = self.carve(512 * 4, F32)
            b['AR'] = self.carve(1024 * 2, BF16)
            b['BK'] = self.carve(1024 * 2, BF16)
            b['BKh'] = self.carve(1024 * 2, BF16)
            b['Z0'] = self.carve(1024 * 2, BF16)
            b['Z1'] = self.carve(1024 * 2, BF16)
            for nm in ('bhtok', 'khtok', 'vtok'):
                b[nm] = self.carve(512 * 2, BF16)
            b['XT'] = self.carve(8 * 512 * 2, BF16)
            b['X0'] = self.carve(8 * 128 * 2, BF16)
            b['XX0'] = self.carve(8 * 256 * 2, BF16)
            b['XX1'] = self.carve(8 * 256 * 2, BF16)
            b['RH'] = self.carve(512 * 2, BF16)
            b['GT'] = self.carve(256 * 2, BF16)
            b['H'] = self.carve(256 * 4, F32)
            b['Ht'] = self.carve(256 * 4, F32)
            b['Hb'] = self.carve(256 * 2, BF16)
            b['ost'] = self.carve(512 * 2, BF16)
            b['id'] = i
            return b
        sets = [mkset(i) for i in range(NSET)]

        def unit(b, si, dr, cg, n):
            s0, S = self.seqs[si]
            t0 = s0 + 128 * n
            sid = b['id']
            cs = slice(4 * cg, 4 * cg + 4)
            sfx = str(dr)
            v3 = lambda nm, c: b[nm].ap.rearrange("p (c t) -> p c t", c=c)
            self.dma_batch('sp', ('r2l', sid), [
                (v3(nm, 4), fm(dn)[:, cs, t0:t0 + 128], self.dres(dn, t0, t0 + 128), [b[nm]])
                for nm, dn in (('rT', 'rT'), ('kkT', 'kkT'), ('bT', 'bT' + sfx), ('kdT', 'kdT' + sfx), ('sigT', 'sigT' + sfx), ('vT', 'vT'))])
            yield
            bk, bh = self.bank()
            bkb = bh.bitcast(BF16)
            self.tr(bk, [(bkb[:, c * 128:(c + 1) * 128], v3('sigT', 4)[:, c, :]) for c in range(4)], [b['sigT'], self.cb])
            self.cp('act', b['sigtok'].ap, bkb[:, 0:512], [bk], [b['sigtok']])
            yield
            tri = self.cbv('tri_f' if dr == 0 else 'tri_b')
            GG = b['GG'].ap.rearrange("p (c a t) -> p c a t", c=4, a=2)
            Ginv = v3('Ginv', 4)
            Ghat = v3('Ghat', 4)
            for j in range(2):
                bk, _ = self.bank()
                self.mm(bk, [(bk.ap[:, cc * 256:(cc + 1) * 256], b['sigtok'].ap[:, (2 * j + cc) * 128:(2 * j + cc + 1) * 128], tri, True, True)
                             for cc in range(2)], [b['sigtok'], self.cb])
                bv = bk.ap.rearrange("p (c a t) -> p c a t", c=2, a=2)
                self.act(b['GG'].ap[:, j * 512:(j + 1) * 512], bk.ap, AF.Exp, [bk], [b['GG']], scale=-KAPPA)
                self.act(Ginv[:, 2 * j:2 * j + 2, :], bv[:, :, 0, :], AF.Exp, [bk], [b['Ginv']], scale=KAPPA)
            tl = 127 if dr == 0 else 0
            gC = GG[:, :, 0, tl:tl + 1]
            self.tt('dve', Ghat, Ginv, gC.broadcast_to([128, 4, 128]), ALU.mult, [b['Ginv'], b['GG']], [b['Ghat']])
            yield
            AR = b['AR'].ap.rearrange("p (c a t) -> p c a t", c=4, a=2)
            BK = b['BK'].ap.rearrange("p (c a t) -> p c a t", c=4, a=2)
            BKh = b['BKh'].ap.rearrange("p (c a t) -> p c a t", c=4, a=2)
            self.stt('dve', AR[:, :, 0, :], v3('kkT', 4), -1.0, GG[:, :, 1, :], ALU.mult, ALU.mult, [b['kkT'], b['GG']], [b['AR']])
            self.tt('pool', AR[:, :, 1, :], v3('rT', 4), GG[:, :, 0, :], ALU.mult, [b['rT'], b['GG']], [b['AR']])
            self.tt('dve', BK[:, :, 0, :], v3('bT', 4), Ginv, ALU.mult, [b['bT'], b['Ginv']], [b['BK']])
            self.tt('pool', BK[:, :, 1, :], v3('kdT', 4), Ginv, ALU.mult, [b['kdT'], b['Ginv']], [b['BK']])
            self.tt('dve', BKh[:, :, 0, :], v3('bT', 4), Ghat, ALU.mult, [b['bT'], b['Ghat']], [b['BKh']])
            self.tt('pool', BKh[:, :, 1, :], v3('kdT', 4), Ghat, ALU.mult, [b['kdT'], b['Ghat']], [b['BKh']])
            yield
            Z = [b['Z0'], b['Z1']]
            Zv = [z.ap.rearrange("p (h x) -> p h x", h=8) for z in Z]
            bk, bh = self.bank()
            bkb = bh.bitcast(BF16)
            self.tr(bk, [(bkb[:, c * 128:(c + 1) * 128], AR[:, c, 0, :]) for c in range(4)], [b['AR'], self.cb])
            self.cp('act', Zv[0][:, :, 0:64], bkb[:, 0:512].rearrange("p (h x) -> p h x", h=8), [bk], [Z[0]])
            for src, srcb, dst, eng in ((lambda c: BKh[:, c, 0, :], b['BKh'], 'bhtok', 'dve'),
                                        (lambda c: BKh[:, c, 1, :], b['BKh'], 'khtok', 'act'),
                                        (lambda c: v3('vT', 4)[:, c, :], b['vT'], 'vtok', 'dve')):
                bk, bh = self.bank()
                bkb = bh.bitcast(BF16)
                self.tr(bk, [(bkb[:, c * 128:(c + 1) * 128], src(c)) for c in range(4)], [srcb, self.cb])
                self.cp(eng, b[dst].ap, bkb[:, 0:512], [bk], [b[dst]])
            yield
            XT = b['XT'].ap.rearrange("p (h x) -> p h x", h=8)
            X0 = b['X0'].ap.rearrange("p (h x) -> p h x", h=8)
            mT = self.cbv('mT_f' if dr == 0 else 'mT_b')
            mL = self.cbv('mL_f' if dr == 0 else 'mL_b')
            for h in range(8):
                c, half = h // 2, h % 2
                pr = slice(64 * half, 64 * half + 64)
                ar2 = AR[pr, c, :, :].rearrange("p a t -> p (a t)")
                bk, _ = self.bank()
                self.mm(bk, [(bk.ap[:, 0:256], BK[pr, c, 0, :], ar2, True, True),
                             (bk.ap[:, 256:512], BK[pr, c, 1, :], ar2, True, True)], [b['BK'], b['AR']])
                self.tt('dve' if h % 2 == 0 else 'dve', XT[:, h, :], bk.ap, mT, ALU.mult, [bk, self.cb], [b['XT']])
            X0s = b['X0'].ap.rearrange("p (c f t) -> p c f t", c=4, f=2)
            for half in range(2):
                bk, _ = self.bank()
                pr = slice(64 * half, 64 * half + 64)
                mms = [(bk.ap[:, c * 128:(c + 1) * 128], AR[pr, c, 0, :], BK[pr, c, 0, :], True, True) for c in range(4)]
                self.mm(bk, mms, [b['AR'], b['BK']])
                self.tt('dve', X0s[:, :, half, :], bk.ap.rearrange("p (c t) -> p c t", c=4),
                        mL.rearrange("p (c t) -> p c t", c=4), ALU.mult, [bk, self.cb], [b['X0']])
            yield
            bk, _ = self.bank()
            self.mm(bk, [(bk.ap[:, h * 64:(h + 1) * 64], XT[:, h, 256:384], b['vtok'].ap[:, h * 64:(h + 1) * 64], True, True)
                         for h in range(8)], [b['XT'], b['vtok']])
            self.cp('act', Zv[0][:, :, 64:128], bk.ap.rearrange("p (h x) -> p h x", h=8), [bk], [Z[0]])
            yield
            XX = [b['XX0'], b['XX1']]
            XXv = [x.ap.rearrange("p (h a t) -> p h a t", h=8, a=2) for x in XX]
            for lev in range(7):
                zc, zn = lev % 2, (lev + 1) % 2
                if lev == 0:
                    Xk = lambda h: X0[:, h, :]
                    XTk = lambda h: XT[:, h, 0:128]
                    xres = [b['X0'], b['XT']]
                else:
                    pv = XXv[(lev - 1) % 2]
                    Xk = lambda h, pv=pv: pv[:, h, 0, :]
                    XTk = lambda h, pv=pv: pv[:, h, 1, :]
                    xres = [XX[(lev - 1) % 2]]
                for j in range(2):
                    bk, _ = self.bank()
                    mms = [(bk.ap, ident, Z[zc].ap[:, j * 512:(j + 1) * 512], True, False)]
                    for hl in range(4):
                        h = 4 * j + hl
                        mms.append((bk.ap[:, hl * 128:(hl + 1) * 128], XTk(h), Zv[zc][:, h, :], False, hl == 3))
                    self.mm(bk, mms, xres + [Z[zc], self.cb])
                    self.cp('act' if j == 0 else 'dve', Z[zn].ap[:, j * 512:(j + 1) * 512], bk.ap, [bk], [Z[zn]])
                if lev < 6:
                    for j in range(4):
                        bk, _ = self.bank()
                        mms = []
                        for hl in range(2):
                            h = 2 * j + hl
                            mms.append((bk.ap[:, hl * 256:hl * 256 + 128], XTk(h), Xk(h), True, True))
                            mms.append((bk.ap[:, hl * 256 + 128:hl * 256 + 256], Xk(h), XTk(h), True, True))
                        self.mm(bk, mms, xres)
                        self.cp('act' if j % 2 == 0 else 'dve', XX[lev % 2].ap[:, j * 512:(j + 1) * 512], bk.ap, [bk], [XX[lev % 2]])
                yield
            Zf, Zfv = Z[1], Zv[1]
            bk, _ = self.bank()
            mms = []
            for h in range(8):
                c, half = h // 2, h % 2
                pr = slice(64 * half, 64 * half + 64)
                mms.append((bk.ap[pr, c * 128:(c + 1) * 128], Zfv[:, h, 0:64], XT[:, h, 128:256], True, True))
            self.mm(bk, mms, [Zf, b['XT']])
            self.tt('dve', v3('RH', 4), bk.ap.rearrange("p (c t) -> p c t", c=4), AR[:, :, 1, :], ALU.add, [bk, b['AR']], [b['RH']])
            bk, _ = self.bank()
            mms = []
            for h in range(8):
                c, half = h // 2, h % 2
                pr = slice(64 * half, 64 * half + 64)
                mms.append((bk.ap[pr, c * 64:(c + 1) * 64], Zfv[:, h, 0:64], b['bhtok'].ap[:, h * 64:(h + 1) * 64], True, True))
            self.mm(bk, mms, [Zf, b['bhtok']])
            self.cp('act', b['GT'].ap, bk.ap[:, 0:256], [bk], [b['GT']])
            yield
            Hb = v3('Hb', 4)
            RH = v3('RH', 4)
            GT = v3('GT', 4)
            bk, _ = self.bank()
            mms = []
            for h in range(8):
                c, half = h // 2, h % 2
                pr = slice(64 * half, 64 * half + 64)
                o = bk.ap[pr, c * 128:(c + 1) * 128]
                mms.append((o, Hb[pr, c, :], RH[pr, c, :], True, False))
                mms.append((o, Zfv[:, h, 64:128], XT[:, h, 128:256], False, False))
                mms.append((o, b['vtok'].ap[:, h * 64:(h + 1) * 64], XT[:, h, 384:512], False, True))
            self.mm(bk, mms, [b['Hb'], b['RH'], Zf, b['XT'], b['vtok']])
            self.cp('act', b['ost'].ap, bk.ap, [bk], [b['ost']])
            self.dma('sp', fm('oT' + sfx)[:, cs, t0:t0 + 128], v3('ost', 4), ('r2o', sid), [b['ost']], self.dres('oT' + sfx, t0, t0 + 128))
            bk, _ = self.bank()
            mms = []
            for h in range(8):
                c, half = h // 2, h % 2
                pr = slice(64 * half, 64 * half + 64)
                o = bk.ap[pr, c * 64:(c + 1) * 64]
                mms.append((o, GT[pr, c, :], Hb[pr, c, :], True, False))
                mms.append((o, b['bhtok'].ap[:, h * 64:(h + 1) * 64], Zfv[:, h, 64:128], False, False))
                mms.append((o, b['khtok'].ap[:, h * 64:(h + 1) * 64], b['vtok'].ap[:, h * 64:(h + 1) * 64], False, True))
            self.mm(bk, mms, [b['GT'], b['Hb'], b['bhtok'], b['khtok'], Zf, b['vtok']])
            Hv = v3('H', 4)
            Htv = v3('Ht', 4)
            self.tt('dve', Htv, Hv, gC.broadcast_to([128, 4, 64]), ALU.mult, [b['H'], b['GG']], [b['Ht']])
            self.tt('dve', Hv, Htv, bk.ap[:, 0:256].rearrange("p (c t) -> p c t", c=4), ALU.add, [bk, b['Ht']], [b['H']])
            self.cp('act', b['Hb'].ap, b['H'].ap, [b['H']], [b['Hb']])
            yield

        def stream(b, si, dr, cg):
            self.memset('pool', b['H'].ap, 0.0, [b['H']])
            self.memset('pool', b['Hb'].ap, 0.0, [b['Hb']])
            nch = self.seqs[si][1] // 128
            order = range(nch) if dr == 0 else range(nch - 1, -1, -1)
            for n in order:
                yield from unit(b, si, dr, cg, n)
        todo = [(si, dr, cg) for si in range(2) for dr in range(2) for cg in range(2)]
        active = []
        free = list(range(NSET))
        while todo or active:
            while todo and free:
                i = free.pop(0)
                active.append((i, stream(sets[i], *todo.pop(0))))
            nxt = []
            for (i, gen) in active:
                try:
                    next(gen)
                    nxt.append((i, gen))
                except StopIteration:
                    free.append(i)
            active = nxt
        self.barrier()

    def odd_r3(self, l):
        self.reset()
        T = 512
        d = self.dram
        fm = lambda n: d[n].rearrange("(c p) s -> p c s", p=128)
        v3 = lambda b, n: b.ap.rearrange("p (c t) -> p c t", c=n)
        xts = [self.carve(8 * T * 4, F32) for _ in range(2)]
        lds = {nm: [self.carve(8 * T * 2, BF16) for _ in range(2)] for nm in ('oT0', 'oT1', 'bonT', 'gT')}
        yfs = [self.carve(8 * T * 2, BF16) for _ in range(2)]
        tmp = {nm: [self.carve(T * 4, F32) for _ in range(4)] for nm in ('y', 'yc', 'sd', 'yn')}
        sqs = [self.carve(T * 2, BF16) for _ in range(4)]
        blk = self.cbv('blk')

        def tile(ti):
            tok0 = ti * T
            xt = xts[ti % 2]
            xt_ap = v3(xt, 8)
            self.dma('sp', xt_ap, fm('xs')[:, :, tok0:tok0 + T], ('x', ti % 2), self.dres('xs', tok0, tok0 + T), [xt])
            L = {nm: lds[nm][ti % 2] for nm in lds}
            self.dma_batch('sp', ('r3', ti % 2), [
                (v3(L[nm], 8), fm(nm)[:, :, tok0:tok0 + T], self.dres(nm, tok0, tok0 + T), [L[nm]]) for nm in lds])
            yf = yfs[ti % 2]
            for c in range(8):
                p = (c % 2) + 2 * (ti % 2)
                y, yc, sd, yn, sqb = tmp['y'][p], tmp['yc'][p], tmp['sd'][p], tmp['yn'][p], sqs[p]
                o0 = v3(L['oT0'], 8)[:, c, :]
                o1 = v3(L['oT1'], 8)[:, c, :]
                self.tt('dve', y.ap, o0, o1, ALU.add, [L['oT0'], L['oT1']], [y])
                bk, _ = self.bank()
                self.mm(bk, [(bk.ap, blk, o0, True, False), (bk.ap, blk, o1, False, True)], [L['oT0'], L['oT1'], self.cb])
                self.stt('dve', yc.ap, bk.ap, -1.0 / 64.0, y.ap, ALU.mult, ALU.add, [bk, y], [yc])
                self.act(sqb.ap, yc.ap, AF.Square, [yc], [sqb], scale=0.125)
                bk, _ = self.bank()
                self.mm(bk, [(bk.ap, blk, sqb.ap, True, True)], [sqb, self.cb])
                self.act(sd.ap, bk.ap, AF.Sqrt, [bk], [sd], bias=GN_EPS, scale=1.0)
                self.recip(sd.ap, sd.ap, [sd], [sd])
                self.tt('dve', yn.ap, yc.ap, sd.ap, ALU.mult, [yc, sd], [yn])
                self.act(yn.ap, yn.ap, AF.Identity, [yn, self.cols], [yn], bias=self.col('lnb%d' % l, c), scale=self.col('lnw%d' % l, c))
                self.tt('pool', yn.ap, yn.ap, v3(L['bonT'], 8)[:, c, :], ALU.add, [yn, L['bonT']], [yn])
                self.tt('pool', v3(yf, 8)[:, c, :], yn.ap, v3(L['gT'], 8)[:, c, :], ALU.mult, [yn, L['gT']], [yf])
                yield

            def ev_add(oc, bk):
                self.tt('dve', xt_ap[:, oc, :], bk.ap, xt_ap[:, oc, :], ALU.add, [bk, xt], [xt])
            yield from self.linear_gen(l, 'wo_r', 8, 8, lambda kc: v3(yf, 8)[:, kc, :], [yf], T, ev_add)
            self.dma('sp', fm('xs')[:, :, tok0:tok0 + T], xt_ap, ('xst', ti % 2), [xt], self.dres('xs', tok0, tok0 + T))
            yield
        self.pipeline((tile(ti) for ti in range(self.ntiles)), 2)
        self.barrier()

    def phase_load(self):
        self.reset()
        T = 512
        d = self.dram
        xts = [self.carve(8 * T * 4, F32) for _ in range(2)]
        for ti in range(self.ntiles):
            tok0 = ti * T
            xt = xts[ti % 2]
            xt_ap = xt.ap.rearrange("p (c t) -> p c t", c=8)
            src = d['xT'].rearrange("(c p) s -> p c s", p=128)[:, :, tok0:tok0 + T]
            self.dma('sp', xt_ap, src, ('x', ti % 2), [], [xt])
            dst = d['xs'].rearrange("(c p) s -> p c s", p=128)[:, :, tok0:tok0 + T]
            self.dma('sp', dst, xt_ap, ('xst', ti % 2), [xt], self.dres('xs', tok0, tok0 + T))
        self.barrier()

    def run(self):
        self.prologue()
        for ph in self.plan:
            fn = getattr(self, 'phase_' + ph[0], None) or getattr(self, ph[0])
            fn(*ph[1:])


def default_plan(depth):
    plan = [('load',)]
    for l in range(depth):
        if l % 2 == 0:
            plan += [('even', l)]
        else:
            plan += [('odd', l)]
        plan += [('caffn', l, l == depth - 1)]
    return plan


def build_program(S_P, S_S, depth, plan):
    nc = bass.Bass("TRN2", target_bir_lowering=False)
    Stot = S_P + S_S
    dram = {}

    def dt_(name, shape, dt, kind):
        dram[name] = nc.dram_tensor(name, shape, dt, kind=kind).ap()
    _, ncols = col_layout(depth)
    dt_('xT', [D, Stot], F32, 'ExternalInput')
    dt_('memT', [D, 2 * NMEM], F32, 'ExternalInput')
    dt_('cols', [128, ncols], F32, 'ExternalInput')
    dt_('cbf', [128, NCB], F32, 'ExternalInput')
    dt_('rope', [128, 2, S_P], F32, 'ExternalInput')
    for l in range(depth):
        tot = weight_offsets(l)[1]
        dt_('wpk%d' % l, [128, tot], F32, 'ExternalInput')
        dt_('wbf%d' % l, [128, tot], BF16, 'Internal')
    dt_('yT', [D, Stot], F32, 'ExternalOutput')
    dt_('xs', [D, Stot], F32, 'Internal')
    for n in ('qT', 'kT', 'uT', 'bgT', 'yaT'):
        dt_(n, [512, Stot], BF16, 'Internal')
    dt_('vtok', [Stot, 512], BF16, 'Internal')
    for n in ('rT', 'kkT', 'vT', 'gT', 'bonT', 'sigT0', 'sigT1', 'bT0', 'bT1', 'kdT0', 'kdT1', 'oT0', 'oT1'):
        dt_(n, [D, Stot], BF16, 'Internal')

    with ExitStack() as es:
        arena = es.enter_context(nc.sbuf_tensor("arena", [128, ARENA_BYTES // 2], BF16))
        banks = [es.enter_context(nc.psum_tensor("pb%d" % i, [128, 512], F32)) for i in range(8)]
        g1 = Gen(nc, S_P, S_S, depth, plan, arena, banks, dram, Sched(), None)
        g1.run()
        wplan = g1.wrec
        sch = Sched()
        g = Gen(nc, S_P, S_S, depth, plan, arena, banks, dram, sch, wplan)
        g.run()
        final_marks = dict(sch.pending)
        keys = set(ENG[:4])
        for e in ENG:
            for (waits, fn, sk, inc) in sch.q[e]:
                for k, v in waits:
                    keys.add(k)
                if sk is not None:
                    keys.add(sk)
        for k in final_marks:
            keys.add(k)
        sems = {}
        for i, k in enumerate(sorted(keys, key=str)):
            sems[k] = es.enter_context(nc.semaphore("s%d" % i))
        handles = {'pe': None, 'act': None, 'dve': None, 'pool': None, 'sp': None}

        def run_q(ename, h):
            for (waits, fn, sk, inc) in sch.q[ename]:
                for k, v in waits:
                    h.wait_ge(sems[k], v)
                if fn is not None:
                    ins = fn(h)
                    ins.then_inc(sems[sk], inc)
        with nc.Block() as block:
            @block.tensor
            def _(t):
                run_q('pe', t)

            @block.scalar
            def _(s):
                run_q('act', s)

            @block.vector
            def _(v):
                run_q('dve', v)

            @block.gpsimd
            def _(gp):
                run_q('pool', gp)

            @block.sync
            def _(sy):
                run_q('sp', sy)
                for k, v in final_marks.items():
                    sy.wait_ge(sems[k], v)
        info = dict(nops=sch.nops, nsems=len(sems), q={e: len(sch.q[e]) for e in ENG})
    return nc, info


def prep_inputs(inputs, S_P, S_S, depth, n_cores=8):
    p = {k: np.asarray(v, np.float32) for k, v in inputs.items()}
    shared = {'cols': pack_cols(depth, p), 'cbf': make_cbf(), 'rope': np.ascontiguousarray(make_rope(S_P))}
    for l in range(depth):
        shared['wpk%d' % l] = pack_weights(l, p)
    in_maps = []
    for i in range(n_cores):
        m = dict(shared)
        m['xT'] = np.ascontiguousarray(np.concatenate([p['x_prompt'][i].T, p['x_sample'][i].T], axis=1))
        m['memT'] = np.ascontiguousarray(np.concatenate([p['mem_prompt'][i].T, p['mem_sample'][i].T], axis=1))
        in_maps.append(m)
    return in_maps


def run_model(inputs, S_P, S_S, depth, plan, n_cores=8, trace=False):
    nc, info = build_program(S_P, S_S, depth, plan)
    in_maps = prep_inputs(inputs, S_P, S_S, depth, n_cores)
    res = run_bass_kernel_spmd(nc, in_maps, core_ids=list(range(n_cores)), trace=trace)
    yp = np.stack([res.results[i]['yT'][:, :S_P].T for i in range(n_cores)], axis=0)
    ys = np.stack([res.results[i]['yT'][:, S_P:].T for i in range(n_cores)], axis=0)
    return (np.ascontiguousarray(yp, dtype=np.float32), np.ascontiguousarray(ys, dtype=np.float32)), info, res


def kernel(**inputs):
    (yp, ys), info, res = run_model(inputs, 4096, 2048, 4, default_plan(4))
    return (yp, ys)
```

```python
import numpy as np
import concourse.bass as bass
import concourse.mybir as mybir
from concourse.bass_utils import run_bass_kernel_spmd
from contextlib import ExitStack

F32 = mybir.dt.float32
BF16 = mybir.dt.bfloat16
AF = mybir.ActivationFunctionType
ALU = mybir.AluOpType

D = 1024
NMEM = 256
DFF = 2816
KAPPA = 0.6065306597126334
RMS_EPS = 1e-6
GN_EPS = 64e-5
ROPE_THETA = 500000.0
ENG = ['pe', 'act', 'dve', 'pool', 'sp']
SAME_ENG_SYNC = False
NO_R1_STORES = False
R1_DEPTH = 2
CA_DEPTH = 2


class Res:
    __slots__ = ('w', 'r', 'const')

    def __init__(self, const=False):
        self.w = None
        self.r = {}
        self.const = const


class Buf:
    __slots__ = ('ap', 'res')

    def __init__(self, ap, const=False):
        self.ap = ap
        self.res = Res(const)


class Sched:
    def __init__(self):
        self.q = {e: [] for e in ENG}
        self.cnt = {e: 0 for e in ENG}
        self.waited = {e: {} for e in ENG}
        self.dcnt = {}
        self.pending = {}
        self.nops = 0

    def _collect(self, eng, reads, writes, is_dma=False):
        deps = {}
        for r in reads:
            d = r.w
            if d is not None and deps.get(d[0], 0) < d[1]:
                deps[d[0]] = d[1]
        for w in writes:
            d = w.w
            if d is not None and deps.get(d[0], 0) < d[1]:
                deps[d[0]] = d[1]
            for k, v in w.r.items():
                if deps.get(k, 0) < v:
                    deps[k] = v
        out = []
        wt = self.waited[eng]
        for k, v in deps.items():
            if k == eng and not is_dma and not SAME_ENG_SYNC:
                continue
            if wt.get(k, 0) >= v:
                continue
            wt[k] = v
            out.append((k, v))
        return out

    def _mark(self, done, reads, writes):
        k, v = done
        for r in reads:
            if not r.const and r.r.get(k, 0) < v:
                r.r[k] = v
        for w in writes:
            w.w = done
            w.r = {}

    def op(self, eng, fn, reads=(), writes=()):
        waits = self._collect(eng, reads, writes)
        self.cnt[eng] += 1
        done = (eng, self.cnt[eng])
        self._mark(done, reads, writes)
        self.q[eng].append((waits, fn, eng, 1))
        self.nops += 1

    def dma(self, qeng, fn, key, reads=(), writes=()):
        waits = self._collect(qeng, reads, writes, is_dma=True)
        k = ('d', key)
        c = self.dcnt.get(k, 0) + 1
        self.dcnt[k] = c
        done = (k, 16 * c)
        self._mark(done, reads, writes)
        self.pending[k] = 16 * c
        self.q[qeng].append((waits, fn, k, 16))
        self.nops += 1

    def wait_all(self, engs, marks):
        for e in engs:
            wt = self.waited[e]
            ws = []
            for k, v in marks.items():
                if wt.get(k, 0) >= v:
                    continue
                wt[k] = v
                ws.append((k, v))
            if ws:
                self.q[e].append((ws, None, None, 0))


def _chunks_lhsT(W, M):
    Kin, Nout = W.shape
    if Kin < 128:
        Wp = np.zeros((128, Nout), np.float32)
        Wp[:Kin] = W
        return [np.ascontiguousarray(Wp[:, o * M:(o + 1) * M]) for o in range(Nout // M)]
    KC = Kin // 128
    W3 = W.reshape(KC, 128, Nout)
    return [np.ascontiguousarray(W3[:, :, o * M:(o + 1) * M].transpose(1, 0, 2)).reshape(128, KC * M)
            for o in range(Nout // M)]


def weight_spec(l):
    sp = []
    if l % 2 == 0:
        sp += [('in_q', 4, 1024), ('in_k', 4, 1024), ('in_v', 2, 2048), ('in_bg', 4, 1024),
               ('in_cg', 4, 1024), ('in_h', 4, 1024), ('w_out', 8, 1024)]
    else:
        sp += [('wr', 8, 1024), ('wk', 8, 1024), ('wv', 8, 1024), ('wo_r', 8, 1024),
               ('w1_0', 1, 512), ('w1_1', 1, 512), ('a1_0', 1, 512), ('a1_1', 1, 512), ('g1', 1, 1024),
               ('w2_0', 1, 1024), ('w2_1', 1, 1024), ('a2_0', 1, 1024), ('a2_1', 1, 1024), ('g2', 1, 1024)]
    sp += [('ca_q', 8, 1024), ('ca_k', 8, 1024), ('ca_v', 4, 2048), ('ca_o', 8, 1024),
           ('ff_gu', 44, 1024), ('ff_d', 8, 2816)]
    return sp


def weight_offsets(l):
    off = {}
    o = 0
    for name, n, F in weight_spec(l):
        off[name] = (o, n, F)
        o += n * F
    return off, o


def pack_weights(l, p):
    ch = {}
    if l % 2 == 0:
        e = l // 2
        win = p['ab_w_in'][e]
        ch['in_q'] = _chunks_lhsT(win[:, 0:512], 128)
        ch['in_k'] = _chunks_lhsT(win[:, 512:1024], 128)
        ch['in_v'] = [h for a in _chunks_lhsT(win[:, 1024:1536], 512) for h in (a[:, 0:2048], a[:, 2048:4096])]
        ch['in_bg'] = _chunks_lhsT(win[:, 1536:2048], 128)
        ch['in_cg'] = _chunks_lhsT(win[:, 2048:2560], 128)
        ch['in_h'] = _chunks_lhsT(win[:, 2560:3072], 128)
        ch['w_out'] = _chunks_lhsT(p['ab_w_out'][e], 128)
    else:
        o = l // 2
        ch['wr'] = _chunks_lhsT(p['rw_wr'][o], 128)
        ch['wk'] = _chunks_lhsT(p['rw_wk'][o], 128)
        ch['wv'] = _chunks_lhsT(p['rw_wv'][o], 128)
        ch['wo_r'] = _chunks_lhsT(p['rw_wo'][o], 128)
        for d in range(2):
            ch['w1_%d' % d] = _chunks_lhsT(p['rw_w1'][o, d], 64)
            ch['a1_%d' % d] = _chunks_lhsT(p['rw_a1'][o, d], 64)
            ch['w2_%d' % d] = _chunks_lhsT(p['rw_w2'][o, d], 1024)
            ch['a2_%d' % d] = _chunks_lhsT(p['rw_a2'][o, d], 1024)
        ch['g1'] = _chunks_lhsT(p['rw_g1'][o], 128)
        ch['g2'] = _chunks_lhsT(p['rw_g2'][o], 1024)
    ch['ca_q'] = _chunks_lhsT(p['ca_wq'][l], 128)
    ch['ca_k'] = _chunks_lhsT(p['ca_wkv'][l][:, 0:1024], 128)
    ch['ca_v'] = [h for a in _chunks_lhsT(p['ca_wkv'][l][:, 1024:2048], 512) for h in (a[:, 0:2048], a[:, 2048:4096])]
    ch['ca_o'] = _chunks_lhsT(p['ca_wo'][l], 128)
    ch['ff_gu'] = _chunks_lhsT(p['ffn_wgu'][l], 128)
    ch['ff_d'] = _chunks_lhsT(p['ffn_wdown'][l], 128)
    parts = []
    for name, n, F in weight_spec(l):
        assert len(ch[name]) == n, (name, len(ch[name]), n)
        for a in ch[name]:
            assert a.shape == (128, F), (name, a.shape, F)
            parts.append(a)
    return np.ascontiguousarray(np.concatenate(parts, axis=1))


def col_layout(depth):
    names = []
    for l in range(depth):
        names += [('nmix%d' % l, 8), ('ncross%d' % l, 8), ('nmem%d' % l, 8), ('nffn%d' % l, 8)]
        if l % 2 == 0:
            names += [('conv%d' % l, 12)]
        else:
            names += [('mu%d' % l, 48), ('w0_%d' % l, 16), ('a0_%d' % l, 16), ('kk%d' % l, 8), ('ka%d' % l, 8),
                      ('rk%d' % l, 8), ('lnw%d' % l, 8), ('lnb%d' % l, 8), ('omka%d' % l, 8)]
    names += [('nfinal', 8)]
    off = {}
    o = 0
    for n, k in names:
        off[n] = o
        o += k
    return off, o


def _colv(v):
    return np.asarray(v, np.float32).reshape(-1, 128).T


def pack_cols(depth, p):
    off, n = col_layout(depth)
    cols = np.zeros((128, n), np.float32)

    def put(name, arr):
        cols[:, off[name]:off[name] + arr.shape[1]] = arr
    for l in range(depth):
        put('nmix%d' % l, _colv(p['norm_mix'][l]))
        put('ncross%d' % l, _colv(p['norm_cross'][l]))
        put('nmem%d' % l, _colv(p['norm_mem'][l]))
        put('nffn%d' % l, _colv(p['norm_ffn'][l]))
        if l % 2 == 0:
            cw = p['ab_conv'][l // 2]
            put('conv%d' % l, np.concatenate([_colv(cw[j]) for j in range(3)], axis=1))
        else:
            o = l // 2
            put('mu%d' % l, np.concatenate([_colv(p['rw_mu'][o, i]) for i in range(6)], axis=1))
            put('w0_%d' % l, np.concatenate([_colv(p['rw_w0'][o, d]) for d in range(2)], axis=1))
            put('a0_%d' % l, np.concatenate([_colv(p['rw_a0'][o, d]) for d in range(2)], axis=1))
            put('kk%d' % l, _colv(p['rw_kk'][o]))
            put('ka%d' % l, _colv(p['rw_ka'][o]))
            put('rk%d' % l, _colv(p['rw_rk'][o].reshape(-1)))
            put('lnw%d' % l, _colv(p['rw_lnx_w'][o]))
            put('lnb%d' % l, _colv(p['rw_lnx_b'][o]))
    put('nfinal', _colv(p['norm_final']))
    return cols


CB = {}
_o = 0
for _n, _k in [('ident', 128), ('ones', 128), ('blk', 128), ('pswap', 128), ('tri_f', 256), ('tri_b', 256),
               ('mT_f', 512), ('mL_f', 512), ('mT_b', 512), ('mL_b', 512),
               ('am_n', 512), ('am_f', 512), ('am_l', 512), ('am_fl', 512), ('am_s', 512)]:
    CB[_n] = (_o, _k)
    _o += _k
NCB = _o


def make_cbf():
    c = np.zeros((128, NCB), np.float32)

    def put(n, a):
        o, k = CB[n]
        assert a.shape == (128, k), (n, a.shape)
        c[:, o:o + k] = a
    I = np.eye(128, dtype=np.float32)
    put('ident', I)
    put('ones', np.ones((128, 128), np.float32))
    blk = np.zeros((128, 128), np.float32)
    blk[0:64, 0:64] = 1
    blk[64:128, 64:128] = 1
    put('blk', blk)
    ps = np.zeros((128, 128), np.float32)
    for m in range(128):
        j = m % 64
        if j < 8:
            ps[m + 8, m] = 1
        elif j < 16:
            ps[m - 8, m] = 1
    put('pswap', ps)
    s = np.arange(128)[:, None]
    t = np.arange(128)[None, :]
    put('tri_f', np.concatenate([(s <= t), (s < t)], axis=1).astype(np.float32))
    put('tri_b', np.concatenate([(s >= t), (s > t)], axis=1).astype(np.float32))
    stf, inf_ = (s < t).astype(np.float32), (s <= t).astype(np.float32)
    stb, inb = (s > t).astype(np.float32), (s >= t).astype(np.float32)
    put('mT_f', np.concatenate([stf, inf_, stf, inf_], axis=1))
    put('mT_b', np.concatenate([stb, inb, stb, inb], axis=1))
    Lf = (t < s).astype(np.float32)
    Lb = (t > s).astype(np.float32)
    put('mL_f', np.concatenate([Lf] * 4, axis=1))
    put('mL_b', np.concatenate([Lb] * 4, axis=1))
    jj = np.arange(128)[:, None]
    i1 = np.arange(128)[None, :]
    i2 = np.arange(256)[None, :]
    M0 = ((jj - i1 >= 0) & (jj - i1 <= 128)).astype(np.float32)
    M1 = ((i2 - jj >= 0) & (i2 - jj <= 128)).astype(np.float32)
    M2 = M1[:, 0:128]
    lo = (jj >= 64).astype(np.float32)
    hi = (jj < 64).astype(np.float32)
    put('am_n', np.concatenate([M0, M1, M2], axis=1))
    put('am_f', np.concatenate([M0 * lo, M1, M2], axis=1))
    put('am_l', np.concatenate([M0, M1, M2 * hi], axis=1))
    put('am_fl', np.concatenate([M0 * lo, M1, M2 * hi], axis=1))
    put('am_s', np.concatenate([M0 * lo, M1[:, 0:128] * hi, np.zeros((128, 256), np.float32)], axis=1))
    return c


def make_rope(S):
    half = 8
    inv = np.power(np.float32(ROPE_THETA), (np.float32(-2.0) * np.arange(half, dtype=np.float32) / np.float32(16.0))).astype(np.float32)
    pos = np.arange(S, dtype=np.float32)
    ang = (pos[None, :] * inv[:, None]).astype(np.float32).astype(np.float64)
    cos = np.ones((128, S), np.float32)
    sin = np.zeros((128, S), np.float32)
    for pbase in (0, 64):
        cos[pbase:pbase + 8] = np.cos(ang)
        cos[pbase + 8:pbase + 16] = np.cos(ang)
        sin[pbase:pbase + 8] = -np.sin(ang)
        sin[pbase + 8:pbase + 16] = np.sin(ang)
    return np.stack([cos, sin], axis=1)


ARENA_BYTES = 204 * 1024
NWS = 5
WS_BYTES = 5632
WLOOK = 3


def _bp(ap):
    b = ap.base_partition
    return b() if callable(b) else b


class Gen:
    def __init__(self, nc, S_P, S_S, depth, plan, arena, banks, dram, sch, wplan):
        self.nc = nc
        self.S_P, self.S_S = S_P, S_S
        self.Stot = S_P + S_S
        self.seqs = [(0, S_P), (S_P, S_S)]
        self.depth = depth
        self.plan = plan
        self.arena = arena
        self.dram = dram
        self.sch = sch
        self.T = 512
        self.ntiles = self.Stot // 512
        self.coff, self.ncols = col_layout(depth)
        self.woff = [weight_offsets(l) for l in range(depth)]
        self.dres_ = {}
        self.wres = [Res() for _ in range(depth)]
        self.pbank = 0
        self.banks = [Buf(b[:, :]) for b in banks]
        self.bank_h = banks
        self.bankres = set(id(b.res) for b in self.banks)
        self.top = 0
        self.cb = self.carve(NCB * 2, BF16, const=True)
        self.cols = self.carve(self.ncols * 4, F32, const=True)
        self.wslots = [self.carve(WS_BYTES, BF16) for _ in range(NWS)]
        self.scr = {e: self.carve(16, F32) for e in ('act', 'dve', 'pool')}
        self.base = self.top
        self.wplan = wplan
        self.wrec = []
        self.wpos = 0
        self.wissued = 0

    def carve(self, nbytes, dt, const=False, shape=None):
        off = self.top
        self.top += (nbytes + 31) // 32 * 32
        assert self.top <= ARENA_BYTES, ('arena overflow', self.top)
        ap = self.arena[:, off // 2:(off + nbytes) // 2]
        if dt == F32:
            ap = ap.bitcast(F32)
        b = Buf(ap, const)
        return b

    def reset(self):
        self.top = self.base

    def bank(self):
        b = self.banks[self.pbank]
        h = self.bank_h[self.pbank]
        self.pbank = (self.pbank + 1) % len(self.banks)
        return b, h

    def cbv(self, name):
        o, k = CB[name]
        return self.cb.ap[:, o:o + k]

    def col(self, name, i):
        o = self.coff[name] + i
        return self.cols.ap[:, o:o + 1]

    def dres(self, name, t0, t1):
        out = []
        for t in range(t0 // 512, (t1 + 511) // 512):
            k = (name, t)
            if k not in self.dres_:
                self.dres_[k] = Res()
            out.append(self.dres_[k])
        return out

    def op(self, eng, fn, reads=(), writes=()):
        rr = [b.res if isinstance(b, Buf) else b for b in reads]
        ww = [b.res if isinstance(b, Buf) else b for b in writes]
        for r in rr:
            if id(r) in self.bankres and r not in ww:
                ww.append(r)
        self.sch.op(eng, fn, rr, ww)

    def dma(self, q, out_ap, in_ap, key, reads=(), writes=()):
        def fn(e, o=out_ap, i=in_ap):
            return e.dma_start(out=o, in_=i)
        self.sch.dma(q, fn, key, [b.res if isinstance(b, Buf) else b for b in reads],
                     [b.res if isinstance(b, Buf) else b for b in writes])

    def dma_batch(self, q, key, items):
        rs, ws = [], []
        for (o, i, rd, wr) in items:
            self.dma(q, o, i, key, rd, wr)
            rs += [b.res if isinstance(b, Buf) else b for b in rd]
            ws += [b.res if isinstance(b, Buf) else b for b in wr]
        k = ('d', key)
        final = self.sch.dcnt[k] * 16
        for r in ws:
            r.w = (k, final)
        for r in rs:
            if not r.const:
                r.r[k] = final

    def mm(self, bank, mms, reads):
        def fn(t, mms=mms):
            ins = None
            for (o, l, r, st, sp) in mms:
                kw = {}
                rp, cp = _bp(l), _bp(o)
                if rp != 0 or cp != 0:
                    kw['tile_position'] = (rp, cp)
                ins = t.matmul(o, l, r, start=st, stop=sp, skip_group_check=True, **kw)
            return ins
        self.op('pe', fn, reads, [bank])

    def tr(self, bank, trs, reads):
        idn = self.cbv('ident')

        def fn(t, trs=trs):
            ins = None
            for (o, i) in trs:
                ins = t.transpose(o, i, idn)
            return ins
        self.op('pe', fn, reads, [bank])

    def act(self, out, in_, func, reads, writes, bias=None, scale=None, eng='act'):
        kw = {}
        if bias is not None:
            kw['bias'] = bias
        if scale is not None:
            kw['scale'] = scale

        def fn(e, out=out, in_=in_):
            return e.activation(out=out, in_=in_, func=func, **kw)
        self.op('act', fn, reads, writes)

    def tt(self, eng, out, in0, in1, op, reads, writes):
        def fn(e, out=out, in0=in0, in1=in1):
            return e.tensor_tensor(out=out, in0=in0, in1=in1, op=op)
        self.op(eng, fn, reads, writes)

    def ts(self, eng, out, in0, s1, s2, op0, op1, reads, writes):
        def fn(e, out=out, in0=in0):
            if s2 is None:
                return e.tensor_scalar(out, in0, s1, None, op0)
            return e.tensor_scalar(out, in0, s1, s2, op0, op1)
        self.op(eng, fn, reads, writes)

    def stt(self, eng, out, in0, scalar, in1, op0, op1, reads, writes):
        eng = 'dve'
        def fn(e, out=out, in0=in0, in1=in1):
            return e.scalar_tensor_tensor(out=out, in0=in0, scalar=scalar, in1=in1, op0=op0, op1=op1)
        self.op(eng, fn, reads, writes)

    def cp(self, eng, out, in_, reads, writes):
        if eng == 'act':
            def fn(e, out=out, in_=in_):
                return e.copy(out, in_)
        else:
            def fn(e, out=out, in_=in_):
                return e.tensor_copy(out=out, in_=in_)
        self.op(eng, fn, reads, writes)

    def recip(self, out, in_, reads, writes):
        def fn(e, out=out, in_=in_):
            return e.reciprocal(out, in_)
        self.op('dve', fn, reads, writes)

    def memset(self, eng, ap, val, writes):
        def fn(e, ap=ap):
            return e.memset(ap, val)
        self.op(eng, fn, [], writes)

    def pipeline(self, gens, depth):
        gens = iter(gens)
        active = []
        while True:
            while len(active) < depth:
                g = next(gens, None)
                if g is None:
                    break
                active.append(g)
            if not active:
                break
            nxt = []
            for g in active:
                try:
                    next(g)
                    nxt.append(g)
                except StopIteration:
                    pass
            active = nxt

    def barrier(self):
        s = self.sch
        b0 = self.banks[0]

        idn = self.cbv('ident')
        bk = self.bank_h[0]

        def pe_fn(t):
            return t.matmul(bk[0:1, 0:1], idn[0:1, 0:1], idn[0:1, 0:1], start=True, stop=True, skip_group_check=True)
        self.op('pe', pe_fn, [self.cb], [b0])
        self.op('act', lambda e: e.memzero(self.scr['act'].ap), [], [self.scr['act']])
        self.op('dve', lambda e: e.memset(self.scr['dve'].ap, 0.0), [], [self.scr['dve']])
        self.op('pool', lambda e: e.memset(self.scr['pool'].ap, 0.0), [], [self.scr['pool']])
        marks = dict(s.pending)
        for e in ('pe', 'act', 'dve', 'pool'):
            marks[e] = s.cnt[e]
        s.wait_all(ENG, marks)
        s.pending = {}

    def _wissue(self, i, req):
        l, name, idx = req
        o, n, F = self.woff[l][0][name]
        slot = self.wslots[i % NWS]
        src = self.dram['wbf%d' % l][:, o + idx * F:o + (idx + 1) * F]
        self.dma('sp', slot.ap[:, 0:F], src, ('w', i % NWS), [self.wres[l]], [slot])
        self.wissued = i + 1

    def wget(self, l, name, idx, M=None):
        req = (l, name, idx)
        i = self.wpos
        self.wpos += 1
        if self.wplan is None:
            self.wrec.append(req)
            self._wissue(i, req)
        else:
            assert self.wplan[i] == req, (i, self.wplan[i], req)
            tgt = min(i + WLOOK, len(self.wplan) - 1)
            while self.wissued <= tgt:
                self._wissue(self.wissued, self.wplan[self.wissued])
        slot = self.wslots[i % NWS]
        F = self.woff[l][0][name][2]
        ap = slot.ap[:, 0:F]
        if M is not None:
            ap = ap.rearrange("p (k m) -> p k m", m=M)
        return slot, ap

    def prologue(self):
        d = self.dram
        self.dma('pool', self.cb.ap, d['cbf'], 'c0', [], [self.cb])
        self.dma('sp', self.cols.ap, d['cols'], 'c1', [], [self.cols])
        PIECE = 8192
        for l in range(self.depth):
            tot = self.woff[l][1]
            for c0 in range(0, tot, PIECE):
                c1 = min(tot, c0 + PIECE)
                self.dma('pool', d['wbf%d' % l][:, c0:c1], d['wpk%d' % l][:, c0:c1], 'wc%d' % l, [], [self.wres[l]])
        for l in range(self.depth):
            if l % 2 == 1:
                o1 = self.coff['omka%d' % l]
                o0 = self.coff['ka%d' % l]
                self.ts('dve', self.cols.ap[:, o1:o1 + 8], self.cols.ap[:, o0:o0 + 8], -1.0, 1.0, ALU.mult, ALU.add,
                        [self.cols], [self.cols])

    def rmsnorm(self, xt, xt_ap, ncolsT, gname, sq, sq_ap, rstd, rstd_ap, outs):
        n = ncolsT
        self.act(sq_ap, xt_ap, AF.Square, [xt], [sq], scale=1.0 / 32.0)
        bk, _ = self.bank()
        ones = self.cbv('ones')
        self.mm(bk, [(bk.ap[:, 0:n], ones, sq_ap[:, c, :], c == 0, c == 7) for c in range(8)], [sq, self.cb])
        self.act(rstd_ap, bk.ap[:, 0:n], AF.Sqrt, [bk], [rstd], bias=RMS_EPS, scale=1.0)
        self.recip(rstd_ap, rstd_ap, [rstd], [rstd])
        for (engs, ob, oap) in outs:
            for c in range(8):
                self.stt(engs[c % len(engs)], oap[:, c, :], xt_ap[:, c, :], self.col(gname, c), rstd_ap,
                         ALU.mult, ALU.mult, [xt, rstd, self.cols], [ob])

    def linear(self, l, wname, n_oc, KC, rhs_fn, rhs_reads, N, evac, oc0=0, M=128, Kp=128):
        for oc in range(n_oc):
            slot, w = self.wget(l, wname, oc0 + oc, M=M)
            bk, _ = self.bank()
            self.mm(bk, [(bk.ap[0:M, 0:N], w[0:Kp, kc, :], rhs_fn(kc), kc == 0, kc == KC - 1) for kc in range(KC)],
                    [slot] + rhs_reads)
            evac(oc, bk)

    def linear_gen(self, l, wname, n_oc, KC, rhs_fn, rhs_reads, N, evac, oc0=0, M=128, Kp=128, every=2):
        for oc in range(n_oc):
            slot, w = self.wget(l, wname, oc0 + oc, M=M)
            bk, _ = self.bank()
            self.mm(bk, [(bk.ap[0:M, 0:N], w[0:Kp, kc, :], rhs_fn(kc), kc == 0, kc == KC - 1) for kc in range(KC)],
                    [slot] + rhs_reads)
            evac(oc, bk)
            if oc % every == every - 1:
                yield

    def phase_caffn(self, l, final):
        self.reset()
        T = 512
        d = self.dram
        xts = [self.carve(8 * T * 4, F32) for _ in range(2)]
        sqs_ = [self.carve(8 * T * 2, BF16) for _ in range(2)]
        xns_ = [self.carve(8 * T * 2, BF16) for _ in range(2)]
        qTs_ = [self.carve(8 * T * 2, BF16) for _ in range(2)]
        oTs_ = [self.carve(8 * T * 2, BF16) for _ in range(2)]
        pTs = [self.carve(2 * T * 2, BF16) for _ in range(4)]
        rdens = [self.carve(T * 4, F32) for _ in range(4)]
        rstds_ = [self.carve(T * 4, F32) for _ in range(2)]
        sgs = [self.carve(T * 2, BF16) for _ in range(4)]
        sq, xn, qT, oT, rstd = sqs_[0], xns_[0], qTs_[0], oTs_[0], rstds_[0]
        hb = self.carve(22 * T * 2, BF16)
        kTs = [self.carve(8 * 256 * 2, BF16) for _ in range(2)]
        vtoks = [self.carve(2 * 1024 * 2, BF16) for _ in range(2)]
        v3 = lambda b, n: b.ap.rearrange("p (c t) -> p c t", c=n)

        mns = []
        for si in range(2):
            mt = xts[si]
            mt_ap = mt.ap[:, 0:8 * 256].rearrange("p (c t) -> p c t", c=8)
            src = d['memT'].rearrange("(c p) t -> p c t", p=128)[:, :, si * 256:(si + 1) * 256]
            self.dma('sp', mt_ap, src, ('x', si), [], [mt])
            mn = sq if si == 0 else xn
            mn_ap = mn.ap[:, 0:8 * 256].rearrange("p (c t) -> p c t", c=8)
            sq_ap = qT.ap[:, 0:8 * 256].rearrange("p (c t) -> p c t", c=8)
            self.rmsnorm(mt, mt_ap, 256, 'nmem%d' % l, qT, sq_ap, rstd, rstd.ap[:, 0:256], [(('dve', 'pool'), mn, mn_ap)])
            mns.append((mn, mn_ap))
        for oc in range(8):
            slot, w = self.wget(l, 'ca_k', oc, M=128)
            for si in range(2):
                mn, mn_ap = mns[si]
                bk, _ = self.bank()
                self.mm(bk, [(bk.ap[:, 0:256], w[:, kc, :], mn_ap[:, kc, :], kc == 0, kc == 7) for kc in range(8)], [slot, mn])
                self.cp('act', v3(kTs[si], 8)[:, oc, :], bk.ap[:, 0:256], [bk], [kTs[si]])
        for half in range(2):
            slot0, w0 = self.wget(l, 'ca_v', 2 * half, M=512)
            slot1, w1 = self.wget(l, 'ca_v', 2 * half + 1, M=512)
            wk = lambda kc: (w0[:, kc, :] if kc < 4 else w1[:, kc - 4, :])
            for si in range(2):
                mn, mn_ap = mns[si]
                for blk in range(2):
                    bk, _ = self.bank()
                    self.mm(bk, [(bk.ap[:, 0:512], mn_ap[:, kc, blk * 128:(blk + 1) * 128], wk(kc), kc == 0, kc == 7)
                                 for kc in range(8)], [slot0, slot1, mn])
                    self.cp('dve', v3(vtoks[si], 2)[:, blk, half * 512:(half + 1) * 512], bk.ap[:, 0:512], [bk], [vtoks[si]])

        ones = self.cbv('ones')
        xsrc = 'xs'
        hb_turn = [0]

        def tile(ti):
            tp = ti % 2
            sq, xn, qT, oT, rstd = sqs_[tp], xns_[tp], qTs_[tp], oTs_[tp], rstds_[tp]
            tok0 = ti * T
            si = 0 if tok0 < self.S_P else 1
            xt = xts[ti % 2]
            xt_ap = v3(xt, 8)
            src = d[xsrc].rearrange("(c p) s -> p c s", p=128)[:, :, tok0:tok0 + T]
            self.dma('sp', xt_ap, src, ('x', ti % 2), self.dres(xsrc, tok0, tok0 + T), [xt])
            xn_ap = v3(xn, 8)
            sq_ap = v3(sq, 8)
            self.rmsnorm(xt, xt_ap, T, 'ncross%d' % l, sq, sq_ap, rstd, rstd.ap, [(('dve', 'pool'), xn, xn_ap)])
            yield
            qT_ap = v3(qT, 8)
            oT_ap = v3(oT, 8)
            kT_ap = v3(kTs[si], 8)
            vt_ap = v3(vtoks[si], 2)

            def ev_q(oc, bk):
                self.cp('act', qT_ap[:, oc, :], bk.ap, [bk], [qT])
            yield from self.linear_gen(l, 'ca_q', 8, 8, lambda kc: xn_ap[:, kc, :], [xn], T, ev_q)
            for h in range(4):
                pT = pTs[2 * tp + h % 2]
                pT_ap = v3(pT, 2)
                rden = rdens[2 * tp + h % 2]
                for blk in range(2):
                    bk, _ = self.bank()
                    self.mm(bk, [(bk.ap, kT_ap[:, 2 * h + dc, blk * 128:(blk + 1) * 128], qT_ap[:, 2 * h + dc, :], dc == 0, dc == 1)
                                 for dc in range(2)], [kTs[si], qT])
                    self.act(pT_ap[:, blk, :], bk.ap, AF.Exp, [bk], [pT], scale=1.0 / 16.0)
                bk, _ = self.bank()
                self.mm(bk, [(bk.ap, ones, pT_ap[:, blk, :], blk == 0, blk == 1) for blk in range(2)], [pT, self.cb])
                self.recip(rden.ap, bk.ap, [bk], [rden])
                for dc in range(2):
                    bk, _ = self.bank()
                    c = 2 * h + dc
                    self.mm(bk, [(bk.ap, vt_ap[:, blk, c * 128:(c + 1) * 128], pT_ap[:, blk, :], blk == 0, blk == 1)
                                 for blk in range(2)], [vtoks[si], pT])
                    self.tt('dve', oT_ap[:, c, :], bk.ap, rden.ap, ALU.mult, [bk, rden], [oT])
                yield

            def ev_add(oc, bk):
                self.tt('dve', xt_ap[:, oc, :], bk.ap, xt_ap[:, oc, :], ALU.add, [bk, xt], [xt])
            yield from self.linear_gen(l, 'ca_o', 8, 8, lambda kc: oT_ap[:, kc, :], [oT], T, ev_add)
            self.rmsnorm(xt, xt_ap, T, 'nffn%d' % l, sq, sq_ap, rstd, rstd.ap, [(('dve', 'pool'), xn, xn_ap)])
            yield
            while hb_turn[0] != ti:
                yield
            h_ap = v3(hb, 22)
            for c in range(22):
                slg, wg = self.wget(l, 'ff_gu', c, M=128)
                bg, _ = self.bank()
                self.mm(bg, [(bg.ap, wg[:, kc, :], xn_ap[:, kc, :], kc == 0, kc == 7) for kc in range(8)], [slg, xn])
                slu, wu = self.wget(l, 'ff_gu', 22 + c, M=128)
                bu, _ = self.bank()
                self.mm(bu, [(bu.ap, wu[:, kc, :], xn_ap[:, kc, :], kc == 0, kc == 7) for kc in range(8)], [slu, xn])
                sg = sgs[2 * tp + c % 2]
                self.act(sg.ap, bg.ap, AF.Silu, [bg], [sg])
                self.tt('dve', h_ap[:, c, :], bu.ap, sg.ap, ALU.mult, [bu, sg], [hb])
                if c % 2 == 1:
                    yield
            yield from self.linear_gen(l, 'ff_d', 8, 22, lambda kc: h_ap[:, kc, :], [hb], T, ev_add, every=1)
            hb_turn[0] = ti + 1
            if not final:
                dst = d['xs'].rearrange("(c p) s -> p c s", p=128)[:, :, tok0:tok0 + T]
                self.dma('sp', dst, xt_ap, ('xst', ti % 2), [xt], self.dres('xs', tok0, tok0 + T))
            else:
                self.rmsnorm(xt, xt_ap, T, 'nfinal', sq, sq_ap, rstd, rstd.ap, [(('dve',), xt, xt_ap)])
                dst = d['yT'].rearrange("(c p) s -> p c s", p=128)[:, :, tok0:tok0 + T]
                self.dma('sp', dst, xt_ap, ('xst', ti % 2), [xt], self.dres('yT', tok0, tok0 + T))
            yield
        self.pipeline((tile(ti) for ti in range(self.ntiles)), CA_DEPTH)
        self.barrier()

    def phase_even(self, l):
        self.even_e1(l)
        self.even_e2(l)
        self.even_e3(l)

    def even_e1(self, l):
        self.reset()
        T = 512
        d = self.dram
        v3 = lambda b, n: b.ap.rearrange("p (c t) -> p c t", c=n)
        xts = [self.carve(8 * T * 4, F32) for _ in range(2)]
        sqs_ = [self.carve(8 * T * 2, BF16) for _ in range(2)]
        xns_ = [self.carve(8 * T * 2, BF16) for _ in range(2)]
        rstds_ = [self.carve(T * 4, F32) for _ in range(2)]
        rope = self.carve(2 * self.S_P * 4, F32, const=True)
        rope_ap = rope.ap.rearrange("p (a s) -> p a s", a=2)
        qraws = [self.carve(T * 2, BF16) for _ in range(2)]
        t1s = [self.carve(T * 4, F32) for _ in range(2)]
        t2s = [self.carve(T * 4, F32) for _ in range(2)]
        cgs = [self.carve(T * 4, F32) for _ in range(2)]
        stg = {n: [self.carve(4 * T * 2, BF16) for _ in range(2)] for n in ('q', 'k', 'u', 'bg', 'v')}
        self.dma('sp', rope_ap, d['rope'], 'c2', [], [rope])
        pswap = self.cbv('pswap')
        cnt = 0

        def tile(ti):
            nonlocal cnt
            sq, xn, rstd = sqs_[ti % 2], xns_[ti % 2], rstds_[ti % 2]
            tok0 = ti * T
            s0 = 0 if tok0 < self.S_P else self.S_P
            pos0 = tok0 - s0
            xt = xts[ti % 2]
            xt_ap = v3(xt, 8)
            src = d['xs'].rearrange("(c p) s -> p c s", p=128)[:, :, tok0:tok0 + T]
            self.dma('sp', xt_ap, src, ('x', ti % 2), self.dres('xs', tok0, tok0 + T), [xt])
            xn_ap = v3(xn, 8)
            self.rmsnorm(xt, xt_ap, T, 'nmix%d' % l, sq, v3(sq, 8), rstd, rstd.ap, [(('dve',), xn, xn_ap)])
            yield
            for wname, sn, dn in (('in_q', 'q', 'qT'), ('in_k', 'k', 'kT')):
                st = stg[sn][ti % 2]
                st_ap = v3(st, 4)
                for oc in range(4):
                    slot, w = self.wget(l, wname, oc, M=128)
                    bq, _ = self.bank()
                    self.mm(bq, [(bq.ap, w[:, kc, :], xn_ap[:, kc, :], kc == 0, kc == 7) for kc in range(8)], [slot, xn])
                    qraw = qraws[cnt % 2]
                    t1 = t1s[cnt % 2]
                    t2 = t2s[cnt % 2]
                    cnt += 1
                    self.cp('act', qraw.ap, bq.ap, [bq], [qraw])
                    bp, _ = self.bank()
                    self.mm(bp, [(bp.ap, pswap, qraw.ap, True, True)], [qraw, self.cb])
                    self.tt('dve', t1.ap, bq.ap, rope_ap[:, 0, pos0:pos0 + T], ALU.mult, [bq, rope], [t1])
                    self.tt('dve', t2.ap, bp.ap, rope_ap[:, 1, pos0:pos0 + T], ALU.mult, [bp, rope], [t2])
                    self.tt('pool', st_ap[:, oc, :], t1.ap, t2.ap, ALU.add, [t1, t2], [st])
                    yield
                dst = d[dn].rearrange("(c p) s -> p c s", p=128)[:, :, tok0:tok0 + T]
                self.dma('sp', dst, st_ap, ('st' + sn, ti % 2), [st], self.dres(dn, tok0, tok0 + T))
            slot0, w0 = self.wget(l, 'in_v', 0, M=512)
            slot1, w1 = self.wget(l, 'in_v', 1, M=512)
            wk = lambda kc: (w0[:, kc, :] if kc < 4 else w1[:, kc - 4, :])
            st = stg['v'][ti % 2]
            st_ap = v3(st, 4)
            for tb in range(4):
                bk, _ = self.bank()
                self.mm(bk, [(bk.ap, xn_ap[:, kc, tb * 128:(tb + 1) * 128], wk(kc), kc == 0, kc == 7) for kc in range(8)],
                        [slot0, slot1, xn])
                self.cp('act', st_ap[:, tb, :], bk.ap, [bk], [st])
            yield
            dst = d['vtok'][tok0:tok0 + T, :].rearrange("(tb p) f -> p tb f", p=128)
            self.dma('sp', dst, st_ap, ('stv', ti % 2), [st], self.dres('vtok', tok0, tok0 + T))
            ust = stg['u'][ti % 2]
            bgst = stg['bg'][ti % 2]
            for cc in range(4):
                slot, w = self.wget(l, 'in_cg', cc, M=128)
                bc, _ = self.bank()
                self.mm(bc, [(bc.ap, w[:, kc, :], xn_ap[:, kc, :], kc == 0, kc == 7) for kc in range(8)], [slot, xn])
                slot, w = self.wget(l, 'in_h', cc, M=128)
                bh, _ = self.bank()
                self.mm(bh, [(bh.ap, w[:, kc, :], xn_ap[:, kc, :], kc == 0, kc == 7) for kc in range(8)], [slot, xn])
                slot, w = self.wget(l, 'in_bg', cc, M=128)
                bb, _ = self.bank()
                self.mm(bb, [(bb.ap, w[:, kc, :], xn_ap[:, kc, :], kc == 0, kc == 7) for kc in range(8)], [slot, xn])
                cg = cgs[cc % 2]
                self.cp('act', cg.ap, bc.ap, [bc], [cg])
                self.tt('dve', v3(ust, 4)[:, cc, :], bh.ap, cg.ap, ALU.mult, [bh, cg], [ust])
                self.cp('act', v3(bgst, 4)[:, cc, :], bb.ap, [bb], [bgst])
                yield
            for sn, dn, st in (('u', 'uT', ust), ('bg', 'bgT', bgst)):
                dst = d[dn].rearrange("(c p) s -> p c s", p=128)[:, :, tok0:tok0 + T]
                self.dma('sp', dst, v3(st, 4), ('st' + sn, ti % 2), [st], self.dres(dn, tok0, tok0 + T))
            yield
        self.pipeline((tile(ti) for ti in range(self.ntiles)), 2)
        self.barrier()

    def even_e2(self, l):
        self.reset()
        d = self.dram
        SM = self.S_P
        DIL = (1, 4, 16)
        q1 = self.carve(SM * 2, BF16)
        qd = {1: q1, 4: self.carve(SM * 2, BF16), 16: self.carve(SM * 2, BF16)}
        kp = {dd: self.carve((SM + 128 * dd) * 2, BF16) for dd in DIL}
        vb = {dd: self.carve((SM // 128 + dd) * 128 * 2, BF16) for dd in DIL}
        acc = self.carve(2 * SM * 4, F32)
        pTs = [self.carve(512 * 2, BF16) for _ in range(6)]
        yast = self.carve(SM * 2, BF16)
        ones = self.cbv('ones')
        pcnt = 0
        for si, (s0, S) in enumerate(self.seqs):
            for hp in range(4):
                frow = slice(hp * 128, (hp + 1) * 128)
                for dd in DIL:
                    L_ = S // dd
                    nb_ = L_ // 128 + 1
                    kv_ = kp[dd].ap[:, 0:S + 128 * dd].rearrange("p (r m) -> p r m", r=dd)
                    self.memset('pool', kv_[:, :, 0:64], 0.0, [kp[dd]])
                    self.memset('pool', kv_[:, :, 64 + L_:128 + L_], 0.0, [kp[dd]])
                    v4_ = vb[dd].ap[:, 0:dd * nb_ * 128].rearrange("p (r b f) -> p r b f", r=dd, b=nb_)
                    self.memset('pool', v4_[0:64, :, 0, :], 0.0, [vb[dd]])
                    self.memset('pool', v4_[64:128, :, nb_ - 1, :], 0.0, [vb[dd]])
                self.dma('sp', q1.ap[:, 0:S], d['qT'][frow, s0:s0 + S], ('e2q', 0), self.dres('qT', s0, s0 + S), [q1])
                self.dma('sp', kp[1].ap[:, 64:64 + S], d['kT'][frow, s0:s0 + S], ('e2k', 0), self.dres('kT', s0, s0 + S), [kp[1]])
                qv, kv, vv = {}, {}, {}
                qv[1] = q1.ap[:, 0:S].rearrange("p (r m) -> p r m", r=1)
                kv[1] = kp[1].ap[:, 0:S + 128].rearrange("p (r m) -> p r m", r=1)
                for dd in (4, 16):
                    L = S // dd
                    qv[dd] = qd[dd].ap[:, 0:S].rearrange("p (r m) -> p r m", r=dd)
                    kv[dd] = kp[dd].ap[:, 0:S + 128 * dd].rearrange("p (r m) -> p r m", r=dd)
                    self.cp('act', qv[dd], q1.ap[:, 0:S].rearrange("p (m r) -> p r m", r=dd), [q1], [qd[dd]])
                    self.cp('pool', kv[dd][:, :, 64:64 + L], kp[1].ap[:, 64:64 + S].rearrange("p (m r) -> p r m", r=dd),
                            [kp[1]], [kp[dd]])
                for dd in DIL:
                    L = S // dd
                    nb = L // 128 + 1
                    v4 = vb[dd].ap[:, 0:dd * nb * 128].rearrange("p (r b f) -> p r b f", r=dd, b=nb)
                    vv[dd] = v4
                    vres = self.dres('vtok', s0, s0 + S)
                    R0 = s0 + 64 * dd
                    R1 = s0 + dd * (L - 64)
                    if nb > 2:
                        srcm = d['vtok'][R0:R0 + dd * 128 * (nb - 2), frow].rearrange("(b j r) f -> r j b f", j=128, r=dd)
                    srcf = d['vtok'][s0:s0 + 64 * dd, frow].rearrange("(j r) f -> r j f", r=dd)
                    srcl = d['vtok'][R1:R1 + 64 * dd, frow].rearrange("(j r) f -> r j f", r=dd)
                    for r in range(dd):
                        if nb > 2:
                            self.dma('sp', v4[:, r, 1:nb - 1, :], srcm[r], ('e2v', dd), vres, [vb[dd]])
                        self.dma('sp', v4[64:128, r, 0, :], srcf[r], ('e2v', dd), vres, [vb[dd]])
                        self.dma('sp', v4[0:64, r, nb - 1, :], srcl[r], ('e2v', dd), vres, [vb[dd]])
                for dd in DIL:
                    L = S // dd
                    Q = min(256, L)
                    ntile = L // Q
                    accv = acc.ap[:, 0:2 * SM].rearrange("p (a s) -> p a s", a=2)[:, :, 0:S].rearrange(
                        "p a (m r) -> p a m r", r=dd)
                    def tbody(r, n, dd=dd, L=L, Q=Q, ntile=ntile, accv=accv):
                        nonlocal pcnt
                        if True:
                            if Q == 128:
                                mname = 'am_s'
                            else:
                                first, last = (n == 0), (n == ntile - 1)
                                mname = 'am_fl' if (first and last) else ('am_f' if first else ('am_l' if last else 'am_n'))
                            mask = self.cbv(mname)
                            W = 512 if Q == 256 else 256
                            qa = qv[dd][:, r, n * Q:(n + 1) * Q]
                            b0 = 2 * n if Q == 256 else 0
                            kb = lambda b: kv[dd][:, r, 128 * b:128 * b + 128]
                            pts = []
                            for hd in range(2):
                                pr = slice(64 * hd, 64 * hd + 64)
                                bk, _ = self.bank()
                                if Q == 256:
                                    mms = [(bk.ap[:, 0:128], kb(b0)[pr], qa[pr, 0:128], True, True),
                                           (bk.ap[:, 128:384], kb(b0 + 1)[pr], qa[pr, 0:256], True, True),
                                           (bk.ap[:, 384:512], kb(b0 + 2)[pr], qa[pr, 128:256], True, True)]
                                else:
                                    mms = [(bk.ap[:, 0:128], kb(0)[pr], qa[pr, 0:128], True, True),
                                           (bk.ap[:, 128:256], kb(1)[pr], qa[pr, 0:128], True, True)]
                                self.mm(bk, mms, [qd[dd], kp[dd]])
                                pT = pTs[pcnt % 6]
                                pcnt += 1
                                pts.append((pT, bk))
                            yield
                            for (pT, bk) in pts:
                                self.act(pT.ap[:, 0:W], bk.ap[:, 0:W], AF.Exp, [bk], [pT], scale=0.125)
                                self.tt('dve', pT.ap[:, 0:W], pT.ap[:, 0:W], mask[:, 0:W], ALU.mult, [pT, self.cb], [pT])
                            pts = [p for (p, _) in pts]
                            yield
                            bo, _ = self.bank()
                            mms = []
                            for hd in range(2):
                                pr = slice(64 * hd, 64 * hd + 64)
                                fs = slice(64 * hd, 64 * hd + 64)
                                pT = pts[hd].ap
                                vblk = lambda b: vv[dd][:, r, b, fs]
                                if Q == 256:
                                    mms += [(bo.ap[pr, 0:256], vblk(b0 + 1), pT[:, 128:384], True, False),
                                            (bo.ap[pr, 0:128], vblk(b0), pT[:, 0:128], False, False),
                                            (bo.ap[pr, 128:256], vblk(b0 + 2), pT[:, 384:512], False, False),
                                            (bo.ap[pr, 256:512], ones[:, 0:64], pT[:, 128:384], False, False),
                                            (bo.ap[pr, 256:384], ones[:, 0:64], pT[:, 0:128], False, False),
                                            (bo.ap[pr, 384:512], ones[:, 0:64], pT[:, 384:512], False, True)]
                                else:
                                    mms += [(bo.ap[pr, 0:128], vblk(0), pT[:, 0:128], True, False),
                                            (bo.ap[pr, 0:128], vblk(1), pT[:, 128:256], False, False),
                                            (bo.ap[pr, 256:384], ones[:, 0:64], pT[:, 0:128], False, False),
                                            (bo.ap[pr, 256:384], ones[:, 0:64], pT[:, 128:256], False, True)]
                            self.mm(bo, mms, [vb[dd], self.cb] + pts)
                            yield
                            av = accv[:, :, n * Q:(n + 1) * Q, r]
                            bov = bo.ap.rearrange("p (a q) -> p a q", a=2)[:, :, 0:Q]
                            if dd == 1:
                                self.cp('dve', av, bov, [bo], [acc])
                            else:
                                self.tt('dve', av, bov, av, ALU.add, [bo, acc], [acc])
                            yield
                    self.pipeline((tbody(r, n) for r in range(dd) for n in range(ntile)), 3)
                a2 = acc.ap[:, 0:2 * SM].rearrange("p (a s) -> p a s", a=2)
                self.recip(a2[:, 1, 0:S], a2[:, 1, 0:S], [acc], [acc])
                self.tt('dve', yast.ap[:, 0:S], a2[:, 0, 0:S], a2[:, 1, 0:S], ALU.mult, [acc], [yast])
                self.dma('sp', d['yaT'][frow, s0:s0 + S], yast.ap[:, 0:S], ('e2y', 0), [yast], self.dres('yaT', s0, s0 + S))
        self.barrier()

    def even_e3(self, l):
        self.reset()
        T = 512
        d = self.dram
        v3 = lambda b, n: b.ap.rearrange("p (c t) -> p c t", c=n)
        xts = [self.carve(8 * T * 4, F32) for _ in range(2)]
        yas = [self.carve(4 * T * 2, BF16) for _ in range(2)]
        uhs = [self.carve(4 * (T + 2) * 2, BF16) for _ in range(2)]
        bgs = [self.carve(4 * T * 2, BF16) for _ in range(2)]
        ybs = [self.carve(4 * T * 2, BF16) for _ in range(2)]
        tas = [self.carve(T * 4, F32) for _ in range(2)]

        def tile(ti):
            tok0 = ti * T
            s0, S = self.seqs[0] if tok0 < self.S_P else self.seqs[1]
            xt, ya, uh, bg, yb = xts[ti % 2], yas[ti % 2], uhs[ti % 2], bgs[ti % 2], ybs[ti % 2]
            xt_ap = v3(xt, 8)
            fm = lambda n: d[n].rearrange("(c p) s -> p c s", p=128)
            self.dma('sp', xt_ap, fm('xs')[:, :, tok0:tok0 + T], ('x', ti % 2), self.dres('xs', tok0, tok0 + T), [xt])
            self.dma('sp', v3(ya, 4), fm('yaT')[:, :, tok0:tok0 + T], ('e3a', ti % 2), self.dres('yaT', tok0, tok0 + T), [ya])
            self.dma('sp', v3(bg, 4), fm('bgT')[:, :, tok0:tok0 + T], ('e3b', ti % 2), self.dres('bgT', tok0, tok0 + T), [bg])
            uh_ap = uh.ap.rearrange("p (c t) -> p c t", c=4)
            lo = tok0 - 1
            hi = tok0 + T + 1
            c0 = 0
            if lo < s0:
                self.memset('pool', uh_ap[:, :, 0:1], 0.0, [uh])
                lo += 1
                c0 = 1
            c1 = T + 2
            if hi > s0 + S:
                self.memset('pool', uh_ap[:, :, T + 1:T + 2], 0.0, [uh])
                hi -= 1
                c1 = T + 1
            self.dma('sp', uh_ap[:, :, c0:c1], fm('uT')[:, :, lo:hi], ('e3u', ti % 2), self.dres('uT', lo, hi), [uh])
            for cc in range(4):
                ta = tas[cc % 2]
                cw = lambda j: self.col('conv%d' % l, j * 4 + cc)
                self.ts('dve', ta.ap, uh_ap[:, cc, 0:T], cw(0), None, ALU.mult, None, [uh, self.cols], [ta])
                self.stt('dve', ta.ap, uh_ap[:, cc, 1:T + 1], cw(1), ta.ap, ALU.mult, ALU.add, [uh, ta, self.cols], [ta])
                self.stt('dve', ta.ap, uh_ap[:, cc, 2:T + 2], cw(2), ta.ap, ALU.mult, ALU.add, [uh, ta, self.cols], [ta])
                self.tt('dve', v3(yb, 4)[:, cc, :], ta.ap, v3(bg, 4)[:, cc, :], ALU.mult, [ta, bg], [yb])
                yield

            def ev_add(oc, bk):
                self.tt('dve', xt_ap[:, oc, :], bk.ap, xt_ap[:, oc, :], ALU.add, [bk, xt], [xt])
            yield from self.linear_gen(l, 'w_out', 8, 8, lambda kc: (v3(ya, 4)[:, kc, :] if kc < 4 else v3(yb, 4)[:, kc - 4, :]),
                                       [ya, yb], T, ev_add)
            self.dma('sp', fm('xs')[:, :, tok0:tok0 + T], xt_ap, ('xst', ti % 2), [xt], self.dres('xs', tok0, tok0 + T))
            yield
        self.pipeline((tile(ti) for ti in range(self.ntiles)), 2)
        self.barrier()

    def phase_odd(self, l):
        self.odd_r1(l)
        self.odd_r2(l)
        self.odd_r3(l)

    def wload(self, l, name, idx, buf, key):
        o, n, F = self.woff[l][0][name]
        src = self.dram['wbf%d' % l][:, o + idx * F:o + (idx + 1) * F]
        self.dma('sp', buf.ap[:, 0:F], src, key, [self.wres[l]], [buf])

    def odd_r1(self, l):
        self.reset()
        T = 512
        d = self.dram
        fm = lambda n: d[n].rearrange("(c p) s -> p c s", p=128)
        xh = self.carve(8 * (T + 2) * 4, F32)
        xh_ap = xh.ap.rearrange("p (c t) -> p c t", c=8)
        xx = self.carve(8 * T * 2, BF16)
        xx_ap = xx.ap.rearrange("p (c t) -> p c t", c=8)
        xxf = self.carve(8 * T * 4, F32) if False else None
        sq = self.carve(8 * (T + 2) * 2, BF16)
        sq_ap = sq.ap.rearrange("p (c t) -> p c t", c=8)
        rstd = self.carve((T + 2) * 4, F32)
        mixes = [self.carve(8 * T * 2, BF16) for _ in range(4)]
        lw = {}
        for nm, F in (('w1_0', 512), ('w1_1', 512), ('a1_0', 512), ('a1_1', 512), ('g1', 1024),
                      ('w2_0', 1024), ('w2_1', 1024), ('a2_0', 1024), ('a2_1', 1024), ('g2', 1024)):
            lw[nm] = self.carve(F * 2, BF16)
        items = []
        for nm in lw:
            o_, n_, F_ = self.woff[l][0][nm]
            items.append((lw[nm].ap[:, 0:F_], d['wbf%d' % l][:, o_:o_ + F_], [self.wres[l]], [lw[nm]]))
        self.dma_batch('sp', 'r1w', items)
        hid = {nm: self.carve(T * 2, BF16) for nm in ('hw0', 'hw1', 'ha0', 'ha1', 'hg')}
        f32t = {nm: [self.carve(T * 4, F32) for _ in range(3 if nm == 'vf' else 2)] for nm in
                ('rf', 'kl', 'vf', 'as0', 'as1', 't0', 't1', 'rn', 'tsum', 'p1')}
        bft = {nm: [self.carve(T * 2, BF16) for _ in range(3 if nm in ('sig0', 'sig1', 'g') else 2)] for nm in
               ('r', 'v', 'g', 'sig0', 'sig1', 'kk', 'b0', 'b1', 'kd0', 'kd1', 'bon', 'sqk', 'rkb')}
        blk = self.cbv('blk')
        mi = 0
        for ti in range(self.ntiles):
            tok0 = ti * T
            s0, S = self.seqs[0] if tok0 < self.S_P else self.seqs[1]
            lo, hi, c0, c1 = tok0 - 1, tok0 + T + 1, 0, T + 2
            if lo < s0:
                self.memset('pool', xh_ap[:, :, 0:1], 0.0, [xh])
                lo, c0 = lo + 1, 1
            if hi > s0 + S:
                self.memset('pool', xh_ap[:, :, T + 1:T + 2], 0.0, [xh])
                hi, c1 = hi - 1, T + 1
            self.dma('sp', xh_ap[:, :, c0:c1], fm('xs')[:, :, lo:hi], ('x', 0), self.dres('xs', lo, hi), [xh])
            self.rmsnorm(xh, xh_ap[:, :, 0:T], T, 'nmix%d' % l, sq, sq_ap[:, :, 0:T], rstd, rstd.ap[:, 0:T],
                         [(('dve',), xh, xh_ap[:, :, 0:T])])
            self.rmsnorm(xh, xh_ap[:, :, T:T + 2], 2, 'nmix%d' % l, sq, sq_ap[:, :, T:T + 2], rstd, rstd.ap[:, T:T + 2],
                         [(('dve',), xh, xh_ap[:, :, T:T + 2])])
            self.tt('dve', xx_ap, xh_ap[:, :, 0:T], xh_ap[:, :, 2:T + 2], ALU.add, [xh], [xx])
            self.stt('dve', xx_ap, xx_ap, 0.5, xh_ap[:, :, 1:T + 1], ALU.mult, ALU.subtract, [xx, xh], [xx])

            def mk_mix(i):
                nonlocal mi
                mb = mixes[mi % 4]
                mi += 1
                m_ap = mb.ap.rearrange("p (c t) -> p c t", c=8)
                for c in range(8):
                    self.stt('dve', m_ap[:, c, :], xx_ap[:, c, :], self.col('mu%d' % l, i * 8 + c), xh_ap[:, c, 1:T + 1],
                             ALU.mult, ALU.add, [xx, xh, self.cols], [mb])
                return mb, m_ap

            def lora1(mb, m_ap, wn, hn, func, M):
                w = lw[wn].ap[:, 0:8 * M].rearrange("p (k m) -> p k m", m=M)
                bk, _ = self.bank()
                self.mm(bk, [(bk.ap[0:M, :], w[:, kc, :], m_ap[:, kc, :], kc == 0, kc == 7) for kc in range(8)], [lw[wn], mb])
                if func is None:
                    self.cp('act', hid[hn].ap[0:M, :], bk.ap[0:M, :], [bk], [hid[hn]])
                else:
                    self.act(hid[hn].ap[0:M, :], bk.ap[0:M, :], func, [bk], [hid[hn]])
            mb, m_ap = mk_mix(1)
            lora1(mb, m_ap, 'w1_0', 'hw0', AF.Tanh, 64)
            lora1(mb, m_ap, 'w1_1', 'hw1', AF.Tanh, 64)
            mb, m_ap = mk_mix(4)
            lora1(mb, m_ap, 'a1_0', 'ha0', None, 64)
            lora1(mb, m_ap, 'a1_1', 'ha1', None, 64)
            mb, m_ap = mk_mix(5)
            lora1(mb, m_ap, 'g1', 'hg', AF.Sigmoid, 128)
            mr, mr_ap = mk_mix(0)
            mk_, mk_ap = mk_mix(2)
            mv, mv_ap = mk_mix(3)
            def proj(oc, wname, m_, m_ap_):
                slot, w = self.wget(l, wname, oc, M=128)
                bk, _ = self.bank()
                self.mm(bk, [(bk.ap, w[:, kc, :], m_ap_[:, kc, :], kc == 0, kc == 7) for kc in range(8)], [slot, m_])
                return bk

            def front(oc):
                F = lambda nm: f32t[nm][oc % len(f32t[nm])]
                B = lambda nm: bft[nm][oc % len(bft[nm])]
                ocs = slice(oc * 128, (oc + 1) * 128)
                bk = proj(oc, 'wk', mk_, mk_ap)
                self.cp('act', F('kl').ap, bk.ap, [bk], [F('kl')])
                self.act(B('sqk').ap, bk.ap, AF.Square, [bk, self.cols], [B('sqk')], scale=self.col('kk%d' % l, oc))
                bk = proj(oc, 'wr', mr, mr_ap)
                self.cp('act', F('rf').ap, bk.ap, [bk], [F('rf')])
                bk = proj(oc, 'wv', mv, mv_ap)
                self.cp('act', F('vf').ap, bk.ap, [bk], [F('vf')])
                bk, _ = self.bank()
                self.mm(bk, [(bk.ap, lw['g2'].ap[:, ocs], hid['hg'].ap, True, True)], [lw['g2'], hid['hg']])
                self.cp('act', B('g').ap, bk.ap, [bk], [B('g')])
                for dd in range(2):
                    bk, _ = self.bank()
                    self.mm(bk, [(bk.ap, lw['w2_%d' % dd].ap[0:64, ocs], hid['hw%d' % dd].ap[0:64, :], True, True)],
                            [lw['w2_%d' % dd], hid['hw%d' % dd]])
                    self.act(B('sig%d' % dd).ap, bk.ap, AF.Sigmoid, [bk, self.cols], [B('sig%d' % dd)],
                             bias=self.col('w0_%d' % l, dd * 8 + oc), scale=1.0)
                    bk, _ = self.bank()
                    self.mm(bk, [(bk.ap, lw['a2_%d' % dd].ap[0:64, ocs], hid['ha%d' % dd].ap[0:64, :], True, True)],
                            [lw['a2_%d' % dd], hid['ha%d' % dd]])
                    self.act(F('as%d' % dd).ap, bk.ap, AF.Sigmoid, [bk, self.cols], [F('as%d' % dd)],
                             bias=self.col('a0_%d' % l, dd * 8 + oc), scale=1.0)
                    self.act(F('t%d' % dd).ap, F('as%d' % dd).ap, AF.Identity, [F('as%d' % dd), self.cols], [F('t%d' % dd)],
                             bias=self.col('omka%d' % l, oc), scale=self.col('ka%d' % l, oc))
                bk, _ = self.bank()
                self.mm(bk, [(bk.ap, blk, B('sqk').ap, True, True)], [B('sqk'), self.cb])
                self.act(F('rn').ap, bk.ap, AF.Sqrt, [bk], [F('rn')], bias=1e-24, scale=1.0)

            def chain(oc):
                F = lambda nm: f32t[nm][oc % len(f32t[nm])]
                B = lambda nm: bft[nm][oc % len(bft[nm])]
                self.cp('dve', B('r').ap, F('rf').ap, [F('rf')], [B('r')])
                self.cp('dve', B('v').ap, F('vf').ap, [F('vf')], [B('v')])
                self.recip(F('rn').ap, F('rn').ap, [F('rn')], [F('rn')])
                self.stt('dve', B('kk').ap, F('kl').ap, self.col('kk%d' % l, oc), F('rn').ap, ALU.mult, ALU.mult,
                         [F('kl'), F('rn'), self.cols], [B('kk')])
                for dd in range(2):
                    self.tt('dve', B('b%d' % dd).ap, B('kk').ap, F('as%d' % dd).ap, ALU.mult, [B('kk'), F('as%d' % dd)], [B('b%d' % dd)])
                    self.tt('dve', B('kd%d' % dd).ap, F('t%d' % dd).ap, F('kl').ap, ALU.mult, [F('t%d' % dd), F('kl')], [B('kd%d' % dd)])
                self.tt('dve', F('tsum').ap, F('t0').ap, F('t1').ap, ALU.add, [F('t0'), F('t1')], [F('tsum')])
                self.tt('dve', F('p1').ap, F('rf').ap, F('kl').ap, ALU.mult, [F('rf'), F('kl')], [F('p1')])
                self.stt('dve', B('rkb').ap, F('p1').ap, self.col('rk%d' % l, oc), F('tsum').ap, ALU.mult, ALU.mult,
                         [F('p1'), F('tsum'), self.cols], [B('rkb')])

            def tail(oc, tok0=tok0):
                F = lambda nm: f32t[nm][oc % len(f32t[nm])]
                B = lambda nm: bft[nm][oc % len(bft[nm])]
                ocs = slice(oc * 128, (oc + 1) * 128)
                bk, _ = self.bank()
                self.mm(bk, [(bk.ap, blk, B('rkb').ap, True, True)], [B('rkb'), self.cb])
                self.tt('dve', B('bon').ap, bk.ap, F('vf').ap, ALU.mult, [bk, F('vf')], [B('bon')])
                self.dma_batch('sp', ('r1s', oc % 2), [
                    (d[dn][ocs, tok0:tok0 + T], B(nm).ap, [B(nm)], self.dres(dn, tok0, tok0 + T))
                    for nm, dn in (('r', 'rT'), ('v', 'vT'), ('g', 'gT'), ('kk', 'kkT'), ('bon', 'bonT'), ('sig0', 'sigT0'),
                                   ('sig1', 'sigT1'), ('b0', 'bT0'), ('b1', 'bT1'), ('kd0', 'kdT0'), ('kd1', 'kdT1'))])
            front(0)
            for oc in range(8):
                if oc + 1 < 8:
                    front(oc + 1)
                chain(oc)
                if oc >= 1:
                    tail(oc - 1)
            tail(7)
        self.barrier()

    def odd_r2(self, l):
        self.reset()
        d = self.dram
        fm = lambda n: d[n].rearrange("(c p) s -> p c s", p=128)
        NSET = 3
        ident = self.cbv('ident')

        def mkset(i):
            b = {}
            for nm in ('rT', 'kkT', 'bT', 'kdT', 'sigT', 'vT'):
                b[nm] = self.carve(512 * 2, BF16)
            b['sigtok'] = self.carve(512 * 2, BF16)
            b['GG'] = self.carve(4 * 2 * 128 * 4, F32)
            b['Ginv'] = self.carve(512 * 4, F32)
            b['Ghat'] = self.carve(512 * 4, F32)
            b['AR'] = self.carve(1024 * 2, BF16)
            b['BK'] = self.carve(1024 * 2, BF16)
            b['BKh'] = self.carve(1024 * 2, BF16)
            b['Z0'] = self.carve(1024 * 2, BF16)
            b['Z1'] = self.carve(1024 * 2, BF16)
            for nm in ('bhtok', 'khtok', 'vtok'):
                b[nm] = self.carve(512 * 2, BF16)
            b['XT'] = self.carve(8 * 512 * 2, BF16)
            b['X0'] = self.carve(8 * 128 * 2, BF16)
            b['XX0'] = self.carve(8 * 256 * 2, BF16)
            b['XX1'] = self.carve(8 * 256 * 2, BF16)
            b['RH'] = self.carve(512 * 2, BF16)
            b['GT'] = self.carve(256 * 2, BF16)
            b['H'] = self.carve(256 * 4, F32)
            b['Ht'] = self.carve(256 * 4, F32)
            b['Hb'] = self.carve(256 * 2, BF16)
            b['ost'] = self.carve(512 * 2, BF16)
            b['id'] = i
            return b
        sets = [mkset(i) for i in range(NSET)]

        def unit(b, si, dr, cg, n):
            s0, S = self.seqs[si]
            t0 = s0 + 128 * n
            sid = b['id']
            cs = slice(4 * cg, 4 * cg + 4)
            sfx = str(dr)
            v3 = lambda nm, c: b[nm].ap.rearrange("p (c t) -> p c t", c=c)
            self.dma_batch('sp', ('r2l', sid), [
                (v3(nm, 4), fm(dn)[:, cs, t0:t0 + 128], self.dres(dn, t0, t0 + 128), [b[nm]])
                for nm, dn in (('rT', 'rT'), ('kkT', 'kkT'), ('bT', 'bT' + sfx), ('kdT', 'kdT' + sfx), ('sigT', 'sigT' + sfx), ('vT', 'vT'))])
            yield
            bk, bh = self.bank()
            bkb = bh.bitcast(BF16)
            self.tr(bk, [(bkb[:, c * 128:(c + 1) * 128], v3('sigT', 4)[:, c, :]) for c in range(4)], [b['sigT'], self.cb])
            self.cp('act', b['sigtok'].ap, bkb[:, 0:512], [bk], [b['sigtok']])
            yield
            tri = self.cbv('tri_f' if dr == 0 else 'tri_b')
            GG = b['GG'].ap.rearrange("p (c a t) -> p c a t", c=4, a=2)
            Ginv = v3('Ginv', 4)
            Ghat = v3('Ghat', 4)
            for j in range(2):
                bk, _ = self.bank()
                self.mm(bk, [(bk.ap[:, cc * 256:(cc + 1) * 256], b['sigtok'].ap[:, (2 * j + cc) * 128:(2 * j + cc + 1) * 128], tri, True, True)
                             for cc in range(2)], [b['sigtok'], self.cb])
                bv = bk.ap.rearrange("p (c a t) -> p c a t", c=2, a=2)
                self.act(b['GG'].ap[:, j * 512:(j + 1) * 512], bk.ap, AF.Exp, [bk], [b['GG']], scale=-KAPPA)
                self.act(Ginv[:, 2 * j:2 * j + 2, :], bv[:, :, 0, :], AF.Exp, [bk], [b['Ginv']], scale=KAPPA)
            tl = 127 if dr == 0 else 0
            gC = GG[:, :, 0, tl:tl + 1]
            self.tt('dve', Ghat, Ginv, gC.broadcast_to([128, 4, 128]), ALU.mult, [b['Ginv'], b['GG']], [b['Ghat']])
            yield
            AR = b['AR'].ap.rearrange("p (c a t) -> p c a t", c=4, a=2)
            BK = b['BK'].ap.rearrange("p (c a t) -> p c a t", c=4, a=2)
            BKh = b['BKh'].ap.rearrange("p (c a t) -> p c a t", c=4, a=2)
            self.stt('dve', AR[:, :, 0, :], v3('kkT', 4), -1.0, GG[:, :, 1, :], ALU.mult, ALU.mult, [b['kkT'], b['GG']], [b['AR']])
            self.tt('pool', AR[:, :, 1, :], v3('rT', 4), GG[:, :, 0, :], ALU.mult, [b['rT'], b['GG']], [b['AR']])
            self.tt('dve', BK[:, :, 0, :], v3('bT', 4), Ginv, ALU.mult, [b['bT'], b['Ginv']], [b['BK']])
            self.tt('pool', BK[:, :, 1, :], v3('kdT', 4), Ginv, ALU.mult, [b['kdT'], b['Ginv']], [b['BK']])
            self.tt('dve', BKh[:, :, 0, :], v3('bT', 4), Ghat, ALU.mult, [b['bT'], b['Ghat']], [b['BKh']])
            self.tt('pool', BKh[:, :, 1, :], v3('kdT', 4), Ghat, ALU.mult, [b['kdT'], b['Ghat']], [b['BKh']])
            yield
            Z = [b['Z0'], b['Z1']]
            Zv = [z.ap.rearrange("p (h x) -> p h x", h=8) for z in Z]
            bk, bh = self.bank()
            bkb = bh.bitcast(BF16)
            self.tr(bk, [(bkb[:, c * 128:(c + 1) * 128], AR[:, c, 0, :]) for c in range(4)], [b['AR'], self.cb])
            self.cp('act', Zv[0][:, :, 0:64], bkb[:, 0:512].rearrange("p (h x) -> p h x", h=8), [bk], [Z[0]])
            for src, srcb, dst, eng in ((lambda c: BKh[:, c, 0, :], b['BKh'], 'bhtok', 'dve'),
                                        (lambda c: BKh[:, c, 1, :], b['BKh'], 'khtok', 'act'),
                                        (lambda c: v3('vT', 4)[:, c, :], b['vT'], 'vtok', 'dve')):
                bk, bh = self.bank()
                bkb = bh.bitcast(BF16)
                self.tr(bk, [(bkb[:, c * 128:(c + 1) * 128], src(c)) for c in range(4)], [srcb, self.cb])
                self.cp(eng, b[dst].ap, bkb[:, 0:512], [bk], [b[dst]])
            yield
            XT = b['XT'].ap.rearrange("p (h x) -> p h x", h=8)
            X0 = b['X0'].ap.rearrange("p (h x) -> p h x", h=8)
            mT = self.cbv('mT_f' if dr == 0 else 'mT_b')
            mL = self.cbv('mL_f' if dr == 0 else 'mL_b')
            for h in range(8):
                c, half = h // 2, h % 2
                pr = slice(64 * half, 64 * half + 64)
                ar2 = AR[pr, c, :, :].rearrange("p a t -> p (a t)")
                bk, _ = self.bank()
                self.mm(bk, [(bk.ap[:, 0:256], BK[pr, c, 0, :], ar2, True, True),
                             (bk.ap[:, 256:512], BK[pr, c, 1, :], ar2, True, True)], [b['BK'], b['AR']])
                self.tt('dve' if h % 2 == 0 else 'dve', XT[:, h, :], bk.ap, mT, ALU.mult, [bk, self.cb], [b['XT']])
            X0s = b['X0'].ap.rearrange("p (c f t) -> p c f t", c=4, f=2)
            for half in range(2):
                bk, _ = self.bank()
                pr = slice(64 * half, 64 * half + 64)
                mms = [(bk.ap[:, c * 128:(c + 1) * 128], AR[pr, c, 0, :], BK[pr, c, 0, :], True, True) for c in range(4)]
                self.mm(bk, mms, [b['AR'], b['BK']])
                self.tt('dve', X0s[:, :, half, :], bk.ap.rearrange("p (c t) -> p c t", c=4),
                        mL.rearrange("p (c t) -> p c t", c=4), ALU.mult, [bk, self.cb], [b['X0']])
            yield
            bk, _ = self.bank()
            self.mm(bk, [(bk.ap[:, h * 64:(h + 1) * 64], XT[:, h, 256:384], b['vtok'].ap[:, h * 64:(h + 1) * 64], True, True)
                         for h in range(8)], [b['XT'], b['vtok']])
            self.cp('act', Zv[0][:, :, 64:128], bk.ap.rearrange("p (h x) -> p h x", h=8), [bk], [Z[0]])
            yield
            XX = [b['XX0'], b['XX1']]
            XXv = [x.ap.rearrange("p (h a t) -> p h a t", h=8, a=2) for x in XX]
            for lev in range(7):
                zc, zn = lev % 2, (lev + 1) % 2
                if lev == 0:
                    Xk = lambda h: X0[:, h, :]
                    XTk = lambda h: XT[:, h, 0:128]
                    xres = [b['X0'], b['XT']]
                else:
                    pv = XXv[(lev - 1) % 2]
                    Xk = lambda h, pv=pv: pv[:, h, 0, :]
                    XTk = lambda h, pv=pv: pv[:, h, 1, :]
                    xres = [XX[(lev - 1) % 2]]
                for j in range(2):
                    bk, _ = self.bank()
                    mms = [(bk.ap, ident, Z[zc].ap[:, j * 512:(j + 1) * 512], True, False)]
                    for hl in range(4):
                        h = 4 * j + hl
                        mms.append((bk.ap[:, hl * 128:(hl + 1) * 128], XTk(h), Zv[zc][:, h, :], False, hl == 3))
                    self.mm(bk, mms, xres + [Z[zc], self.cb])
                    self.cp('act' if j == 0 else 'dve', Z[zn].ap[:, j * 512:(j + 1) * 512], bk.ap, [bk], [Z[zn]])
                if lev < 6:
                    for j in range(4):
                        bk, _ = self.bank()
                        mms = []
                        for hl in range(2):
                            h = 2 * j + hl
                            mms.append((bk.ap[:, hl * 256:hl * 256 + 128], XTk(h), Xk(h), True, True))
                            mms.append((bk.ap[:, hl * 256 + 128:hl * 256 + 256], Xk(h), XTk(h), True, True))
                        self.mm(bk, mms, xres)
                        self.cp('act' if j % 2 == 0 else 'dve', XX[lev % 2].ap[:, j * 512:(j + 1) * 512], bk.ap, [bk], [XX[lev % 2]])
                yield
            Zf, Zfv = Z[1], Zv[1]
            bk, _ = self.bank()
            mms = []
            for h in range(8):
                c, half = h // 2, h % 2
                pr = slice(64 * half, 64 * half + 64)
                mms.append((bk.ap[pr, c * 128:(c + 1) * 128], Zfv[:, h, 0:64], XT[:, h, 128:256], True, True))
            self.mm(bk, mms, [Zf, b['XT']])
            self.tt('dve', v3('RH', 4), bk.ap.rearrange("p (c t) -> p c t", c=4), AR[:, :, 1, :], ALU.add, [bk, b['AR']], [b['RH']])
            bk, _ = self.bank()
            mms = []
            for h in range(8):
                c, half = h // 2, h % 2
                pr = slice(64 * half, 64 * half + 64)
                mms.append((bk.ap[pr, c * 64:(c + 1) * 64], Zfv[:, h, 0:64], b['bhtok'].ap[:, h * 64:(h + 1) * 64], True, True))
            self.mm(bk, mms, [Zf, b['bhtok']])
            self.cp('act', b['GT'].ap, bk.ap[:, 0:256], [bk], [b['GT']])
            yield
            Hb = v3('Hb', 4)
            RH = v3('RH', 4)
            GT = v3('GT', 4)
            bk, _ = self.bank()
            mms = []
            for h in range(8):
                c, half = h // 2, h % 2
                pr = slice(64 * half, 64 * half + 64)
                o = bk.ap[pr, c * 128:(c + 1) * 128]
                mms.append((o, Hb[pr, c, :], RH[pr, c, :], True, False))
                mms.append((o, Zfv[:, h, 64:128], XT[:, h, 128:256], False, False))
                mms.append((o, b['vtok'].ap[:, h * 64:(h + 1) * 64], XT[:, h, 384:512], False, True))
            self.mm(bk, mms, [b['Hb'], b['RH'], Zf, b['XT'], b['vtok']])
            self.cp('act', b['ost'].ap, bk.ap, [bk], [b['ost']])
            self.dma('sp', fm('oT' + sfx)[:, cs, t0:t0 + 128], v3('ost', 4), ('r2o', sid), [b['ost']], self.dres('oT' + sfx, t0, t0 + 128))
            bk, _ = self.bank()
            mms = []
            for h in range(8):
                c, half = h // 2, h % 2
                pr = slice(64 * half, 64 * half + 64)
                o = bk.ap[pr, c * 64:(c + 1) * 64]
                mms.append((o, GT[pr, c, :], Hb[pr, c, :], True, False))
                mms.append((o, b['bhtok'].ap[:, h * 64:(h + 1) * 64], Zfv[:, h, 64:128], False, False))
                mms.append((o, b['khtok'].ap[:, h * 64:(h + 1) * 64], b['vtok'].ap[:, h * 64:(h + 1) * 64], False, True))
            self.mm(bk, mms, [b['GT'], b['Hb'], b['bhtok'], b['khtok'], Zf, b['vtok']])
            Hv = v3('H', 4)
            Htv = v3('Ht', 4)
            self.tt('dve', Htv, Hv, gC.broadcast_to([128, 4, 64]), ALU.mult, [b['H'], b['GG']], [b['Ht']])
            self.tt('dve', Hv, Htv, bk.ap[:, 0:256].rearrange("p (c t) -> p c t", c=4), ALU.add, [bk, b['Ht']], [b['H']])
            self.cp('act', b['Hb'].ap, b['H'].ap, [b['H']], [b['Hb']])
            yield

        def stream(b, si, dr, cg):
            self.memset('pool', b['H'].ap, 0.0, [b['H']])
            self.memset('pool', b['Hb'].ap, 0.0, [b['Hb']])
            nch = self.seqs[si][1] // 128
            order = range(nch) if dr == 0 else range(nch - 1, -1, -1)
            for n in order:
                yield from unit(b, si, dr, cg, n)
        todo = [(si, dr, cg) for si in range(2) for dr in range(2) for cg in range(2)]
        active = []
        free = list(range(NSET))
        while todo or active:
            while todo and free:
                i = free.pop(0)
                active.append((i, stream(sets[i], *todo.pop(0))))
            nxt = []
            for (i, gen) in active:
                try:
                    next(gen)
                    nxt.append((i, gen))
                except StopIteration:
                    free.append(i)
            active = nxt
        self.barrier()

    def odd_r3(self, l):
        self.reset()
        T = 512
        d = self.dram
        fm = lambda n: d[n].rearrange("(c p) s -> p c s", p=128)
        v3 = lambda b, n: b.ap.rearrange("p (c t) -> p c t", c=n)
        xts = [self.carve(8 * T * 4, F32) for _ in range(2)]
        lds = {nm: [self.carve(8 * T * 2, BF16) for _ in range(2)] for nm in ('oT0', 'oT1', 'bonT', 'gT')}
        yfs = [self.carve(8 * T * 2, BF16) for _ in range(2)]
        tmp = {nm: [self.carve(T * 4, F32) for _ in range(4)] for nm in ('y', 'yc', 'sd', 'yn')}
        sqs = [self.carve(T * 2, BF16) for _ in range(4)]
        blk = self.cbv('blk')

        def tile(ti):
            tok0 = ti * T
            xt = xts[ti % 2]
            xt_ap = v3(xt, 8)
            self.dma('sp', xt_ap, fm('xs')[:, :, tok0:tok0 + T], ('x', ti % 2), self.dres('xs', tok0, tok0 + T), [xt])
            L = {nm: lds[nm][ti % 2] for nm in lds}
            self.dma_batch('sp', ('r3', ti % 2), [
                (v3(L[nm], 8), fm(nm)[:, :, tok0:tok0 + T], self.dres(nm, tok0, tok0 + T), [L[nm]]) for nm in lds])
            yf = yfs[ti % 2]
            for c in range(8):
                p = (c % 2) + 2 * (ti % 2)
                y, yc, sd, yn, sqb = tmp['y'][p], tmp['yc'][p], tmp['sd'][p], tmp['yn'][p], sqs[p]
                o0 = v3(L['oT0'], 8)[:, c, :]
                o1 = v3(L['oT1'], 8)[:, c, :]
                self.tt('dve', y.ap, o0, o1, ALU.add, [L['oT0'], L['oT1']], [y])
                bk, _ = self.bank()
                self.mm(bk, [(bk.ap, blk, o0, True, False), (bk.ap, blk, o1, False, True)], [L['oT0'], L['oT1'], self.cb])
                self.stt('dve', yc.ap, bk.ap, -1.0 / 64.0, y.ap, ALU.mult, ALU.add, [bk, y], [yc])
                self.act(sqb.ap, yc.ap, AF.Square, [yc], [sqb], scale=0.125)
                bk, _ = self.bank()
                self.mm(bk, [(bk.ap, blk, sqb.ap, True, True)], [sqb, self.cb])
                self.act(sd.ap, bk.ap, AF.Sqrt, [bk], [sd], bias=GN_EPS, scale=1.0)
                self.recip(sd.ap, sd.ap, [sd], [sd])
                self.tt('dve', yn.ap, yc.ap, sd.ap, ALU.mult, [yc, sd], [yn])
                self.act(yn.ap, yn.ap, AF.Identity, [yn, self.cols], [yn], bias=self.col('lnb%d' % l, c), scale=self.col('lnw%d' % l, c))
                self.tt('pool', yn.ap, yn.ap, v3(L['bonT'], 8)[:, c, :], ALU.add, [yn, L['bonT']], [yn])
                self.tt('pool', v3(yf, 8)[:, c, :], yn.ap, v3(L['gT'], 8)[:, c, :], ALU.mult, [yn, L['gT']], [yf])
                yield

            def ev_add(oc, bk):
                self.tt('dve', xt_ap[:, oc, :], bk.ap, xt_ap[:, oc, :], ALU.add, [bk, xt], [xt])
            yield from self.linear_gen(l, 'wo_r', 8, 8, lambda kc: v3(yf, 8)[:, kc, :], [yf], T, ev_add)
            self.dma('sp', fm('xs')[:, :, tok0:tok0 + T], xt_ap, ('xst', ti % 2), [xt], self.dres('xs', tok0, tok0 + T))
            yield
        self.pipeline((tile(ti) for ti in range(self.ntiles)), 2)
        self.barrier()

    def phase_load(self):
        self.reset()
        T = 512
        d = self.dram
        xts = [self.carve(8 * T * 4, F32) for _ in range(2)]
        for ti in range(self.ntiles):
            tok0 = ti * T
            xt = xts[ti % 2]
            xt_ap = xt.ap.rearrange("p (c t) -> p c t", c=8)
            src = d['xT'].rearrange("(c p) s -> p c s", p=128)[:, :, tok0:tok0 + T]
            self.dma('sp', xt_ap, src, ('x', ti % 2), [], [xt])
            dst = d['xs'].rearrange("(c p) s -> p c s", p=128)[:, :, tok0:tok0 + T]
            self.dma('sp', dst, xt_ap, ('xst', ti % 2), [xt], self.dres('xs', tok0, tok0 + T))
        self.barrier()

    def run(self):
        self.prologue()
        for ph in self.plan:
            fn = getattr(self, 'phase_' + ph[0], None) or getattr(self, ph[0])
            fn(*ph[1:])


def default_plan(depth):
    plan = [('load',)]
    for l in range(depth):
        if l % 2 == 0:
            plan += [('even', l)]
        else:
            plan += [('odd', l)]
        plan += [('caffn', l, l == depth - 1)]
    return plan


def build_program(S_P, S_S, depth, plan):
    nc = bass.Bass("TRN2", target_bir_lowering=False)
    Stot = S_P + S_S
    dram = {}

    def dt_(name, shape, dt, kind):
        dram[name] = nc.dram_tensor(name, shape, dt, kind=kind).ap()
    _, ncols = col_layout(depth)
    dt_('xT', [D, Stot], F32, 'ExternalInput')
    dt_('memT', [D, 2 * NMEM], F32, 'ExternalInput')
    dt_('cols', [128, ncols], F32, 'ExternalInput')
    dt_('cbf', [128, NCB], F32, 'ExternalInput')
    dt_('rope', [128, 2, S_P], F32, 'ExternalInput')
    for l in range(depth):
        tot = weight_offsets(l)[1]
        dt_('wpk%d' % l, [128, tot], F32, 'ExternalInput')
        dt_('wbf%d' % l, [128, tot], BF16, 'Internal')
    dt_('yT', [D, Stot], F32, 'ExternalOutput')
    dt_('xs', [D, Stot], F32, 'Internal')
    for n in ('qT', 'kT', 'uT', 'bgT', 'yaT'):
        dt_(n, [512, Stot], BF16, 'Internal')
    dt_('vtok', [Stot, 512], BF16, 'Internal')
    for n in ('rT', 'kkT', 'vT', 'gT', 'bonT', 'sigT0', 'sigT1', 'bT0', 'bT1', 'kdT0', 'kdT1', 'oT0', 'oT1'):
        dt_(n, [D, Stot], BF16, 'Internal')

    with ExitStack() as es:
        arena = es.enter_context(nc.sbuf_tensor("arena", [128, ARENA_BYTES // 2], BF16))
        banks = [es.enter_context(nc.psum_tensor("pb%d" % i, [128, 512], F32)) for i in range(8)]
        g1 = Gen(nc, S_P, S_S, depth, plan, arena, banks, dram, Sched(), None)
        g1.run()
        wplan = g1.wrec
        sch = Sched()
        g = Gen(nc, S_P, S_S, depth, plan, arena, banks, dram, sch, wplan)
        g.run()
        final_marks = dict(sch.pending)
        keys = set(ENG[:4])
        for e in ENG:
            for (waits, fn, sk, inc) in sch.q[e]:
                for k, v in waits:
                    keys.add(k)
                if sk is not None:
                    keys.add(sk)
        for k in final_marks:
            keys.add(k)
        sems = {}
        for i, k in enumerate(sorted(keys, key=str)):
            sems[k] = es.enter_context(nc.semaphore("s%d" % i))
        handles = {'pe': None, 'act': None, 'dve': None, 'pool': None, 'sp': None}

        def run_q(ename, h):
            for (waits, fn, sk, inc) in sch.q[ename]:
                for k, v in waits:
                    h.wait_ge(sems[k], v)
                if fn is not None:
                    ins = fn(h)
                    ins.then_inc(sems[sk], inc)
        with nc.Block() as block:
            @block.tensor
            def _(t):
                run_q('pe', t)

            @block.scalar
            def _(s):
                run_q('act', s)

            @block.vector
            def _(v):
                run_q('dve', v)

            @block.gpsimd
            def _(gp):
                run_q('pool', gp)

            @block.sync
            def _(sy):
                run_q('sp', sy)
                for k, v in final_marks.items():
                    sy.wait_ge(sems[k], v)
        info = dict(nops=sch.nops, nsems=len(sems), q={e: len(sch.q[e]) for e in ENG})
    return nc, info


def prep_inputs(inputs, S_P, S_S, depth, n_cores=8):
    p = {k: np.asarray(v, np.float32) for k, v in inputs.items()}
    shared = {'cols': pack_cols(depth, p), 'cbf': make_cbf(), 'rope': np.ascontiguousarray(make_rope(S_P))}
    for l in range(depth):
        shared['wpk%d' % l] = pack_weights(l, p)
    in_maps = []
    for i in range(n_cores):
        m = dict(shared)
        m['xT'] = np.ascontiguousarray(np.concatenate([p['x_prompt'][i].T, p['x_sample'][i].T], axis=1))
        m['memT'] = np.ascontiguousarray(np.concatenate([p['mem_prompt'][i].T, p['mem_sample'][i].T], axis=1))
        in_maps.append(m)
    return in_maps


def run_model(inputs, S_P, S_S, depth, plan, n_cores=8, trace=False):
    nc, info = build_program(S_P, S_S, depth, plan)
    in_maps = prep_inputs(inputs, S_P, S_S, depth, n_cores)
    res = run_bass_kernel_spmd(nc, in_maps, core_ids=list(range(n_cores)), trace=trace)
    yp = np.stack([res.results[i]['yT'][:, :S_P].T for i in range(n_cores)], axis=0)
    ys = np.stack([res.results[i]['yT'][:, S_P:].T for i in range(n_cores)], axis=0)
    return (np.ascontiguousarray(yp, dtype=np.float32), np.ascontiguousarray(ys, dtype=np.float32)), info, res


def kernel(**inputs):
    (yp, ys), info, res = run_model(inputs, 4096, 2048, 4, default_plan(4))
    return (yp, ys)
```

```python
import numpy as np
import concourse.bass as bass
import concourse.mybir as mybir
from concourse.bass_utils import run_bass_kernel_spmd
from contextlib import ExitStack

F32 = mybir.dt.float32
BF16 = mybir.dt.bfloat16
AF = mybir.ActivationFunctionType
ALU = mybir.AluOpType

D = 1024
NMEM = 256
DFF = 2816
KAPPA = 0.6065306597126334
RMS_EPS = 1e-6
GN_EPS = 64e-5
ROPE_THETA = 500000.0
ENG = ['pe', 'act', 'dve', 'pool', 'sp']
SAME_ENG_SYNC = False
NO_R1_STORES = False
R1_DEPTH = 2
CA_DEPTH = 2
R2_STAG = 4
PIPE_STAG = 0
R2_SPLIT = False


class Res:
    __slots__ = ('w', 'r', 'const')

    def __init__(self, const=False):
        self.w = None
        self.r = {}
        self.const = const


class Buf:
    __slots__ = ('ap', 'res')

    def __init__(self, ap, const=False):
        self.ap = ap
        self.res = Res(const)


class Sched:
    def __init__(self):
        self.q = {e: [] for e in ENG}
        self.cnt = {e: 0 for e in ENG}
        self.waited = {e: {} for e in ENG}
        self.dcnt = {}
        self.pending = {}
        self.nops = 0

    def _collect(self, eng, reads, writes, is_dma=False):
        deps = {}
        for r in reads:
            d = r.w
            if d is not None and deps.get(d[0], 0) < d[1]:
                deps[d[0]] = d[1]
        for w in writes:
            d = w.w
            if d is not None and deps.get(d[0], 0) < d[1]:
                deps[d[0]] = d[1]
            for k, v in w.r.items():
                if deps.get(k, 0) < v:
                    deps[k] = v
        out = []
        wt = self.waited[eng]
        for k, v in deps.items():
            if k == eng and not is_dma and not SAME_ENG_SYNC:
                continue
            if wt.get(k, 0) >= v:
                continue
            wt[k] = v
            out.append((k, v))
        return out

    def _mark(self, done, reads, writes):
        k, v = done
        for r in reads:
            if not r.const and r.r.get(k, 0) < v:
                r.r[k] = v
        for w in writes:
            w.w = done
            w.r = {}

    def op(self, eng, fn, reads=(), writes=()):
        waits = self._collect(eng, reads, writes)
        self.cnt[eng] += 1
        done = (eng, self.cnt[eng])
        self._mark(done, reads, writes)
        self.q[eng].append((waits, fn, eng, 1))
        self.nops += 1

    def dma(self, qeng, fn, key, reads=(), writes=()):
        waits = self._collect(qeng, reads, writes, is_dma=True)
        k = ('d', key)
        c = self.dcnt.get(k, 0) + 1
        self.dcnt[k] = c
        done = (k, 16 * c)
        self._mark(done, reads, writes)
        self.pending[k] = 16 * c
        self.q[qeng].append((waits, fn, k, 16))
        self.nops += 1

    def wait_all(self, engs, marks):
        for e in engs:
            wt = self.waited[e]
            ws = []
            for k, v in marks.items():
                if wt.get(k, 0) >= v:
                    continue
                wt[k] = v
                ws.append((k, v))
            if ws:
                self.q[e].append((ws, None, None, 0))


def _chunks_lhsT(W, M):
    Kin, Nout = W.shape
    if Kin < 128:
        Wp = np.zeros((128, Nout), np.float32)
        Wp[:Kin] = W
        return [np.ascontiguousarray(Wp[:, o * M:(o + 1) * M]) for o in range(Nout // M)]
    KC = Kin // 128
    W3 = W.reshape(KC, 128, Nout)
    return [np.ascontiguousarray(W3[:, :, o * M:(o + 1) * M].transpose(1, 0, 2)).reshape(128, KC * M)
            for o in range(Nout // M)]


def weight_spec(l):
    sp = []
    if l % 2 == 0:
        sp += [('in_q', 4, 1024), ('in_k', 4, 1024), ('in_v', 2, 2048), ('in_bg', 4, 1024),
               ('in_cg', 4, 1024), ('in_h', 4, 1024), ('w_out', 8, 1024)]
    else:
        sp += [('wr', 8, 1024), ('wk', 8, 1024), ('wv', 8, 1024), ('wo_r', 8, 1024),
               ('w1_0', 1, 512), ('w1_1', 1, 512), ('a1_0', 1, 512), ('a1_1', 1, 512), ('g1', 1, 1024),
               ('w2_0', 1, 1024), ('w2_1', 1, 1024), ('a2_0', 1, 1024), ('a2_1', 1, 1024), ('g2', 1, 1024)]
    sp += [('ca_q', 8, 1024), ('ca_k', 8, 1024), ('ca_v', 4, 2048), ('ca_o', 8, 1024),
           ('ff_gu', 44, 1024), ('ff_d', 8, 2816)]
    return sp


def weight_offsets(l):
    off = {}
    o = 0
    for name, n, F in weight_spec(l):
        off[name] = (o, n, F)
        o += n * F
    return off, o


def pack_weights(l, p):
    ch = {}
    if l % 2 == 0:
        e = l // 2
        win = p['ab_w_in'][e]
        ch['in_q'] = _chunks_lhsT(win[:, 0:512], 128)
        ch['in_k'] = _chunks_lhsT(win[:, 512:1024], 128)
        ch['in_v'] = [h for a in _chunks_lhsT(win[:, 1024:1536], 512) for h in (a[:, 0:2048], a[:, 2048:4096])]
        ch['in_bg'] = _chunks_lhsT(win[:, 1536:2048], 128)
        ch['in_cg'] = _chunks_lhsT(win[:, 2048:2560], 128)
        ch['in_h'] = _chunks_lhsT(win[:, 2560:3072], 128)
        ch['w_out'] = _chunks_lhsT(p['ab_w_out'][e], 128)
    else:
        o = l // 2
        ch['wr'] = _chunks_lhsT(p['rw_wr'][o], 128)
        ch['wk'] = _chunks_lhsT(p['rw_wk'][o], 128)
        ch['wv'] = _chunks_lhsT(p['rw_wv'][o], 128)
        ch['wo_r'] = _chunks_lhsT(p['rw_wo'][o], 128)
        for d in range(2):
            ch['w1_%d' % d] = _chunks_lhsT(p['rw_w1'][o, d], 64)
            ch['a1_%d' % d] = _chunks_lhsT(p['rw_a1'][o, d], 64)
            ch['w2_%d' % d] = _chunks_lhsT(p['rw_w2'][o, d], 1024)
            ch['a2_%d' % d] = _chunks_lhsT(p['rw_a2'][o, d], 1024)
        ch['g1'] = _chunks_lhsT(p['rw_g1'][o], 128)
        ch['g2'] = _chunks_lhsT(p['rw_g2'][o], 1024)
    ch['ca_q'] = _chunks_lhsT(p['ca_wq'][l], 128)
    ch['ca_k'] = _chunks_lhsT(p['ca_wkv'][l][:, 0:1024], 128)
    ch['ca_v'] = [h for a in _chunks_lhsT(p['ca_wkv'][l][:, 1024:2048], 512) for h in (a[:, 0:2048], a[:, 2048:4096])]
    ch['ca_o'] = _chunks_lhsT(p['ca_wo'][l], 128)
    ch['ff_gu'] = _chunks_lhsT(p['ffn_wgu'][l], 128)
    ch['ff_d'] = _chunks_lhsT(p['ffn_wdown'][l], 128)
    parts = []
    for name, n, F in weight_spec(l):
        assert len(ch[name]) == n, (name, len(ch[name]), n)
        for a in ch[name]:
            assert a.shape == (128, F), (name, a.shape, F)
            parts.append(a)
    return np.ascontiguousarray(np.concatenate(parts, axis=1))


def col_layout(depth):
    names = []
    for l in range(depth):
        names += [('nmix%d' % l, 8), ('ncross%d' % l, 8), ('nmem%d' % l, 8), ('nffn%d' % l, 8)]
        if l % 2 == 0:
            names += [('conv%d' % l, 12)]
        else:
            names += [('mu%d' % l, 48), ('w0_%d' % l, 16), ('a0_%d' % l, 16), ('kk%d' % l, 8), ('ka%d' % l, 8),
                      ('rk%d' % l, 8), ('lnw%d' % l, 8), ('lnb%d' % l, 8), ('omka%d' % l, 8)]
    names += [('nfinal', 8)]
    off = {}
    o = 0
    for n, k in names:
        off[n] = o
        o += k
    return off, o


def _colv(v):
    return np.asarray(v, np.float32).reshape(-1, 128).T


def pack_cols(depth, p):
    off, n = col_layout(depth)
    cols = np.zeros((128, n), np.float32)

    def put(name, arr):
        cols[:, off[name]:off[name] + arr.shape[1]] = arr
    for l in range(depth):
        put('nmix%d' % l, _colv(p['norm_mix'][l]))
        put('ncross%d' % l, _colv(p['norm_cross'][l]))
        put('nmem%d' % l, _colv(p['norm_mem'][l]))
        put('nffn%d' % l, _colv(p['norm_ffn'][l]))
        if l % 2 == 0:
            cw = p['ab_conv'][l // 2]
            put('conv%d' % l, np.concatenate([_colv(cw[j]) for j in range(3)], axis=1))
        else:
            o = l // 2
            put('mu%d' % l, np.concatenate([_colv(p['rw_mu'][o, i]) for i in range(6)], axis=1))
            put('w0_%d' % l, np.concatenate([_colv(p['rw_w0'][o, d]) for d in range(2)], axis=1))
            put('a0_%d' % l, np.concatenate([_colv(p['rw_a0'][o, d]) for d in range(2)], axis=1))
            put('kk%d' % l, _colv(p['rw_kk'][o]))
            put('ka%d' % l, _colv(p['rw_ka'][o]))
            put('rk%d' % l, _colv(p['rw_rk'][o].reshape(-1)))
            put('lnw%d' % l, _colv(p['rw_lnx_w'][o]))
            put('lnb%d' % l, _colv(p['rw_lnx_b'][o]))
    put('nfinal', _colv(p['norm_final']))
    return cols


CB = {}
_o = 0
for _n, _k in [('ident', 128), ('ones', 128), ('blk', 128), ('pswap', 128), ('tri_f', 256), ('tri_b', 256),
               ('mT_f', 512), ('mL_f', 512), ('mT_b', 512), ('mL_b', 512),
               ('am_n', 512), ('am_f', 512), ('am_l', 512), ('am_fl', 512), ('am_s', 512)]:
    CB[_n] = (_o, _k)
    _o += _k
NCB = _o


def make_cbf():
    c = np.zeros((128, NCB), np.float32)

    def put(n, a):
        o, k = CB[n]
        assert a.shape == (128, k), (n, a.shape)
        c[:, o:o + k] = a
    I = np.eye(128, dtype=np.float32)
    put('ident', I)
    put('ones', np.ones((128, 128), np.float32))
    blk = np.zeros((128, 128), np.float32)
    blk[0:64, 0:64] = 1
    blk[64:128, 64:128] = 1
    put('blk', blk)
    ps = np.zeros((128, 128), np.float32)
    for m in range(128):
        j = m % 64
        if j < 8:
            ps[m + 8, m] = 1
        elif j < 16:
            ps[m - 8, m] = 1
    put('pswap', ps)
    s = np.arange(128)[:, None]
    t = np.arange(128)[None, :]
    put('tri_f', np.concatenate([(s <= t), (s < t)], axis=1).astype(np.float32))
    put('tri_b', np.concatenate([(s >= t), (s > t)], axis=1).astype(np.float32))
    stf, inf_ = (s < t).astype(np.float32), (s <= t).astype(np.float32)
    stb, inb = (s > t).astype(np.float32), (s >= t).astype(np.float32)
    put('mT_f', np.concatenate([stf, inf_, stf, inf_], axis=1))
    put('mT_b', np.concatenate([stb, inb, stb, inb], axis=1))
    Lf = (t < s).astype(np.float32)
    Lb = (t > s).astype(np.float32)
    put('mL_f', np.concatenate([Lf] * 4, axis=1))
    put('mL_b', np.concatenate([Lb] * 4, axis=1))
    jj = np.arange(128)[:, None]
    i1 = np.arange(128)[None, :]
    i2 = np.arange(256)[None, :]
    M0 = ((jj - i1 >= 0) & (jj - i1 <= 128)).astype(np.float32)
    M1 = ((i2 - jj >= 0) & (i2 - jj <= 128)).astype(np.float32)
    M2 = M1[:, 0:128]
    lo = (jj >= 64).astype(np.float32)
    hi = (jj < 64).astype(np.float32)
    put('am_n', np.concatenate([M0, M1, M2], axis=1))
    put('am_f', np.concatenate([M0 * lo, M1, M2], axis=1))
    put('am_l', np.concatenate([M0, M1, M2 * hi], axis=1))
    put('am_fl', np.concatenate([M0 * lo, M1, M2 * hi], axis=1))
    put('am_s', np.concatenate([M0 * lo, M1[:, 0:128] * hi, np.zeros((128, 256), np.float32)], axis=1))
    return c


def make_rope(S):
    half = 8
    inv = np.power(np.float32(ROPE_THETA), (np.float32(-2.0) * np.arange(half, dtype=np.float32) / np.float32(16.0))).astype(np.float32)
    pos = np.arange(S, dtype=np.float32)
    ang = (pos[None, :] * inv[:, None]).astype(np.float32).astype(np.float64)
    cos = np.ones((128, S), np.float32)
    sin = np.zeros((128, S), np.float32)
    for pbase in (0, 64):
        cos[pbase:pbase + 8] = np.cos(ang)
        cos[pbase + 8:pbase + 16] = np.cos(ang)
        sin[pbase:pbase + 8] = -np.sin(ang)
        sin[pbase + 8:pbase + 16] = np.sin(ang)
    return np.stack([cos, sin], axis=1)


ARENA_BYTES = 204 * 1024
NWS = 5
WS_BYTES = 5632
WLOOK = 3


def _bp(ap):
    b = ap.base_partition
    return b() if callable(b) else b


class Gen:
    def __init__(self, nc, S_P, S_S, depth, plan, arena, banks, dram, sch, wplan):
        self.nc = nc
        self.S_P, self.S_S = S_P, S_S
        self.Stot = S_P + S_S
        self.seqs = [(0, S_P), (S_P, S_S)]
        self.depth = depth
        self.plan = plan
        self.arena = arena
        self.dram = dram
        self.sch = sch
        self.T = 512
        self.ntiles = self.Stot // 512
        self.coff, self.ncols = col_layout(depth)
        self.woff = [weight_offsets(l) for l in range(depth)]
        self.dres_ = {}
        self.wres = [Res() for _ in range(depth)]
        self.pbank = 0
        self.banks = [Buf(b[:, :]) for b in banks]
        self.bank_h = banks
        self.bankres = set(id(b.res) for b in self.banks)
        self.top = 0
        self.cb = self.carve(NCB * 2, BF16, const=True)
        self.cols = self.carve(self.ncols * 4, F32, const=True)
        self.wslots = [self.carve(WS_BYTES, BF16) for _ in range(NWS)]
        self.scr = {e: self.carve(16, F32) for e in ('act', 'dve', 'pool')}
        self.base = self.top
        self.wplan = wplan
        self.wrec = []
        self.wpos = 0
        self.wissued = 0

    def carve(self, nbytes, dt, const=False, shape=None):
        off = self.top
        self.top += (nbytes + 31) // 32 * 32
        assert self.top <= ARENA_BYTES, ('arena overflow', self.top)
        ap = self.arena[:, off // 2:(off + nbytes) // 2]
        if dt == F32:
            ap = ap.bitcast(F32)
        b = Buf(ap, const)
        return b

    def reset(self):
        self.top = self.base

    def bank(self):
        b = self.banks[self.pbank]
        h = self.bank_h[self.pbank]
        self.pbank = (self.pbank + 1) % len(self.banks)
        return b, h

    def cbv(self, name):
        o, k = CB[name]
        return self.cb.ap[:, o:o + k]

    def col(self, name, i):
        o = self.coff[name] + i
        return self.cols.ap[:, o:o + 1]

    def dres(self, name, t0, t1):
        out = []
        for t in range(t0 // 512, (t1 + 511) // 512):
            k = (name, t)
            if k not in self.dres_:
                self.dres_[k] = Res()
            out.append(self.dres_[k])
        return out

    def op(self, eng, fn, reads=(), writes=()):
        rr = [b.res if isinstance(b, Buf) else b for b in reads]
        ww = [b.res if isinstance(b, Buf) else b for b in writes]
        for r in rr:
            if id(r) in self.bankres and r not in ww:
                ww.append(r)
        self.sch.op(eng, fn, rr, ww)

    def dma(self, q, out_ap, in_ap, key, reads=(), writes=()):
        def fn(e, o=out_ap, i=in_ap):
            return e.dma_start(out=o, in_=i)
        self.sch.dma(q, fn, key, [b.res if isinstance(b, Buf) else b for b in reads],
                     [b.res if isinstance(b, Buf) else b for b in writes])

    def dma_batch(self, q, key, items):
        rs, ws = [], []
        for (o, i, rd, wr) in items:
            self.dma(q, o, i, key, rd, wr)
            rs += [b.res if isinstance(b, Buf) else b for b in rd]
            ws += [b.res if isinstance(b, Buf) else b for b in wr]
        k = ('d', key)
        final = self.sch.dcnt[k] * 16
        for r in ws:
            r.w = (k, final)
        for r in rs:
            if not r.const:
                r.r[k] = final

    def mm(self, bank, mms, reads):
        def fn(t, mms=mms):
            ins = None
            for (o, l, r, st, sp) in mms:
                kw = {}
                rp, cp = _bp(l), _bp(o)
                if rp != 0 or cp != 0:
                    kw['tile_position'] = (rp, cp)
                ins = t.matmul(o, l, r, start=st, stop=sp, skip_group_check=True, **kw)
            return ins
        self.op('pe', fn, reads, [bank])

    def tr(self, bank, trs, reads):
        idn = self.cbv('ident')

        def fn(t, trs=trs):
            ins = None
            for (o, i) in trs:
                ins = t.transpose(o, i, idn)
            return ins
        self.op('pe', fn, reads, [bank])

    def act(self, out, in_, func, reads, writes, bias=None, scale=None, eng='act'):
        kw = {}
        if bias is not None:
            kw['bias'] = bias
        if scale is not None:
            kw['scale'] = scale

        def fn(e, out=out, in_=in_):
            return e.activation(out=out, in_=in_, func=func, **kw)
        self.op('act', fn, reads, writes)

    def tt(self, eng, out, in0, in1, op, reads, writes):
        def fn(e, out=out, in0=in0, in1=in1):
            return e.tensor_tensor(out=out, in0=in0, in1=in1, op=op)
        self.op(eng, fn, reads, writes)

    def ts(self, eng, out, in0, s1, s2, op0, op1, reads, writes):
        def fn(e, out=out, in0=in0):
            if s2 is None:
                return e.tensor_scalar(out, in0, s1, None, op0)
            return e.tensor_scalar(out, in0, s1, s2, op0, op1)
        self.op(eng, fn, reads, writes)

    def stt(self, eng, out, in0, scalar, in1, op0, op1, reads, writes):
        eng = 'dve'
        def fn(e, out=out, in0=in0, in1=in1):
            return e.scalar_tensor_tensor(out=out, in0=in0, scalar=scalar, in1=in1, op0=op0, op1=op1)
        self.op(eng, fn, reads, writes)

    def cp(self, eng, out, in_, reads, writes):
        if eng == 'act':
            def fn(e, out=out, in_=in_):
                return e.copy(out, in_)
        else:
            def fn(e, out=out, in_=in_):
                return e.tensor_copy(out=out, in_=in_)
        self.op(eng, fn, reads, writes)

    def recip(self, out, in_, reads, writes):
        def fn(e, out=out, in_=in_):
            return e.reciprocal(out, in_)
        self.op('dve', fn, reads, writes)

    def memset(self, eng, ap, val, writes):
        def fn(e, ap=ap):
            return e.memset(ap, val)
        self.op(eng, fn, [], writes)

    def pipeline(self, gens, depth, stag=None):
        if stag is None:
            stag = PIPE_STAG
        gens = iter(gens)
        active = []
        rnd = 0
        done = False
        while True:
            if not done and len(active) < depth and (stag <= 0 or rnd % stag == 0 or not active):
                while len(active) < depth:
                    g = next(gens, None)
                    if g is None:
                        done = True
                        break
                    active.append(g)
                    if stag > 0:
                        break
            rnd += 1
            if not active:
                if done:
                    break
                continue
            nxt = []
            for g in active:
                try:
                    next(g)
                    nxt.append(g)
                except StopIteration:
                    pass
            active = nxt

    def barrier(self):
        s = self.sch
        b0 = self.banks[0]

        idn = self.cbv('ident')
        bk = self.bank_h[0]

        def pe_fn(t):
            return t.matmul(bk[0:1, 0:1], idn[0:1, 0:1], idn[0:1, 0:1], start=True, stop=True, skip_group_check=True)
        self.op('pe', pe_fn, [self.cb], [b0])
        self.op('act', lambda e: e.memzero(self.scr['act'].ap), [], [self.scr['act']])
        self.op('dve', lambda e: e.memset(self.scr['dve'].ap, 0.0), [], [self.scr['dve']])
        self.op('pool', lambda e: e.memset(self.scr['pool'].ap, 0.0), [], [self.scr['pool']])
        marks = dict(s.pending)
        for e in ('pe', 'act', 'dve', 'pool'):
            marks[e] = s.cnt[e]
        s.wait_all(ENG, marks)
        s.pending = {}

    def _wissue(self, i, req):
        l, name, idx = req
        o, n, F = self.woff[l][0][name]
        slot = self.wslots[i % NWS]
        src = self.dram['wbf%d' % l][:, o + idx * F:o + (idx + 1) * F]
        self.dma('sp', slot.ap[:, 0:F], src, ('w', i % NWS), [self.wres[l]], [slot])
        self.wissued = i + 1

    def wget(self, l, name, idx, M=None):
        req = (l, name, idx)
        i = self.wpos
        self.wpos += 1
        if self.wplan is None:
            self.wrec.append(req)
            self._wissue(i, req)
        else:
            assert self.wplan[i] == req, (i, self.wplan[i], req)
            tgt = min(i + WLOOK, len(self.wplan) - 1)
            while self.wissued <= tgt:
                self._wissue(self.wissued, self.wplan[self.wissued])
        slot = self.wslots[i % NWS]
        F = self.woff[l][0][name][2]
        ap = slot.ap[:, 0:F]
        if M is not None:
            ap = ap.rearrange("p (k m) -> p k m", m=M)
        return slot, ap

    def prologue(self):
        d = self.dram
        self.dma('pool', self.cb.ap, d['cbf'], 'c0', [], [self.cb])
        self.dma('sp', self.cols.ap, d['cols'], 'c1', [], [self.cols])
        PIECE = 8192
        for l in range(self.depth):
            tot = self.woff[l][1]
            for c0 in range(0, tot, PIECE):
                c1 = min(tot, c0 + PIECE)
                self.dma('pool', d['wbf%d' % l][:, c0:c1], d['wpk%d' % l][:, c0:c1], 'wc%d' % l, [], [self.wres[l]])
        for l in range(self.depth):
            if l % 2 == 1:
                o1 = self.coff['omka%d' % l]
                o0 = self.coff['ka%d' % l]
                self.ts('dve', self.cols.ap[:, o1:o1 + 8], self.cols.ap[:, o0:o0 + 8], -1.0, 1.0, ALU.mult, ALU.add,
                        [self.cols], [self.cols])

    def rmsnorm(self, xt, xt_ap, ncolsT, gname, sq, sq_ap, rstd, rstd_ap, outs):
        n = ncolsT
        self.act(sq_ap, xt_ap, AF.Square, [xt], [sq], scale=1.0 / 32.0)
        bk, _ = self.bank()
        ones = self.cbv('ones')
        self.mm(bk, [(bk.ap[:, 0:n], ones, sq_ap[:, c, :], c == 0, c == 7) for c in range(8)], [sq, self.cb])
        self.act(rstd_ap, bk.ap[:, 0:n], AF.Sqrt, [bk], [rstd], bias=RMS_EPS, scale=1.0)
        self.recip(rstd_ap, rstd_ap, [rstd], [rstd])
        for (engs, ob, oap) in outs:
            for c in range(8):
                self.stt(engs[c % len(engs)], oap[:, c, :], xt_ap[:, c, :], self.col(gname, c), rstd_ap,
                         ALU.mult, ALU.mult, [xt, rstd, self.cols], [ob])

    def linear(self, l, wname, n_oc, KC, rhs_fn, rhs_reads, N, evac, oc0=0, M=128, Kp=128):
        for oc in range(n_oc):
            slot, w = self.wget(l, wname, oc0 + oc, M=M)
            bk, _ = self.bank()
            self.mm(bk, [(bk.ap[0:M, 0:N], w[0:Kp, kc, :], rhs_fn(kc), kc == 0, kc == KC - 1) for kc in range(KC)],
                    [slot] + rhs_reads)
            evac(oc, bk)

    def linear_gen(self, l, wname, n_oc, KC, rhs_fn, rhs_reads, N, evac, oc0=0, M=128, Kp=128, every=2):
        for oc in range(n_oc):
            slot, w = self.wget(l, wname, oc0 + oc, M=M)
            bk, _ = self.bank()
            self.mm(bk, [(bk.ap[0:M, 0:N], w[0:Kp, kc, :], rhs_fn(kc), kc == 0, kc == KC - 1) for kc in range(KC)],
                    [slot] + rhs_reads)
            evac(oc, bk)
            if oc % every == every - 1:
                yield

    def phase_caffn(self, l, final):
        self.reset()
        T = 512
        d = self.dram
        xts = [self.carve(8 * T * 4, F32) for _ in range(2)]
        sqs_ = [self.carve(8 * T * 2, BF16) for _ in range(2)]
        xns_ = [self.carve(8 * T * 2, BF16) for _ in range(2)]
        qTs_ = [self.carve(8 * T * 2, BF16) for _ in range(2)]
        oTs_ = [self.carve(8 * T * 2, BF16) for _ in range(2)]
        pTs = [self.carve(2 * T * 2, BF16) for _ in range(4)]
        rdens = [self.carve(T * 4, F32) for _ in range(4)]
        rstds_ = [self.carve(T * 4, F32) for _ in range(2)]
        sgs = [self.carve(T * 2, BF16) for _ in range(4)]
        sq, xn, qT, oT, rstd = sqs_[0], xns_[0], qTs_[0], oTs_[0], rstds_[0]
        hb = self.carve(22 * T * 2, BF16)
        kTs = [self.carve(8 * 256 * 2, BF16) for _ in range(2)]
        vtoks = [self.carve(2 * 1024 * 2, BF16) for _ in range(2)]
        v3 = lambda b, n: b.ap.rearrange("p (c t) -> p c t", c=n)

        mns = []
        for si in range(2):
            mt = xts[si]
            mt_ap = mt.ap[:, 0:8 * 256].rearrange("p (c t) -> p c t", c=8)
            src = d['memT'].rearrange("(c p) t -> p c t", p=128)[:, :, si * 256:(si + 1) * 256]
            self.dma('sp', mt_ap, src, ('x', si), [], [mt])
            mn = sq if si == 0 else xn
            mn_ap = mn.ap[:, 0:8 * 256].rearrange("p (c t) -> p c t", c=8)
            sq_ap = qT.ap[:, 0:8 * 256].rearrange("p (c t) -> p c t", c=8)
            self.rmsnorm(mt, mt_ap, 256, 'nmem%d' % l, qT, sq_ap, rstd, rstd.ap[:, 0:256], [(('dve', 'pool'), mn, mn_ap)])
            mns.append((mn, mn_ap))
        for oc in range(8):
            slot, w = self.wget(l, 'ca_k', oc, M=128)
            for si in range(2):
                mn, mn_ap = mns[si]
                bk, _ = self.bank()
                self.mm(bk, [(bk.ap[:, 0:256], w[:, kc, :], mn_ap[:, kc, :], kc == 0, kc == 7) for kc in range(8)], [slot, mn])
                self.cp('act', v3(kTs[si], 8)[:, oc, :], bk.ap[:, 0:256], [bk], [kTs[si]])
        for half in range(2):
            slot0, w0 = self.wget(l, 'ca_v', 2 * half, M=512)
            slot1, w1 = self.wget(l, 'ca_v', 2 * half + 1, M=512)
            wk = lambda kc: (w0[:, kc, :] if kc < 4 else w1[:, kc - 4, :])
            for si in range(2):
                mn, mn_ap = mns[si]
                for blk in range(2):
                    bk, _ = self.bank()
                    self.mm(bk, [(bk.ap[:, 0:512], mn_ap[:, kc, blk * 128:(blk + 1) * 128], wk(kc), kc == 0, kc == 7)
                                 for kc in range(8)], [slot0, slot1, mn])
                    self.cp('dve', v3(vtoks[si], 2)[:, blk, half * 512:(half + 1) * 512], bk.ap[:, 0:512], [bk], [vtoks[si]])

        ones = self.cbv('ones')
        xsrc = 'xs'
        hb_turn = [0]

        def tile(ti):
            tp = ti % 2
            sq, xn, qT, oT, rstd = sqs_[tp], xns_[tp], qTs_[tp], oTs_[tp], rstds_[tp]
            tok0 = ti * T
            si = 0 if tok0 < self.S_P else 1
            xt = xts[ti % 2]
            xt_ap = v3(xt, 8)
            src = d[xsrc].rearrange("(c p) s -> p c s", p=128)[:, :, tok0:tok0 + T]
            self.dma('sp', xt_ap, src, ('x', ti % 2), self.dres(xsrc, tok0, tok0 + T), [xt])
            xn_ap = v3(xn, 8)
            sq_ap = v3(sq, 8)
            self.rmsnorm(xt, xt_ap, T, 'ncross%d' % l, sq, sq_ap, rstd, rstd.ap, [(('dve', 'pool'), xn, xn_ap)])
            yield
            qT_ap = v3(qT, 8)
            oT_ap = v3(oT, 8)
            kT_ap = v3(kTs[si], 8)
            vt_ap = v3(vtoks[si], 2)

            def ev_q(oc, bk):
                self.cp('act', qT_ap[:, oc, :], bk.ap, [bk], [qT])
            yield from self.linear_gen(l, 'ca_q', 8, 8, lambda kc: xn_ap[:, kc, :], [xn], T, ev_q)
            for h in range(4):
                pT = pTs[2 * tp + h % 2]
                pT_ap = v3(pT, 2)
                rden = rdens[2 * tp + h % 2]
                for blk in range(2):
                    bk, _ = self.bank()
                    self.mm(bk, [(bk.ap, kT_ap[:, 2 * h + dc, blk * 128:(blk + 1) * 128], qT_ap[:, 2 * h + dc, :], dc == 0, dc == 1)
                                 for dc in range(2)], [kTs[si], qT])
                    self.act(pT_ap[:, blk, :], bk.ap, AF.Exp, [bk], [pT], scale=1.0 / 16.0)
                bk, _ = self.bank()
                self.mm(bk, [(bk.ap, ones, pT_ap[:, blk, :], blk == 0, blk == 1) for blk in range(2)], [pT, self.cb])
                self.recip(rden.ap, bk.ap, [bk], [rden])
                for dc in range(2):
                    bk, _ = self.bank()
                    c = 2 * h + dc
                    self.mm(bk, [(bk.ap, vt_ap[:, blk, c * 128:(c + 1) * 128], pT_ap[:, blk, :], blk == 0, blk == 1)
                                 for blk in range(2)], [vtoks[si], pT])
                    self.tt('dve', oT_ap[:, c, :], bk.ap, rden.ap, ALU.mult, [bk, rden], [oT])
                yield

            def ev_add(oc, bk):
                self.tt('dve', xt_ap[:, oc, :], bk.ap, xt_ap[:, oc, :], ALU.add, [bk, xt], [xt])
            yield from self.linear_gen(l, 'ca_o', 8, 8, lambda kc: oT_ap[:, kc, :], [oT], T, ev_add)
            self.rmsnorm(xt, xt_ap, T, 'nffn%d' % l, sq, sq_ap, rstd, rstd.ap, [(('dve', 'pool'), xn, xn_ap)])
            yield
            while hb_turn[0] != ti:
                yield
            h_ap = v3(hb, 22)
            for c in range(22):
                slg, wg = self.wget(l, 'ff_gu', c, M=128)
                bg, _ = self.bank()
                self.mm(bg, [(bg.ap, wg[:, kc, :], xn_ap[:, kc, :], kc == 0, kc == 7) for kc in range(8)], [slg, xn])
                slu, wu = self.wget(l, 'ff_gu', 22 + c, M=128)
                bu, _ = self.bank()
                self.mm(bu, [(bu.ap, wu[:, kc, :], xn_ap[:, kc, :], kc == 0, kc == 7) for kc in range(8)], [slu, xn])
                sg = sgs[2 * tp + c % 2]
                self.act(sg.ap, bg.ap, AF.Silu, [bg], [sg])
                self.tt('dve', h_ap[:, c, :], bu.ap, sg.ap, ALU.mult, [bu, sg], [hb])
                if c % 2 == 1:
                    yield
            yield from self.linear_gen(l, 'ff_d', 8, 22, lambda kc: h_ap[:, kc, :], [hb], T, ev_add, every=1)
            hb_turn[0] = ti + 1
            if not final:
                dst = d['xs'].rearrange("(c p) s -> p c s", p=128)[:, :, tok0:tok0 + T]
                self.dma('sp', dst, xt_ap, ('xst', ti % 2), [xt], self.dres('xs', tok0, tok0 + T))
            else:
                self.rmsnorm(xt, xt_ap, T, 'nfinal', sq, sq_ap, rstd, rstd.ap, [(('dve',), xt, xt_ap)])
                dst = d['yT'].rearrange("(c p) s -> p c s", p=128)[:, :, tok0:tok0 + T]
                self.dma('sp', dst, xt_ap, ('xst', ti % 2), [xt], self.dres('yT', tok0, tok0 + T))
            yield
        self.pipeline((tile(ti) for ti in range(self.ntiles)), CA_DEPTH)
        self.barrier()

    def phase_even(self, l):
        self.even_e1(l)
        self.even_e2(l)
        self.even_e3(l)

    def even_e1(self, l):
        self.reset()
        T = 512
        d = self.dram
        v3 = lambda b, n: b.ap.rearrange("p (c t) -> p c t", c=n)
        xts = [self.carve(8 * T * 4, F32) for _ in range(2)]
        sqs_ = [self.carve(8 * T * 2, BF16) for _ in range(2)]
        xns_ = [self.carve(8 * T * 2, BF16) for _ in range(2)]
        rstds_ = [self.carve(T * 4, F32) for _ in range(2)]
        rope = self.carve(2 * self.S_P * 4, F32, const=True)
        rope_ap = rope.ap.rearrange("p (a s) -> p a s", a=2)
        qraws = [self.carve(T * 2, BF16) for _ in range(2)]
        t1s = [self.carve(T * 4, F32) for _ in range(2)]
        t2s = [self.carve(T * 4, F32) for _ in range(2)]
        cgs = [self.carve(T * 4, F32) for _ in range(2)]
        stg = {n: [self.carve(4 * T * 2, BF16) for _ in range(2)] for n in ('q', 'k', 'u', 'bg', 'v')}
        self.dma('sp', rope_ap, d['rope'], 'c2', [], [rope])
        pswap = self.cbv('pswap')
        cnt = 0

        def tile(ti):
            nonlocal cnt
            sq, xn, rstd = sqs_[ti % 2], xns_[ti % 2], rstds_[ti % 2]
            tok0 = ti * T
            s0 = 0 if tok0 < self.S_P else self.S_P
            pos0 = tok0 - s0
            xt = xts[ti % 2]
            xt_ap = v3(xt, 8)
            src = d['xs'].rearrange("(c p) s -> p c s", p=128)[:, :, tok0:tok0 + T]
            self.dma('sp', xt_ap, src, ('x', ti % 2), self.dres('xs', tok0, tok0 + T), [xt])
            xn_ap = v3(xn, 8)
            self.rmsnorm(xt, xt_ap, T, 'nmix%d' % l, sq, v3(sq, 8), rstd, rstd.ap, [(('dve',), xn, xn_ap)])
            yield
            for wname, sn, dn in (('in_q', 'q', 'qT'), ('in_k', 'k', 'kT')):
                st = stg[sn][ti % 2]
                st_ap = v3(st, 4)
                for oc in range(4):
                    slot, w = self.wget(l, wname, oc, M=128)
                    bq, _ = self.bank()
                    self.mm(bq, [(bq.ap, w[:, kc, :], xn_ap[:, kc, :], kc == 0, kc == 7) for kc in range(8)], [slot, xn])
                    qraw = qraws[cnt % 2]
                    t1 = t1s[cnt % 2]
                    t2 = t2s[cnt % 2]
                    cnt += 1
                    self.cp('act', qraw.ap, bq.ap, [bq], [qraw])
                    bp, _ = self.bank()
                    self.mm(bp, [(bp.ap, pswap, qraw.ap, True, True)], [qraw, self.cb])
                    self.tt('dve', t1.ap, bq.ap, rope_ap[:, 0, pos0:pos0 + T], ALU.mult, [bq, rope], [t1])
                    self.tt('dve', t2.ap, bp.ap, rope_ap[:, 1, pos0:pos0 + T], ALU.mult, [bp, rope], [t2])
                    self.tt('pool', st_ap[:, oc, :], t1.ap, t2.ap, ALU.add, [t1, t2], [st])
                    yield
                dst = d[dn].rearrange("(c p) s -> p c s", p=128)[:, :, tok0:tok0 + T]
                self.dma('sp', dst, st_ap, ('st' + sn, ti % 2), [st], self.dres(dn, tok0, tok0 + T))
            slot0, w0 = self.wget(l, 'in_v', 0, M=512)
            slot1, w1 = self.wget(l, 'in_v', 1, M=512)
            wk = lambda kc: (w0[:, kc, :] if kc < 4 else w1[:, kc - 4, :])
            st = stg['v'][ti % 2]
            st_ap = v3(st, 4)
            for tb in range(4):
                bk, _ = self.bank()
                self.mm(bk, [(bk.ap, xn_ap[:, kc, tb * 128:(tb + 1) * 128], wk(kc), kc == 0, kc == 7) for kc in range(8)],
                        [slot0, slot1, xn])
                self.cp('act', st_ap[:, tb, :], bk.ap, [bk], [st])
            yield
            dst = d['vtok'][tok0:tok0 + T, :].rearrange("(tb p) f -> p tb f", p=128)
            self.dma('sp', dst, st_ap, ('stv', ti % 2), [st], self.dres('vtok', tok0, tok0 + T))
            ust = stg['u'][ti % 2]
            bgst = stg['bg'][ti % 2]
            for cc in range(4):
                slot, w = self.wget(l, 'in_cg', cc, M=128)
                bc, _ = self.bank()
                self.mm(bc, [(bc.ap, w[:, kc, :], xn_ap[:, kc, :], kc == 0, kc == 7) for kc in range(8)], [slot, xn])
                slot, w = self.wget(l, 'in_h', cc, M=128)
                bh, _ = self.bank()
                self.mm(bh, [(bh.ap, w[:, kc, :], xn_ap[:, kc, :], kc == 0, kc == 7) for kc in range(8)], [slot, xn])
                slot, w = self.wget(l, 'in_bg', cc, M=128)
                bb, _ = self.bank()
                self.mm(bb, [(bb.ap, w[:, kc, :], xn_ap[:, kc, :], kc == 0, kc == 7) for kc in range(8)], [slot, xn])
                cg = cgs[cc % 2]
                self.cp('act', cg.ap, bc.ap, [bc], [cg])
                self.tt('dve', v3(ust, 4)[:, cc, :], bh.ap, cg.ap, ALU.mult, [bh, cg], [ust])
                self.cp('act', v3(bgst, 4)[:, cc, :], bb.ap, [bb], [bgst])
                yield
            for sn, dn, st in (('u', 'uT', ust), ('bg', 'bgT', bgst)):
                dst = d[dn].rearrange("(c p) s -> p c s", p=128)[:, :, tok0:tok0 + T]
                self.dma('sp', dst, v3(st, 4), ('st' + sn, ti % 2), [st], self.dres(dn, tok0, tok0 + T))
            yield
        self.pipeline((tile(ti) for ti in range(self.ntiles)), 2)
        self.barrier()

    def even_e2(self, l):
        self.reset()
        d = self.dram
        SM = self.S_P
        DIL = (1, 4, 16)
        q1 = self.carve(SM * 2, BF16)
        qd = {1: q1, 4: self.carve(SM * 2, BF16), 16: self.carve(SM * 2, BF16)}
        kp = {dd: self.carve((SM + 128 * dd) * 2, BF16) for dd in DIL}
        vb = {dd: self.carve((SM // 128 + dd) * 128 * 2, BF16) for dd in DIL}
        acc = self.carve(2 * SM * 4, F32)
        pTs = [self.carve(512 * 2, BF16) for _ in range(6)]
        yast = self.carve(SM * 2, BF16)
        ones = self.cbv('ones')
        pcnt = 0
        for si, (s0, S) in enumerate(self.seqs):
            for hp in range(4):
                frow = slice(hp * 128, (hp + 1) * 128)
                for dd in DIL:
                    L_ = S // dd
                    nb_ = L_ // 128 + 1
                    kv_ = kp[dd].ap[:, 0:S + 128 * dd].rearrange("p (r m) -> p r m", r=dd)
                    self.memset('pool', kv_[:, :, 0:64], 0.0, [kp[dd]])
                    self.memset('pool', kv_[:, :, 64 + L_:128 + L_], 0.0, [kp[dd]])
                    v4_ = vb[dd].ap[:, 0:dd * nb_ * 128].rearrange("p (r b f) -> p r b f", r=dd, b=nb_)
                    self.memset('pool', v4_[0:64, :, 0, :], 0.0, [vb[dd]])
                    self.memset('pool', v4_[64:128, :, nb_ - 1, :], 0.0, [vb[dd]])
                self.dma('sp', q1.ap[:, 0:S], d['qT'][frow, s0:s0 + S], ('e2q', 0), self.dres('qT', s0, s0 + S), [q1])
                self.dma('sp', kp[1].ap[:, 64:64 + S], d['kT'][frow, s0:s0 + S], ('e2k', 0), self.dres('kT', s0, s0 + S), [kp[1]])
                qv, kv, vv = {}, {}, {}
                qv[1] = q1.ap[:, 0:S].rearrange("p (r m) -> p r m", r=1)
                kv[1] = kp[1].ap[:, 0:S + 128].rearrange("p (r m) -> p r m", r=1)
                for dd in (4, 16):
                    L = S // dd
                    qv[dd] = qd[dd].ap[:, 0:S].rearrange("p (r m) -> p r m", r=dd)
                    kv[dd] = kp[dd].ap[:, 0:S + 128 * dd].rearrange("p (r m) -> p r m", r=dd)
                    self.cp('act', qv[dd], q1.ap[:, 0:S].rearrange("p (m r) -> p r m", r=dd), [q1], [qd[dd]])
                    self.cp('pool', kv[dd][:, :, 64:64 + L], kp[1].ap[:, 64:64 + S].rearrange("p (m r) -> p r m", r=dd),
                            [kp[1]], [kp[dd]])
                for dd in DIL:
                    L = S // dd
                    nb = L // 128 + 1
                    v4 = vb[dd].ap[:, 0:dd * nb * 128].rearrange("p (r b f) -> p r b f", r=dd, b=nb)
                    vv[dd] = v4
                    vres = self.dres('vtok', s0, s0 + S)
                    R0 = s0 + 64 * dd
                    R1 = s0 + dd * (L - 64)
                    if nb > 2:
                        srcm = d['vtok'][R0:R0 + dd * 128 * (nb - 2), frow].rearrange("(b j r) f -> r j b f", j=128, r=dd)
                    srcf = d['vtok'][s0:s0 + 64 * dd, frow].rearrange("(j r) f -> r j f", r=dd)
                    srcl = d['vtok'][R1:R1 + 64 * dd, frow].rearrange("(j r) f -> r j f", r=dd)
                    for r in range(dd):
                        if nb > 2:
                            self.dma('sp', v4[:, r, 1:nb - 1, :], srcm[r], ('e2v', dd), vres, [vb[dd]])
                        self.dma('sp', v4[64:128, r, 0, :], srcf[r], ('e2v', dd), vres, [vb[dd]])
                        self.dma('sp', v4[0:64, r, nb - 1, :], srcl[r], ('e2v', dd), vres, [vb[dd]])
                for dd in DIL:
                    L = S // dd
                    Q = min(256, L)
                    ntile = L // Q
                    accv = acc.ap[:, 0:2 * SM].rearrange("p (a s) -> p a s", a=2)[:, :, 0:S].rearrange(
                        "p a (m r) -> p a m r", r=dd)
                    def tbody(r, n, dd=dd, L=L, Q=Q, ntile=ntile, accv=accv):
                        nonlocal pcnt
                        if True:
                            if Q == 128:
                                mname = 'am_s'
                            else:
                                first, last = (n == 0), (n == ntile - 1)
                                mname = 'am_fl' if (first and last) else ('am_f' if first else ('am_l' if last else 'am_n'))
                            mask = self.cbv(mname)
                            W = 512 if Q == 256 else 256
                            qa = qv[dd][:, r, n * Q:(n + 1) * Q]
                            b0 = 2 * n if Q == 256 else 0
                            kb = lambda b: kv[dd][:, r, 128 * b:128 * b + 128]
                            pts = []
                            for hd in range(2):
                                pr = slice(64 * hd, 64 * hd + 64)
                                bk, _ = self.bank()
                                if Q == 256:
                                    mms = [(bk.ap[:, 0:128], kb(b0)[pr], qa[pr, 0:128], True, True),
                                           (bk.ap[:, 128:384], kb(b0 + 1)[pr], qa[pr, 0:256], True, True),
                                           (bk.ap[:, 384:512], kb(b0 + 2)[pr], qa[pr, 128:256], True, True)]
                                else:
                                    mms = [(bk.ap[:, 0:128], kb(0)[pr], qa[pr, 0:128], True, True),
                                           (bk.ap[:, 128:256], kb(1)[pr], qa[pr, 0:128], True, True)]
                                self.mm(bk, mms, [qd[dd], kp[dd]])
                                pT = pTs[pcnt % 6]
                                pcnt += 1
                                pts.append((pT, bk))
                            yield
                            for (pT, bk) in pts:
                                self.act(pT.ap[:, 0:W], bk.ap[:, 0:W], AF.Exp, [bk], [pT], scale=0.125)
                                self.tt('dve', pT.ap[:, 0:W], pT.ap[:, 0:W], mask[:, 0:W], ALU.mult, [pT, self.cb], [pT])
                            pts = [p for (p, _) in pts]
                            yield
                            bo, _ = self.bank()
                            mms = []
                            for hd in range(2):
                                pr = slice(64 * hd, 64 * hd + 64)
                                fs = slice(64 * hd, 64 * hd + 64)
                                pT = pts[hd].ap
                                vblk = lambda b: vv[dd][:, r, b, fs]
                                if Q == 256:
                                    mms += [(bo.ap[pr, 0:256], vblk(b0 + 1), pT[:, 128:384], True, False),
                                            (bo.ap[pr, 0:128], vblk(b0), pT[:, 0:128], False, False),
                                            (bo.ap[pr, 128:256], vblk(b0 + 2), pT[:, 384:512], False, False),
                                            (bo.ap[pr, 256:512], ones[:, 0:64], pT[:, 128:384], False, False),
                                            (bo.ap[pr, 256:384], ones[:, 0:64], pT[:, 0:128], False, False),
                                            (bo.ap[pr, 384:512], ones[:, 0:64], pT[:, 384:512], False, True)]
                                else:
                                    mms += [(bo.ap[pr, 0:128], vblk(0), pT[:, 0:128], True, False),
                                            (bo.ap[pr, 0:128], vblk(1), pT[:, 128:256], False, False),
                                            (bo.ap[pr, 256:384], ones[:, 0:64], pT[:, 0:128], False, False),
                                            (bo.ap[pr, 256:384], ones[:, 0:64], pT[:, 128:256], False, True)]
                            self.mm(bo, mms, [vb[dd], self.cb] + pts)
                            yield
                            av = accv[:, :, n * Q:(n + 1) * Q, r]
                            bov = bo.ap.rearrange("p (a q) -> p a q", a=2)[:, :, 0:Q]
                            if dd == 1:
                                self.cp('dve', av, bov, [bo], [acc])
                            else:
                                self.tt('dve', av, bov, av, ALU.add, [bo, acc], [acc])
                            yield
                    self.pipeline((tbody(r, n) for r in range(dd) for n in range(ntile)), 3)
                a2 = acc.ap[:, 0:2 * SM].rearrange("p (a s) -> p a s", a=2)
                self.recip(a2[:, 1, 0:S], a2[:, 1, 0:S], [acc], [acc])
                self.tt('dve', yast.ap[:, 0:S], a2[:, 0, 0:S], a2[:, 1, 0:S], ALU.mult, [acc], [yast])
                self.dma('sp', d['yaT'][frow, s0:s0 + S], yast.ap[:, 0:S], ('e2y', 0), [yast], self.dres('yaT', s0, s0 + S))
        self.barrier()

    def even_e3(self, l):
        self.reset()
        T = 512
        d = self.dram
        v3 = lambda b, n: b.ap.rearrange("p (c t) -> p c t", c=n)
        xts = [self.carve(8 * T * 4, F32) for _ in range(2)]
        yas = [self.carve(4 * T * 2, BF16) for _ in range(2)]
        uhs = [self.carve(4 * (T + 2) * 2, BF16) for _ in range(2)]
        bgs = [self.carve(4 * T * 2, BF16) for _ in range(2)]
        ybs = [self.carve(4 * T * 2, BF16) for _ in range(2)]
        tas = [self.carve(T * 4, F32) for _ in range(2)]

        def tile(ti):
            tok0 = ti * T
            s0, S = self.seqs[0] if tok0 < self.S_P else self.seqs[1]
            xt, ya, uh, bg, yb = xts[ti % 2], yas[ti % 2], uhs[ti % 2], bgs[ti % 2], ybs[ti % 2]
            xt_ap = v3(xt, 8)
            fm = lambda n: d[n].rearrange("(c p) s -> p c s", p=128)
            self.dma('sp', xt_ap, fm('xs')[:, :, tok0:tok0 + T], ('x', ti % 2), self.dres('xs', tok0, tok0 + T), [xt])
            self.dma('sp', v3(ya, 4), fm('yaT')[:, :, tok0:tok0 + T], ('e3a', ti % 2), self.dres('yaT', tok0, tok0 + T), [ya])
            self.dma('sp', v3(bg, 4), fm('bgT')[:, :, tok0:tok0 + T], ('e3b', ti % 2), self.dres('bgT', tok0, tok0 + T), [bg])
            uh_ap = uh.ap.rearrange("p (c t) -> p c t", c=4)
            lo = tok0 - 1
            hi = tok0 + T + 1
            c0 = 0
            if lo < s0:
                self.memset('pool', uh_ap[:, :, 0:1], 0.0, [uh])
                lo += 1
                c0 = 1
            c1 = T + 2
            if hi > s0 + S:
                self.memset('pool', uh_ap[:, :, T + 1:T + 2], 0.0, [uh])
                hi -= 1
                c1 = T + 1
            self.dma('sp', uh_ap[:, :, c0:c1], fm('uT')[:, :, lo:hi], ('e3u', ti % 2), self.dres('uT', lo, hi), [uh])
            for cc in range(4):
                ta = tas[cc % 2]
                cw = lambda j: self.col('conv%d' % l, j * 4 + cc)
                self.ts('dve', ta.ap, uh_ap[:, cc, 0:T], cw(0), None, ALU.mult, None, [uh, self.cols], [ta])
                self.stt('dve', ta.ap, uh_ap[:, cc, 1:T + 1], cw(1), ta.ap, ALU.mult, ALU.add, [uh, ta, self.cols], [ta])
                self.stt('dve', ta.ap, uh_ap[:, cc, 2:T + 2], cw(2), ta.ap, ALU.mult, ALU.add, [uh, ta, self.cols], [ta])
                self.tt('dve', v3(yb, 4)[:, cc, :], ta.ap, v3(bg, 4)[:, cc, :], ALU.mult, [ta, bg], [yb])
                yield

            def ev_add(oc, bk):
                self.tt('dve', xt_ap[:, oc, :], bk.ap, xt_ap[:, oc, :], ALU.add, [bk, xt], [xt])
            yield from self.linear_gen(l, 'w_out', 8, 8, lambda kc: (v3(ya, 4)[:, kc, :] if kc < 4 else v3(yb, 4)[:, kc - 4, :]),
                                       [ya, yb], T, ev_add)
            self.dma('sp', fm('xs')[:, :, tok0:tok0 + T], xt_ap, ('xst', ti % 2), [xt], self.dres('xs', tok0, tok0 + T))
            yield
        self.pipeline((tile(ti) for ti in range(self.ntiles)), 2)
        self.barrier()

    def phase_odd(self, l):
        self.odd_r1(l)
        self.odd_r2(l)
        self.odd_r3(l)

    def wload(self, l, name, idx, buf, key):
        o, n, F = self.woff[l][0][name]
        src = self.dram['wbf%d' % l][:, o + idx * F:o + (idx + 1) * F]
        self.dma('sp', buf.ap[:, 0:F], src, key, [self.wres[l]], [buf])

    def odd_r1(self, l):
        self.reset()
        T = 512
        d = self.dram
        fm = lambda n: d[n].rearrange("(c p) s -> p c s", p=128)
        xh = self.carve(8 * (T + 2) * 4, F32)
        xh_ap = xh.ap.rearrange("p (c t) -> p c t", c=8)
        xx = self.carve(8 * T * 2, BF16)
        xx_ap = xx.ap.rearrange("p (c t) -> p c t", c=8)
        xxf = self.carve(8 * T * 4, F32) if False else None
        sq = self.carve(8 * (T + 2) * 2, BF16)
        sq_ap = sq.ap.rearrange("p (c t) -> p c t", c=8)
        rstd = self.carve((T + 2) * 4, F32)
        mixes = [self.carve(8 * T * 2, BF16) for _ in range(4)]
        lw = {}
        for nm, F in (('w1_0', 512), ('w1_1', 512), ('a1_0', 512), ('a1_1', 512), ('g1', 1024),
                      ('w2_0', 1024), ('w2_1', 1024), ('a2_0', 1024), ('a2_1', 1024), ('g2', 1024)):
            lw[nm] = self.carve(F * 2, BF16)
        items = []
        for nm in lw:
            o_, n_, F_ = self.woff[l][0][nm]
            items.append((lw[nm].ap[:, 0:F_], d['wbf%d' % l][:, o_:o_ + F_], [self.wres[l]], [lw[nm]]))
        self.dma_batch('sp', 'r1w', items)
        hid = {nm: self.carve(T * 2, BF16) for nm in ('hw0', 'hw1', 'ha0', 'ha1', 'hg')}
        f32t = {nm: [self.carve(T * 4, F32) for _ in range(3 if nm == 'vf' else 2)] for nm in
                ('rf', 'kl', 'vf', 'as0', 'as1', 't0', 't1', 'rn', 'tsum', 'p1')}
        bft = {nm: [self.carve(T * 2, BF16) for _ in range(3 if nm in ('sig0', 'sig1', 'g') else 2)] for nm in
               ('r', 'v', 'g', 'sig0', 'sig1', 'kk', 'b0', 'b1', 'kd0', 'kd1', 'bon', 'sqk', 'rkb')}
        blk = self.cbv('blk')
        mi = 0
        for ti in range(self.ntiles):
            tok0 = ti * T
            s0, S = self.seqs[0] if tok0 < self.S_P else self.seqs[1]
            lo, hi, c0, c1 = tok0 - 1, tok0 + T + 1, 0, T + 2
            if lo < s0:
                self.memset('pool', xh_ap[:, :, 0:1], 0.0, [xh])
                lo, c0 = lo + 1, 1
            if hi > s0 + S:
                self.memset('pool', xh_ap[:, :, T + 1:T + 2], 0.0, [xh])
                hi, c1 = hi - 1, T + 1
            self.dma('sp', xh_ap[:, :, c0:c1], fm('xs')[:, :, lo:hi], ('x', 0), self.dres('xs', lo, hi), [xh])
            self.rmsnorm(xh, xh_ap[:, :, 0:T], T, 'nmix%d' % l, sq, sq_ap[:, :, 0:T], rstd, rstd.ap[:, 0:T],
                         [(('dve',), xh, xh_ap[:, :, 0:T])])
            self.rmsnorm(xh, xh_ap[:, :, T:T + 2], 2, 'nmix%d' % l, sq, sq_ap[:, :, T:T + 2], rstd, rstd.ap[:, T:T + 2],
                         [(('dve',), xh, xh_ap[:, :, T:T + 2])])
            self.tt('dve', xx_ap, xh_ap[:, :, 0:T], xh_ap[:, :, 2:T + 2], ALU.add, [xh], [xx])
            self.stt('dve', xx_ap, xx_ap, 0.5, xh_ap[:, :, 1:T + 1], ALU.mult, ALU.subtract, [xx, xh], [xx])

            def mk_mix(i):
                nonlocal mi
                mb = mixes[mi % 4]
                mi += 1
                m_ap = mb.ap.rearrange("p (c t) -> p c t", c=8)
                for c in range(8):
                    self.stt('dve', m_ap[:, c, :], xx_ap[:, c, :], self.col('mu%d' % l, i * 8 + c), xh_ap[:, c, 1:T + 1],
                             ALU.mult, ALU.add, [xx, xh, self.cols], [mb])
                return mb, m_ap

            def lora1(mb, m_ap, wn, hn, func, M):
                w = lw[wn].ap[:, 0:8 * M].rearrange("p (k m) -> p k m", m=M)
                bk, _ = self.bank()
                self.mm(bk, [(bk.ap[0:M, :], w[:, kc, :], m_ap[:, kc, :], kc == 0, kc == 7) for kc in range(8)], [lw[wn], mb])
                if func is None:
                    self.cp('act', hid[hn].ap[0:M, :], bk.ap[0:M, :], [bk], [hid[hn]])
                else:
                    self.act(hid[hn].ap[0:M, :], bk.ap[0:M, :], func, [bk], [hid[hn]])
            mb, m_ap = mk_mix(1)
            lora1(mb, m_ap, 'w1_0', 'hw0', AF.Tanh, 64)
            lora1(mb, m_ap, 'w1_1', 'hw1', AF.Tanh, 64)
            mb, m_ap = mk_mix(4)
            lora1(mb, m_ap, 'a1_0', 'ha0', None, 64)
            lora1(mb, m_ap, 'a1_1', 'ha1', None, 64)
            mb, m_ap = mk_mix(5)
            lora1(mb, m_ap, 'g1', 'hg', AF.Sigmoid, 128)
            mr, mr_ap = mk_mix(0)
            mk_, mk_ap = mk_mix(2)
            mv, mv_ap = mk_mix(3)
            def proj(oc, wname, m_, m_ap_):
                slot, w = self.wget(l, wname, oc, M=128)
                bk, _ = self.bank()
                self.mm(bk, [(bk.ap, w[:, kc, :], m_ap_[:, kc, :], kc == 0, kc == 7) for kc in range(8)], [slot, m_])
                return bk

            def front(oc):
                F = lambda nm: f32t[nm][oc % len(f32t[nm])]
                B = lambda nm: bft[nm][oc % len(bft[nm])]
                ocs = slice(oc * 128, (oc + 1) * 128)
                bk = proj(oc, 'wk', mk_, mk_ap)
                self.cp('act', F('kl').ap, bk.ap, [bk], [F('kl')])
                self.act(B('sqk').ap, bk.ap, AF.Square, [bk, self.cols], [B('sqk')], scale=self.col('kk%d' % l, oc))
                bk = proj(oc, 'wr', mr, mr_ap)
                self.cp('act', F('rf').ap, bk.ap, [bk], [F('rf')])
                bk = proj(oc, 'wv', mv, mv_ap)
                self.cp('act', F('vf').ap, bk.ap, [bk], [F('vf')])
                bk, _ = self.bank()
                self.mm(bk, [(bk.ap, lw['g2'].ap[:, ocs], hid['hg'].ap, True, True)], [lw['g2'], hid['hg']])
                self.cp('act', B('g').ap, bk.ap, [bk], [B('g')])
                for dd in range(2):
                    bk, _ = self.bank()
                    self.mm(bk, [(bk.ap, lw['w2_%d' % dd].ap[0:64, ocs], hid['hw%d' % dd].ap[0:64, :], True, True)],
                            [lw['w2_%d' % dd], hid['hw%d' % dd]])
                    self.act(B('sig%d' % dd).ap, bk.ap, AF.Sigmoid, [bk, self.cols], [B('sig%d' % dd)],
                             bias=self.col('w0_%d' % l, dd * 8 + oc), scale=1.0)
                    bk, _ = self.bank()
                    self.mm(bk, [(bk.ap, lw['a2_%d' % dd].ap[0:64, ocs], hid['ha%d' % dd].ap[0:64, :], True, True)],
                            [lw['a2_%d' % dd], hid['ha%d' % dd]])
                    self.act(F('as%d' % dd).ap, bk.ap, AF.Sigmoid, [bk, self.cols], [F('as%d' % dd)],
                             bias=self.col('a0_%d' % l, dd * 8 + oc), scale=1.0)
                    self.act(F('t%d' % dd).ap, F('as%d' % dd).ap, AF.Identity, [F('as%d' % dd), self.cols], [F('t%d' % dd)],
                             bias=self.col('omka%d' % l, oc), scale=self.col('ka%d' % l, oc))
                bk, _ = self.bank()
                self.mm(bk, [(bk.ap, blk, B('sqk').ap, True, True)], [B('sqk'), self.cb])
                self.act(F('rn').ap, bk.ap, AF.Sqrt, [bk], [F('rn')], bias=1e-24, scale=1.0)

            def chain(oc):
                F = lambda nm: f32t[nm][oc % len(f32t[nm])]
                B = lambda nm: bft[nm][oc % len(bft[nm])]
                self.cp('dve', B('r').ap, F('rf').ap, [F('rf')], [B('r')])
                self.cp('dve', B('v').ap, F('vf').ap, [F('vf')], [B('v')])
                self.recip(F('rn').ap, F('rn').ap, [F('rn')], [F('rn')])
                self.stt('dve', B('kk').ap, F('kl').ap, self.col('kk%d' % l, oc), F('rn').ap, ALU.mult, ALU.mult,
                         [F('kl'), F('rn'), self.cols], [B('kk')])
                for dd in range(2):
                    self.tt('dve', B('b%d' % dd).ap, B('kk').ap, F('as%d' % dd).ap, ALU.mult, [B('kk'), F('as%d' % dd)], [B('b%d' % dd)])
                    self.tt('dve', B('kd%d' % dd).ap, F('t%d' % dd).ap, F('kl').ap, ALU.mult, [F('t%d' % dd), F('kl')], [B('kd%d' % dd)])
                self.tt('dve', F('tsum').ap, F('t0').ap, F('t1').ap, ALU.add, [F('t0'), F('t1')], [F('tsum')])
                self.tt('dve', F('p1').ap, F('rf').ap, F('kl').ap, ALU.mult, [F('rf'), F('kl')], [F('p1')])
                self.stt('dve', B('rkb').ap, F('p1').ap, self.col('rk%d' % l, oc), F('tsum').ap, ALU.mult, ALU.mult,
                         [F('p1'), F('tsum'), self.cols], [B('rkb')])

            def tail(oc, tok0=tok0):
                F = lambda nm: f32t[nm][oc % len(f32t[nm])]
                B = lambda nm: bft[nm][oc % len(bft[nm])]
                ocs = slice(oc * 128, (oc + 1) * 128)
                bk, _ = self.bank()
                self.mm(bk, [(bk.ap, blk, B('rkb').ap, True, True)], [B('rkb'), self.cb])
                self.tt('dve', B('bon').ap, bk.ap, F('vf').ap, ALU.mult, [bk, F('vf')], [B('bon')])
                self.dma_batch('sp', ('r1s', oc % 2), [
                    (d[dn][ocs, tok0:tok0 + T], B(nm).ap, [B(nm)], self.dres(dn, tok0, tok0 + T))
                    for nm, dn in (('r', 'rT'), ('v', 'vT'), ('g', 'gT'), ('kk', 'kkT'), ('bon', 'bonT'), ('sig0', 'sigT0'),
                                   ('sig1', 'sigT1'), ('b0', 'bT0'), ('b1', 'bT1'), ('kd0', 'kdT0'), ('kd1', 'kdT1'))])
            front(0)
            for oc in range(8):
                if oc + 1 < 8:
                    front(oc + 1)
                chain(oc)
                if oc >= 1:
                    tail(oc - 1)
            tail(7)
        self.barrier()

    def odd_r2(self, l):
        self.reset()
        d = self.dram
        fm = lambda n: d[n].rearrange("(c p) s -> p c s", p=128)
        NSET = 3
        ident = self.cbv('ident')

        def mkset(i):
            b = {}
            for nm in ('rT', 'kkT', 'bT', 'kdT', 'sigT', 'vT'):
                b[nm] = self.carve(512 * 2, BF16)
            b['sigtok'] = self.carve(512 * 2, BF16)
            b['GG'] = self.carve(4 * 2 * 128 * 4, F32)
            b['Ginv'] = self.carve(512 * 4, F32)
            b['Ghat'] = self.carve(512 * 4, F32)
            b['AR'] = self.carve(1024 * 2, BF16)
            b['BK'] = self.carve(1024 * 2, BF16)
            b['BKh'] = self.carve(1024 * 2, BF16)
            b['Z0'] = self.carve(1024 * 2, BF16)
            b['Z1'] = self.carve(1024 * 2, BF16)
            for nm in ('bhtok', 'khtok', 'vtok'):
                b[nm] = self.carve(512 * 2, BF16)
            b['XT'] = self.carve(8 * 512 * 2, BF16)
            b['X0'] = self.carve(8 * 128 * 2, BF16)
            b['XX0'] = self.carve(8 * 256 * 2, BF16)
            b['XX1'] = self.carve(8 * 256 * 2, BF16)
            b['RH'] = self.carve(512 * 2, BF16)
            b['GT'] = self.carve(256 * 2, BF16)
            b['H'] = self.carve(256 * 4, F32)
            b['Ht'] = self.carve(256 * 4, F32)
            b['Hb'] = self.carve(256 * 2, BF16)
            b['ost'] = self.carve(512 * 2, BF16)
            b['id'] = i
            return b
        sets = [mkset(i) for i in range(NSET)]

        def unit(b, si, dr, cg, n):
            s0, S = self.seqs[si]
            t0 = s0 + 128 * n
            sid = b['id']
            cs = slice(4 * cg, 4 * cg + 4)
            sfx = str(dr)
            v3 = lambda nm, c: b[nm].ap.rearrange("p (c t) -> p c t", c=c)
            self.dma_batch('sp', ('r2l', sid), [
                (v3(nm, 4), fm(dn)[:, cs, t0:t0 + 128], self.dres(dn, t0, t0 + 128), [b[nm]])
                for nm, dn in (('rT', 'rT'), ('kkT', 'kkT'), ('bT', 'bT' + sfx), ('kdT', 'kdT' + sfx), ('sigT', 'sigT' + sfx), ('vT', 'vT'))])
            yield
            bk, bh = self.bank()
            bkb = bh.bitcast(BF16)
            self.tr(bk, [(bkb[:, c * 128:(c + 1) * 128], v3('sigT', 4)[:, c, :]) for c in range(4)], [b['sigT'], self.cb])
            self.cp('act', b['sigtok'].ap, bkb[:, 0:512], [bk], [b['sigtok']])
            yield
            tri = self.cbv('tri_f' if dr == 0 else 'tri_b')
            GG = b['GG'].ap.rearrange("p (c a t) -> p c a t", c=4, a=2)
            Ginv = v3('Ginv', 4)
            Ghat = v3('Ghat', 4)
            for j in range(2):
                bk, _ = self.bank()
                self.mm(bk, [(bk.ap[:, cc * 256:(cc + 1) * 256], b['sigtok'].ap[:, (2 * j + cc) * 128:(2 * j + cc + 1) * 128], tri, True, True)
                             for cc in range(2)], [b['sigtok'], self.cb])
                bv = bk.ap.rearrange("p (c a t) -> p c a t", c=2, a=2)
                self.act(b['GG'].ap[:, j * 512:(j + 1) * 512], bk.ap, AF.Exp, [bk], [b['GG']], scale=-KAPPA)
                self.act(Ginv[:, 2 * j:2 * j + 2, :], bv[:, :, 0, :], AF.Exp, [bk], [b['Ginv']], scale=KAPPA)
            tl = 127 if dr == 0 else 0
            gC = GG[:, :, 0, tl:tl + 1]
            self.tt('dve', Ghat, Ginv, gC.broadcast_to([128, 4, 128]), ALU.mult, [b['Ginv'], b['GG']], [b['Ghat']])
            yield
            AR = b['AR'].ap.rearrange("p (c a t) -> p c a t", c=4, a=2)
            BK = b['BK'].ap.rearrange("p (c a t) -> p c a t", c=4, a=2)
            BKh = b['BKh'].ap.rearrange("p (c a t) -> p c a t", c=4, a=2)
            self.stt('dve', AR[:, :, 0, :], v3('kkT', 4), -1.0, GG[:, :, 1, :], ALU.mult, ALU.mult, [b['kkT'], b['GG']], [b['AR']])
            self.tt('pool', AR[:, :, 1, :], v3('rT', 4), GG[:, :, 0, :], ALU.mult, [b['rT'], b['GG']], [b['AR']])
            self.tt('dve', BK[:, :, 0, :], v3('bT', 4), Ginv, ALU.mult, [b['bT'], b['Ginv']], [b['BK']])
            self.tt('pool', BK[:, :, 1, :], v3('kdT', 4), Ginv, ALU.mult, [b['kdT'], b['Ginv']], [b['BK']])
            self.tt('dve', BKh[:, :, 0, :], v3('bT', 4), Ghat, ALU.mult, [b['bT'], b['Ghat']], [b['BKh']])
            self.tt('pool', BKh[:, :, 1, :], v3('kdT', 4), Ghat, ALU.mult, [b['kdT'], b['Ghat']], [b['BKh']])
            yield
            Z = [b['Z0'], b['Z1']]
            Zv = [z.ap.rearrange("p (h x) -> p h x", h=8) for z in Z]
            bk, bh = self.bank()
            bkb = bh.bitcast(BF16)
            self.tr(bk, [(bkb[:, c * 128:(c + 1) * 128], AR[:, c, 0, :]) for c in range(4)], [b['AR'], self.cb])
            self.cp('act', Zv[0][:, :, 0:64], bkb[:, 0:512].rearrange("p (h x) -> p h x", h=8), [bk], [Z[0]])
            for src, srcb, dst, eng in ((lambda c: BKh[:, c, 0, :], b['BKh'], 'bhtok', 'dve'),
                                        (lambda c: BKh[:, c, 1, :], b['BKh'], 'khtok', 'act'),
                                        (lambda c: v3('vT', 4)[:, c, :], b['vT'], 'vtok', 'dve')):
                bk, bh = self.bank()
                bkb = bh.bitcast(BF16)
                self.tr(bk, [(bkb[:, c * 128:(c + 1) * 128], src(c)) for c in range(4)], [srcb, self.cb])
                self.cp(eng, b[dst].ap, bkb[:, 0:512], [bk], [b[dst]])
            yield
            XT = b['XT'].ap.rearrange("p (h x) -> p h x", h=8)
            X0 = b['X0'].ap.rearrange("p (h x) -> p h x", h=8)
            mT = self.cbv('mT_f' if dr == 0 else 'mT_b')
            mL = self.cbv('mL_f' if dr == 0 else 'mL_b')
            for h in range(8):
                c, half = h // 2, h % 2
                pr = slice(64 * half, 64 * half + 64)
                ar2 = AR[pr, c, :, :].rearrange("p a t -> p (a t)")
                bk, _ = self.bank()
                self.mm(bk, [(bk.ap[:, 0:256], BK[pr, c, 0, :], ar2, True, True),
                             (bk.ap[:, 256:512], BK[pr, c, 1, :], ar2, True, True)], [b['BK'], b['AR']])
                self.tt('dve' if h % 2 == 0 else 'dve', XT[:, h, :], bk.ap, mT, ALU.mult, [bk, self.cb], [b['XT']])
            X0s = b['X0'].ap.rearrange("p (c f t) -> p c f t", c=4, f=2)
            for half in range(2):
                bk, _ = self.bank()
                pr = slice(64 * half, 64 * half + 64)
                mms = [(bk.ap[:, c * 128:(c + 1) * 128], AR[pr, c, 0, :], BK[pr, c, 0, :], True, True) for c in range(4)]
                self.mm(bk, mms, [b['AR'], b['BK']])
                self.tt('dve', X0s[:, :, half, :], bk.ap.rearrange("p (c t) -> p c t", c=4),
                        mL.rearrange("p (c t) -> p c t", c=4), ALU.mult, [bk, self.cb], [b['X0']])
            yield
            bk, _ = self.bank()
            self.mm(bk, [(bk.ap[:, h * 64:(h + 1) * 64], XT[:, h, 256:384], b['vtok'].ap[:, h * 64:(h + 1) * 64], True, True)
                         for h in range(8)], [b['XT'], b['vtok']])
            self.cp('act', Zv[0][:, :, 64:128], bk.ap.rearrange("p (h x) -> p h x", h=8), [bk], [Z[0]])
            yield
            XX = [b['XX0'], b['XX1']]
            XXv = [x.ap.rearrange("p (h a t) -> p h a t", h=8, a=2) for x in XX]
            for lev in range(7):
                zc, zn = lev % 2, (lev + 1) % 2
                if lev == 0:
                    Xk = lambda h: X0[:, h, :]
                    XTk = lambda h: XT[:, h, 0:128]
                    xres = [b['X0'], b['XT']]
                else:
                    pv = XXv[(lev - 1) % 2]
                    Xk = lambda h, pv=pv: pv[:, h, 0, :]
                    XTk = lambda h, pv=pv: pv[:, h, 1, :]
                    xres = [XX[(lev - 1) % 2]]
                for j in range(2):
                    bk, _ = self.bank()
                    mms = [(bk.ap, ident, Z[zc].ap[:, j * 512:(j + 1) * 512], True, False)]
                    for hl in range(4):
                        h = 4 * j + hl
                        mms.append((bk.ap[:, hl * 128:(hl + 1) * 128], XTk(h), Zv[zc][:, h, :], False, hl == 3))
                    self.mm(bk, mms, xres + [Z[zc], self.cb])
                    self.cp('act' if j == 0 else 'dve', Z[zn].ap[:, j * 512:(j + 1) * 512], bk.ap, [bk], [Z[zn]])
                if R2_SPLIT:
                    yield
                if lev < 6:
                    for j in range(4):
                        bk, _ = self.bank()
                        mms = []
                        for hl in range(2):
                            h = 2 * j + hl
                            mms.append((bk.ap[:, hl * 256:hl * 256 + 128], XTk(h), Xk(h), True, True))
                            mms.append((bk.ap[:, hl * 256 + 128:hl * 256 + 256], Xk(h), XTk(h), True, True))
                        self.mm(bk, mms, xres)
                        self.cp('act' if j % 2 == 0 else 'dve', XX[lev % 2].ap[:, j * 512:(j + 1) * 512], bk.ap, [bk], [XX[lev % 2]])
                yield
            Zf, Zfv = Z[1], Zv[1]
            bk, _ = self.bank()
            mms = []
            for h in range(8):
                c, half = h // 2, h % 2
                pr = slice(64 * half, 64 * half + 64)
                mms.append((bk.ap[pr, c * 128:(c + 1) * 128], Zfv[:, h, 0:64], XT[:, h, 128:256], True, True))
            self.mm(bk, mms, [Zf, b['XT']])
            self.tt('dve', v3('RH', 4), bk.ap.rearrange("p (c t) -> p c t", c=4), AR[:, :, 1, :], ALU.add, [bk, b['AR']], [b['RH']])
            bk, _ = self.bank()
            mms = []
            for h in range(8):
                c, half = h // 2, h % 2
                pr = slice(64 * half, 64 * half + 64)
                mms.append((bk.ap[pr, c * 64:(c + 1) * 64], Zfv[:, h, 0:64], b['bhtok'].ap[:, h * 64:(h + 1) * 64], True, True))
            self.mm(bk, mms, [Zf, b['bhtok']])
            self.cp('act', b['GT'].ap, bk.ap[:, 0:256], [bk], [b['GT']])
            yield
            Hb = v3('Hb', 4)
            RH = v3('RH', 4)
            GT = v3('GT', 4)
            bk, _ = self.bank()
            mms = []
            for h in range(8):
                c, half = h // 2, h % 2
                pr = slice(64 * half, 64 * half + 64)
                o = bk.ap[pr, c * 128:(c + 1) * 128]
                mms.append((o, Hb[pr, c, :], RH[pr, c, :], True, False))
                mms.append((o, Zfv[:, h, 64:128], XT[:, h, 128:256], False, False))
                mms.append((o, b['vtok'].ap[:, h * 64:(h + 1) * 64], XT[:, h, 384:512], False, True))
            self.mm(bk, mms, [b['Hb'], b['RH'], Zf, b['XT'], b['vtok']])
            self.cp('act', b['ost'].ap, bk.ap, [bk], [b['ost']])
            self.dma('sp', fm('oT' + sfx)[:, cs, t0:t0 + 128], v3('ost', 4), ('r2o', sid), [b['ost']], self.dres('oT' + sfx, t0, t0 + 128))
            bk, _ = self.bank()
            mms = []
            for h in range(8):
                c, half = h // 2, h % 2
                pr = slice(64 * half, 64 * half + 64)
                o = bk.ap[pr, c * 64:(c + 1) * 64]
                mms.append((o, GT[pr, c, :], Hb[pr, c, :], True, False))
                mms.append((o, b['bhtok'].ap[:, h * 64:(h + 1) * 64], Zfv[:, h, 64:128], False, False))
                mms.append((o, b['khtok'].ap[:, h * 64:(h + 1) * 64], b['vtok'].ap[:, h * 64:(h + 1) * 64], False, True))
            self.mm(bk, mms, [b['GT'], b['Hb'], b['bhtok'], b['khtok'], Zf, b['vtok']])
            Hv = v3('H', 4)
            Htv = v3('Ht', 4)
            self.tt('dve', Htv, Hv, gC.broadcast_to([128, 4, 64]), ALU.mult, [b['H'], b['GG']], [b['Ht']])
            self.tt('dve', Hv, Htv, bk.ap[:, 0:256].rearrange("p (c t) -> p c t", c=4), ALU.add, [bk, b['Ht']], [b['H']])
            self.cp('act', b['Hb'].ap, b['H'].ap, [b['H']], [b['Hb']])
            yield

        def stream(b, si, dr, cg):
            self.memset('pool', b['H'].ap, 0.0, [b['H']])
            self.memset('pool', b['Hb'].ap, 0.0, [b['Hb']])
            nch = self.seqs[si][1] // 128
            order = range(nch) if dr == 0 else range(nch - 1, -1, -1)
            for n in order:
                yield from unit(b, si, dr, cg, n)
        todo = [(si, dr, cg) for si in range(2) for dr in range(2) for cg in range(2)]
        active = []
        free = list(range(NSET))
        rnd = 0
        while todo or active:
            while todo and free and (len(active) == 0 or rnd % R2_STAG == 0):
                i = free.pop(0)
                active.append((i, stream(sets[i], *todo.pop(0))))
                if R2_STAG > 1:
                    break
            rnd += 1
            nxt = []
            for (i, gen) in active:
                try:
                    next(gen)
                    nxt.append((i, gen))
                except StopIteration:
                    free.append(i)
            active = nxt
        self.barrier()

    def odd_r3(self, l):
        self.reset()
        T = 512
        d = self.dram
        fm = lambda n: d[n].rearrange("(c p) s -> p c s", p=128)
        v3 = lambda b, n: b.ap.rearrange("p (c t) -> p c t", c=n)
        xts = [self.carve(8 * T * 4, F32) for _ in range(2)]
        lds = {nm: [self.carve(8 * T * 2, BF16) for _ in range(2)] for nm in ('oT0', 'oT1', 'bonT', 'gT')}
        yfs = [self.carve(8 * T * 2, BF16) for _ in range(2)]
        tmp = {nm: [self.carve(T * 4, F32) for _ in range(4)] for nm in ('y', 'yc', 'sd', 'yn')}
        sqs = [self.carve(T * 2, BF16) for _ in range(4)]
        blk = self.cbv('blk')

        def tile(ti):
            tok0 = ti * T
            xt = xts[ti % 2]
            xt_ap = v3(xt, 8)
            self.dma('sp', xt_ap, fm('xs')[:, :, tok0:tok0 + T], ('x', ti % 2), self.dres('xs', tok0, tok0 + T), [xt])
            L = {nm: lds[nm][ti % 2] for nm in lds}
            self.dma_batch('sp', ('r3', ti % 2), [
                (v3(L[nm], 8), fm(nm)[:, :, tok0:tok0 + T], self.dres(nm, tok0, tok0 + T), [L[nm]]) for nm in lds])
            yf = yfs[ti % 2]
            for c in range(8):
                p = (c % 2) + 2 * (ti % 2)
                y, yc, sd, yn, sqb = tmp['y'][p], tmp['yc'][p], tmp['sd'][p], tmp['yn'][p], sqs[p]
                o0 = v3(L['oT0'], 8)[:, c, :]
                o1 = v3(L['oT1'], 8)[:, c, :]
                self.tt('dve', y.ap, o0, o1, ALU.add, [L['oT0'], L['oT1']], [y])
                bk, _ = self.bank()
                self.mm(bk, [(bk.ap, blk, o0, True, False), (bk.ap, blk, o1, False, True)], [L['oT0'], L['oT1'], self.cb])
                self.stt('dve', yc.ap, bk.ap, -1.0 / 64.0, y.ap, ALU.mult, ALU.add, [bk, y], [yc])
                self.act(sqb.ap, yc.ap, AF.Square, [yc], [sqb], scale=0.125)
                bk, _ = self.bank()
                self.mm(bk, [(bk.ap, blk, sqb.ap, True, True)], [sqb, self.cb])
                self.act(sd.ap, bk.ap, AF.Sqrt, [bk], [sd], bias=GN_EPS, scale=1.0)
                self.recip(sd.ap, sd.ap, [sd], [sd])
                self.tt('dve', yn.ap, yc.ap, sd.ap, ALU.mult, [yc, sd], [yn])
                self.act(yn.ap, yn.ap, AF.Identity, [yn, self.cols], [yn], bias=self.col('lnb%d' % l, c), scale=self.col('lnw%d' % l, c))
                self.tt('pool', yn.ap, yn.ap, v3(L['bonT'], 8)[:, c, :], ALU.add, [yn, L['bonT']], [yn])
                self.tt('pool', v3(yf, 8)[:, c, :], yn.ap, v3(L['gT'], 8)[:, c, :], ALU.mult, [yn, L['gT']], [yf])
                yield

            def ev_add(oc, bk):
                self.tt('dve', xt_ap[:, oc, :], bk.ap, xt_ap[:, oc, :], ALU.add, [bk, xt], [xt])
            yield from self.linear_gen(l, 'wo_r', 8, 8, lambda kc: v3(yf, 8)[:, kc, :], [yf], T, ev_add)
            self.dma('sp', fm('xs')[:, :, tok0:tok0 + T], xt_ap, ('xst', ti % 2), [xt], self.dres('xs', tok0, tok0 + T))
            yield
        self.pipeline((tile(ti) for ti in range(self.ntiles)), 2)
        self.barrier()

    def phase_load(self):
        self.reset()
        T = 512
        d = self.dram
        xts = [self.carve(8 * T * 4, F32) for _ in range(2)]
        for ti in range(self.ntiles):
            tok0 = ti * T
            xt = xts[ti % 2]
            xt_ap = xt.ap.rearrange("p (c t) -> p c t", c=8)
            src = d['xT'].rearrange("(c p) s -> p c s", p=128)[:, :, tok0:tok0 + T]
            self.dma('sp', xt_ap, src, ('x', ti % 2), [], [xt])
            dst = d['xs'].rearrange("(c p) s -> p c s", p=128)[:, :, tok0:tok0 + T]
            self.dma('sp', dst, xt_ap, ('xst', ti % 2), [xt], self.dres('xs', tok0, tok0 + T))
        self.barrier()

    def run(self):
        self.prologue()
        for ph in self.plan:
            fn = getattr(self, 'phase_' + ph[0], None) or getattr(self, ph[0])
            fn(*ph[1:])


def default_plan(depth):
    plan = [('load',)]
    for l in range(depth):
        if l % 2 == 0:
            plan += [('even', l)]
        else:
            plan += [('odd', l)]
        plan += [('caffn', l, l == depth - 1)]
    return plan


def build_program(S_P, S_S, depth, plan):
    nc = bass.Bass("TRN2", target_bir_lowering=False)
    Stot = S_P + S_S
    dram = {}

    def dt_(name, shape, dt, kind):
        dram[name] = nc.dram_tensor(name, shape, dt, kind=kind).ap()
    _, ncols = col_layout(depth)
    dt_('xT', [D, Stot], F32, 'ExternalInput')
    dt_('memT', [D, 2 * NMEM], F32, 'ExternalInput')
    dt_('cols', [128, ncols], F32, 'ExternalInput')
    dt_('cbf', [128, NCB], F32, 'ExternalInput')
    dt_('rope', [128, 2, S_P], F32, 'ExternalInput')
    for l in range(depth):
        tot = weight_offsets(l)[1]
        dt_('wpk%d' % l, [128, tot], F32, 'ExternalInput')
        dt_('wbf%d' % l, [128, tot], BF16, 'Internal')
    dt_('yT', [D, Stot], F32, 'ExternalOutput')
    dt_('xs', [D, Stot], F32, 'Internal')
    for n in ('qT', 'kT', 'uT', 'bgT', 'yaT'):
        dt_(n, [512, Stot], BF16, 'Internal')
    dt_('vtok', [Stot, 512], BF16, 'Internal')
    for n in ('rT', 'kkT', 'vT', 'gT', 'bonT', 'sigT0', 'sigT1', 'bT0', 'bT1', 'kdT0', 'kdT1', 'oT0', 'oT1'):
        dt_(n, [D, Stot], BF16, 'Internal')

    with ExitStack() as es:
        arena = es.enter_context(nc.sbuf_tensor("arena", [128, ARENA_BYTES // 2], BF16))
        banks = [es.enter_context(nc.psum_tensor("pb%d" % i, [128, 512], F32)) for i in range(8)]
        g1 = Gen(nc, S_P, S_S, depth, plan, arena, banks, dram, Sched(), None)
        g1.run()
        wplan = g1.wrec
        sch = Sched()
        g = Gen(nc, S_P, S_S, depth, plan, arena, banks, dram, sch, wplan)
        g.run()
        final_marks = dict(sch.pending)
        keys = set(ENG[:4])
        for e in ENG:
            for (waits, fn, sk, inc) in sch.q[e]:
                for k, v in waits:
                    keys.add(k)
                if sk is not None:
                    keys.add(sk)
        for k in final_marks:
            keys.add(k)
        sems = {}
        for i, k in enumerate(sorted(keys, key=str)):
            sems[k] = es.enter_context(nc.semaphore("s%d" % i))
        handles = {'pe': None, 'act': None, 'dve': None, 'pool': None, 'sp': None}

        def run_q(ename, h):
            for (waits, fn, sk, inc) in sch.q[ename]:
                for k, v in waits:
                    h.wait_ge(sems[k], v)
                if fn is not None:
                    ins = fn(h)
                    ins.then_inc(sems[sk], inc)
        with nc.Block() as block:
            @block.tensor
            def _(t):
                run_q('pe', t)

            @block.scalar
            def _(s):
                run_q('act', s)

            @block.vector
            def _(v):
                run_q('dve', v)

            @block.gpsimd
            def _(gp):
                run_q('pool', gp)

            @block.sync
            def _(sy):
                run_q('sp', sy)
                for k, v in final_marks.items():
                    sy.wait_ge(sems[k], v)
        info = dict(nops=sch.nops, nsems=len(sems), q={e: len(sch.q[e]) for e in ENG})
    return nc, info


def prep_inputs(inputs, S_P, S_S, depth, n_cores=8):
    p = {k: np.asarray(v, np.float32) for k, v in inputs.items()}
    shared = {'cols': pack_cols(depth, p), 'cbf': make_cbf(), 'rope': np.ascontiguousarray(make_rope(S_P))}
    for l in range(depth):
        shared['wpk%d' % l] = pack_weights(l, p)
    in_maps = []
    for i in range(n_cores):
        m = dict(shared)
        m['xT'] = np.ascontiguousarray(np.concatenate([p['x_prompt'][i].T, p['x_sample'][i].T], axis=1))
        m['memT'] = np.ascontiguousarray(np.concatenate([p['mem_prompt'][i].T, p['mem_sample'][i].T], axis=1))
        in_maps.append(m)
    return in_maps


def run_model(inputs, S_P, S_S, depth, plan, n_cores=8, trace=False):
    nc, info = build_program(S_P, S_S, depth, plan)
    in_maps = prep_inputs(inputs, S_P, S_S, depth, n_cores)
    res = run_bass_kernel_spmd(nc, in_maps, core_ids=list(range(n_cores)), trace=trace)
    yp = np.stack([res.results[i]['yT'][:, :S_P].T for i in range(n_cores)], axis=0)
    ys = np.stack([res.results[i]['yT'][:, S_P:].T for i in range(n_cores)], axis=0)
    return (np.ascontiguousarray(yp, dtype=np.float32), np.ascontiguousarray(ys, dtype=np.float32)), info, res


def kernel(**inputs):
    (yp, ys), info, res = run_model(inputs, 4096, 2048, 4, default_plan(4))
    return (yp, ys)
```
